# Optimizing a Trainium2 kernel written in Bass

```python
import math
import jax
import jax.numpy as jnp
from jax import lax
import numpy as np

D_MODEL = 1024
BATCH = 8
SEQ = 2048
DEPTH = 4

N_MIXERS = 3
N_SSD_LAYERS = (DEPTH + 2) // N_MIXERS
N_LRU_LAYERS = (DEPTH + 1) // N_MIXERS
N_ATTN_LAYERS = DEPTH // N_MIXERS
NORM_EPS = 1e-6
D_FF = 4 * D_MODEL
N_MOD = 6

SSD_D_INNER = 2 * D_MODEL
SSD_HEAD_DIM = 64
SSD_N_HEADS = SSD_D_INNER // SSD_HEAD_DIM
SSD_N_GROUPS = 8
SSD_HEADS_PER_GROUP = SSD_N_HEADS // SSD_N_GROUPS
SSD_D_STATE = 128
SSD_CONV_WIDTH = 4
SSD_CHUNK = 128
SSD_CONV_DIM = SSD_D_INNER + 2 * SSD_N_GROUPS * SSD_D_STATE
SSD_IN_DIM = SSD_D_INNER + SSD_CONV_DIM + SSD_N_HEADS

LRU_WIDTH = D_MODEL
LRU_N_BLOCKS = 4
LRU_BLOCK = LRU_WIDTH // LRU_N_BLOCKS
LRU_CONV_WIDTH = 4
LRU_C = 8.0

ATTN_HEAD_DIM = 64
ATTN_HEADS_PER_GROUP = 8
ATTN_GROUP_WIDTH = ATTN_HEADS_PER_GROUP * ATTN_HEAD_DIM
ATTN_CONFIGS = ((128, 1), (512, 4), (2048, 16))
ATTN_N_GROUPS = len(ATTN_CONFIGS)
ATTN_QKV_DIM = ATTN_N_GROUPS * 3 * ATTN_GROUP_WIDTH
ROPE_THETA = 500000.0
ROPE_DIM = ATTN_HEAD_DIM // 4
MASK_VALUE = -1e30

kernel_name = 'hybrid_ssd_rglru_dilated_attn_trunk'


def rms_norm(x, g):
    xf = x.astype(jnp.float32)
    xf = xf * lax.rsqrt(jnp.mean(jnp.square(xf), axis=-1, keepdims=True) + NORM_EPS)
    return xf.astype(x.dtype) * g


def grouped_rms_norm(x, g, n_groups):
    b, s, d = x.shape
    xg = x.astype(jnp.float32).reshape(b, s, n_groups, d // n_groups)
    xg = xg * lax.rsqrt(jnp.mean(jnp.square(xg), axis=-1, keepdims=True) + NORM_EPS)
    return xg.reshape(b, s, d).astype(x.dtype) * g


def causal_dwconv(x, w, bias):
    k = w.shape[0]
    y = lax.conv_general_dilated(x, w[:, None, :], window_strides=(1,), padding=[(k - 1, 0)],
                                 dimension_numbers=('NWC', 'WIO', 'NWC'),
                                 feature_group_count=x.shape[-1])
    return y + bias


def rope_tables(positions, dtype):
    inv_freq = ROPE_THETA ** (-jnp.arange(0, ROPE_DIM, 2, dtype=jnp.float32) / ROPE_DIM)
    ang = positions.astype(jnp.float32)[..., None] * inv_freq
    return jnp.cos(ang)[:, :, None, :].astype(dtype), jnp.sin(ang)[:, :, None, :].astype(dtype)


def apply_partial_rope(x, cos, sin):
    xr, xp = x[..., :ROPE_DIM], x[..., ROPE_DIM:]
    x1, x2 = jnp.split(xr, 2, axis=-1)
    rot = jnp.concatenate([x1 * cos - x2 * sin, x2 * cos + x1 * sin], axis=-1)
    return jnp.concatenate([rot, xp], axis=-1)


def ssd_mixer(h, w_in, conv_w, conv_b, dt_bias, a_log, d_skip, norm_g, w_out):
    b, s, _ = h.shape
    f32 = jnp.float32
    nc, L = s // SSD_CHUNK, SSD_CHUNK
    G, E, P, N = SSD_N_GROUPS, SSD_HEADS_PER_GROUP, SSD_HEAD_DIM, SSD_D_STATE
    proj = h @ w_in
    z, xbc, dt = jnp.split(proj, [SSD_D_INNER, SSD_D_INNER + SSD_CONV_DIM], axis=-1)
    xbc = jax.nn.silu(causal_dwconv(xbc, conv_w, conv_b))
    xs, bm, cm = jnp.split(xbc, [SSD_D_INNER, SSD_D_INNER + G * N], axis=-1)
    dt = jax.nn.softplus(dt.astype(f32) + dt_bias.astype(f32))
    a = -jnp.exp(a_log.astype(f32)).reshape(G, E)
    x_c = xs.astype(f32).reshape(b, nc, L, G, E, P)
    b_c = bm.astype(f32).reshape(b, nc, L, G, N)
    c_c = cm.astype(f32).reshape(b, nc, L, G, N)
    dt_c = dt.reshape(b, nc, L, G, E)
    a_cs = jnp.cumsum(dt_c * a, axis=2)
    xdt = x_c * dt_c[..., None]
    tri = jnp.tril(jnp.ones((L, L), dtype=bool))[:, :, None, None]
    seg = a_cs[:, :, :, None] - a_cs[:, :, None, :]
    decay = jnp.exp(jnp.where(tri, seg, -jnp.inf))
    cb = jnp.einsum('bclgn,bcsgn->bclsg', c_c, b_c)
    y_diag = jnp.einsum('bclsg,bclsge,bcsgep->bclgep', cb, decay, xdt)
    decay_to_end = jnp.exp(a_cs[:, :, -1:] - a_cs)
    chunk_states = jnp.einsum('bclgn,bclge,bclgep->bcgepn', b_c, decay_to_end, xdt)
    chunk_decay = jnp.exp(a_cs[:, :, -1])

    def step(state, inp):
        dec, new = inp
        return state * dec[..., None, None] + new, state

    init = jnp.zeros((b, G, E, P, N), f32)
    _, prev_states = lax.scan(step, init, (jnp.moveaxis(chunk_decay, 1, 0),
                                           jnp.moveaxis(chunk_states, 1, 0)))
    prev_states = jnp.moveaxis(prev_states, 0, 1)
    y_off = jnp.einsum('bclgn,bcgepn,bclge->bclgep', c_c, prev_states, jnp.exp(a_cs))
    y = y_diag + y_off + x_c * d_skip.astype(f32).reshape(G, E)[..., None]
    y = y.reshape(b, s, SSD_D_INNER) * jax.nn.silu(z.astype(f32))
    y = grouped_rms_norm(y, norm_g.astype(f32), G).astype(h.dtype)
    return y @ w_out


def rglru_mixer(h, w_in, conv_w, conv_b, w_ga, b_ga, w_gx, b_gx, lam, w_out):
    b, s, _ = h.shape
    f32 = jnp.float32
    gate_branch, xr = jnp.split(h @ w_in, 2, axis=-1)
    xr = causal_dwconv(xr, conv_w, conv_b)
    xb = xr.reshape(b, s, LRU_N_BLOCKS, LRU_BLOCK)
    r = jax.nn.sigmoid(jnp.einsum('bshi,hij->bshj', xb, w_ga) + b_ga.reshape(LRU_N_BLOCKS, LRU_BLOCK))
    i = jax.nn.sigmoid(jnp.einsum('bshi,hij->bshj', xb, w_gx) + b_gx.reshape(LRU_N_BLOCKS, LRU_BLOCK))
    r = r.reshape(b, s, LRU_WIDTH).astype(f32)
    i = i.reshape(b, s, LRU_WIDTH).astype(f32)
    log_a = -LRU_C * r * jax.nn.softplus(-lam.astype(f32))
    a = jnp.exp(log_a)
    u = jnp.sqrt(-jnp.expm1(2.0 * log_a)) * (i * xr.astype(f32))

    def combine(left, right):
        a_l, b_l = left
        a_r, b_r = right
        return a_l * a_r, a_r * b_l + b_r

    _, hs = lax.associative_scan(combine, (a, u), axis=1)
    y = hs.astype(h.dtype) * jax.nn.gelu(gate_branch, approximate=True)
    return y @ w_out


def dilated_group_attention(q, k, v, dilation, span):
    b, s, hh, dh = q.shape
    m = s // dilation
    nblk = -(-m // span)
    mp = nblk * span

    def to_sub(t):
        t = t.reshape(b, m, dilation, hh, dh)
        t = jnp.pad(t, ((0, 0), (0, mp - m), (0, 0), (0, 0), (0, 0)))
        return t.reshape(b, nblk, span, dilation, hh, dh)

    qs, ks, vs = to_sub(q), to_sub(k), to_sub(v)
    pad = ((0, 0), (1, 0), (0, 0), (0, 0), (0, 0), (0, 0))
    kp, vp = jnp.pad(ks, pad), jnp.pad(vs, pad)
    k_win = jnp.concatenate([kp[:, :-1], kp[:, 1:]], axis=2)
    v_win = jnp.concatenate([vp[:, :-1], vp[:, 1:]], axis=2)
    scores = jnp.einsum('bnqrhd,bnkrhd->bnrhqk', qs, k_win).astype(jnp.float32)
    qi = jnp.arange(span)[:, None]
    ki = jnp.arange(2 * span)[None, :]
    dist = qi + span - ki
    key_pos = jnp.arange(nblk)[:, None, None] * span - span + ki[None]
    valid = (dist >= 0) & (dist <= span) & (key_pos >= 0)
    scores = jnp.where(valid[None, :, None, None], scores, MASK_VALUE)
    lse = jax.nn.logsumexp(scores, axis=-1)
    probs = jnp.exp(scores - lse[..., None]).astype(v.dtype)
    out = jnp.einsum('bnrhqk,bnkrhd->bnqrhd', probs, v_win)
    out = out.reshape(b, mp * dilation, hh, dh)[:, :s]
    lse = jnp.transpose(lse, (0, 1, 4, 2, 3)).reshape(b, mp * dilation, hh)[:, :s]
    return out, lse


def dilated_attention_mixer(h, cos, sin, w_qkv, w_out):
    b, s, _ = h.shape
    scale = ATTN_HEAD_DIM ** -0.5
    qkv = (h @ w_qkv).reshape(b, s, ATTN_N_GROUPS, 3, ATTN_HEADS_PER_GROUP, ATTN_HEAD_DIM)
    outs, lses = [], []
    for g, (window, dilation) in enumerate(ATTN_CONFIGS):
        q = apply_partial_rope(qkv[:, :, g, 0], cos, sin) * scale
        k = apply_partial_rope(qkv[:, :, g, 1], cos, sin)
        v = qkv[:, :, g, 2]
        o, lse = dilated_group_attention(q, k, v, dilation, window // dilation)
        outs.append(o.astype(jnp.float32))
        lses.append(lse)
    w = jax.nn.softmax(jnp.stack(lses, axis=0), axis=0)
    o = jnp.sum(w[..., None] * jnp.stack(outs, axis=0), axis=0)
    return o.reshape(b, s, ATTN_GROUP_WIDTH).astype(h.dtype) @ w_out


def _normal(key, shape, fan_in):
    return jax.random.normal(key, shape, jnp.float32) * fan_in ** -0.5


def _gain(key, shape):
    return 1.0 + 0.05 * jax.random.normal(key, shape, jnp.float32)


def _small(key, shape):
    return 0.02 * jax.random.normal(key, shape, jnp.float32)


def setup_inputs(seed: int = 0) -> dict:
    key = jax.random.key(seed)
    ks = jax.random.split(key, 32)
    nA, nB, nC = N_SSD_LAYERS, N_LRU_LAYERS, N_ATTN_LAYERS
    x = jax.random.normal(ks[0], (BATCH, SEQ, D_MODEL), jnp.float32)
    c = jax.random.normal(ks[1], (BATCH, D_MODEL), jnp.float32)
    offsets = jax.random.randint(ks[2], (BATCH, 1), 0, 1024, dtype=jnp.int32)
    positions = offsets + jnp.arange(SEQ, dtype=jnp.int32)[None, :]
    ada_w = 0.2 * _normal(ks[3], (DEPTH, D_MODEL, N_MOD * D_MODEL), D_MODEL)
    ada_b = _small(ks[4], (DEPTH, N_MOD * D_MODEL))
    norm_mix_pre = _gain(ks[5], (DEPTH, D_MODEL))
    norm_mix_post = _gain(ks[6], (DEPTH, D_MODEL))
    norm_mlp_pre = _gain(ks[7], (DEPTH, D_MODEL))
    norm_mlp_post = _gain(ks[8], (DEPTH, D_MODEL))
    mlp_w1 = _normal(ks[9], (DEPTH, D_MODEL, D_FF), D_MODEL)
    mlp_w2 = _normal(ks[10], (DEPTH, D_FF, D_MODEL), D_FF)
    ssd_w_in = _normal(ks[11], (nA, D_MODEL, SSD_IN_DIM), D_MODEL)
    ssd_conv_w = _normal(ks[12], (nA, SSD_CONV_WIDTH, SSD_CONV_DIM), SSD_CONV_WIDTH)
    ssd_conv_b = _small(ks[13], (nA, SSD_CONV_DIM))
    dt0 = jnp.exp(jax.random.uniform(ks[14], (nA, SSD_N_HEADS), jnp.float32,
                                     minval=math.log(1e-3), maxval=math.log(1e-1)))
    ssd_dt_bias = dt0 + jnp.log(-jnp.expm1(-dt0))
    ssd_a_log = jnp.log(jax.random.uniform(ks[15], (nA, SSD_N_HEADS), jnp.float32, minval=1.0, maxval=16.0))
    ssd_d = _gain(ks[16], (nA, SSD_N_HEADS))
    ssd_norm = _gain(ks[17], (nA, SSD_D_INNER))
    ssd_w_out = _normal(ks[18], (nA, SSD_D_INNER, D_MODEL), SSD_D_INNER)
    lru_w_in = _normal(ks[19], (nB, D_MODEL, 2 * LRU_WIDTH), D_MODEL)
    lru_conv_w = _normal(ks[20], (nB, LRU_CONV_WIDTH, LRU_WIDTH), LRU_CONV_WIDTH)
    lru_conv_b = _small(ks[21], (nB, LRU_WIDTH))
    lru_w_gate_a = _normal(ks[22], (nB, LRU_N_BLOCKS, LRU_BLOCK, LRU_BLOCK), LRU_BLOCK)
    lru_b_gate_a = _small(ks[23], (nB, LRU_WIDTH))
    lru_w_gate_x = _normal(ks[24], (nB, LRU_N_BLOCKS, LRU_BLOCK, LRU_BLOCK), LRU_BLOCK)
    lru_b_gate_x = _small(ks[25], (nB, LRU_WIDTH))
    a_target = jax.random.uniform(ks[26], (nB, LRU_WIDTH), jnp.float32, minval=0.9, maxval=0.999)
    p = a_target ** (1.0 / LRU_C)
    lru_lambda = jnp.log(p) - jnp.log1p(-p)
    lru_w_out = _normal(ks[27], (nB, LRU_WIDTH, D_MODEL), LRU_WIDTH)
    attn_w_qkv = _normal(ks[28], (nC, D_MODEL, ATTN_QKV_DIM), D_MODEL)
    attn_w_out = _normal(ks[29], (nC, ATTN_GROUP_WIDTH, D_MODEL), ATTN_GROUP_WIDTH)
    return {'x': x, 'c': c, 'positions': positions, 'ada_w': ada_w, 'ada_b': ada_b,
            'norm_mix_pre': norm_mix_pre, 'norm_mix_post': norm_mix_post,
            'norm_mlp_pre': norm_mlp_pre, 'norm_mlp_post': norm_mlp_post,
            'mlp_w1': mlp_w1, 'mlp_w2': mlp_w2,
            'ssd_w_in': ssd_w_in, 'ssd_conv_w': ssd_conv_w, 'ssd_conv_b': ssd_conv_b,
            'ssd_dt_bias': ssd_dt_bias, 'ssd_a_log': ssd_a_log, 'ssd_d': ssd_d,
            'ssd_norm': ssd_norm, 'ssd_w_out': ssd_w_out,
            'lru_w_in': lru_w_in, 'lru_conv_w': lru_conv_w, 'lru_conv_b': lru_conv_b,
            'lru_w_gate_a': lru_w_gate_a, 'lru_b_gate_a': lru_b_gate_a,
            'lru_w_gate_x': lru_w_gate_x, 'lru_b_gate_x': lru_b_gate_x,
            'lru_lambda': lru_lambda, 'lru_w_out': lru_w_out,
            'attn_w_qkv': attn_w_qkv, 'attn_w_out': attn_w_out}


def reference(x, c, positions, ada_w, ada_b, norm_mix_pre, norm_mix_post, norm_mlp_pre, norm_mlp_post,
              mlp_w1, mlp_w2, ssd_w_in, ssd_conv_w, ssd_conv_b, ssd_dt_bias, ssd_a_log, ssd_d, ssd_norm,
              ssd_w_out, lru_w_in, lru_conv_w, lru_conv_b, lru_w_gate_a, lru_b_gate_a, lru_w_gate_x,
              lru_b_gate_x, lru_lambda, lru_w_out, attn_w_qkv, attn_w_out):
    cos, sin = rope_tables(positions, x.dtype)
    mod = jnp.einsum('bd,lde->lbe', jax.nn.silu(c), ada_w) + ada_b[:, None, :]
    for layer in range(DEPTH):
        sh_m, sc_m, g_m, sh_f, sc_f, g_f = jnp.split(mod[layer][:, None, :], N_MOD, axis=-1)
        h = rms_norm(x, norm_mix_pre[layer]) * (1.0 + sc_m) + sh_m
        kind, occ = layer % N_MIXERS, layer // N_MIXERS
        if kind == 0:
            y = ssd_mixer(h, ssd_w_in[occ], ssd_conv_w[occ], ssd_conv_b[occ], ssd_dt_bias[occ],
                          ssd_a_log[occ], ssd_d[occ], ssd_norm[occ], ssd_w_out[occ])
        elif kind == 1:
            y = rglru_mixer(h, lru_w_in[occ], lru_conv_w[occ], lru_conv_b[occ], lru_w_gate_a[occ],
                            lru_b_gate_a[occ], lru_w_gate_x[occ], lru_b_gate_x[occ], lru_lambda[occ],
                            lru_w_out[occ])
        else:
            y = dilated_attention_mixer(h, cos, sin, attn_w_qkv[occ], attn_w_out[occ])
        x = x + (1.0 + g_m) * rms_norm(y, norm_mix_post[layer])
        h = rms_norm(x, norm_mlp_pre[layer]) * (1.0 + sc_f) + sh_f
        y = jnp.square(jax.nn.relu(h @ mlp_w1[layer])) @ mlp_w2[layer]
        x = x + (1.0 + g_f) * rms_norm(y, norm_mlp_post[layer])
    return x
```

```python
import math
import numpy as np
import concourse.bass as bass
import concourse.mybir as mybir
from concourse.bass_utils import run_bass_kernel_spmd

F32 = mybir.dt.float32
BF16 = mybir.dt.bfloat16
I32 = mybir.dt.int32
AF = mybir.ActivationFunctionType
ALU = mybir.AluOpType

D = 1024
S = 2048
DEPTH = 4
EPS = 1e-6
NCH = 8


def _box(ap):
    t = ap.tensor
    if type(t).__name__ == "DRamTensorHandle":
        return None
    if type(t).__name__ == "PSumTensorHandle":
        return (t.name, 0, 128, 0, 2048)
    es = mybir.dt.size(ap.dtype)
    dims = list(ap.ap)
    pstep, pcnt = dims[0]
    off = ap.offset
    if pstep == 0:
        p0, f0 = 0, off
    else:
        p0 = off // pstep
        f0 = off - p0 * pstep
    f1 = f0 + sum((c - 1) * abs(s) for s, c in dims[1:]) + 1
    return (t.name, p0, p0 + pcnt, f0 * es, f1 * es)


def _overlap(a, b):
    return a[1] < b[2] and b[1] < a[2] and a[3] < b[4] and b[3] < a[4]


def _covers(a, b):
    return a[1] <= b[1] and a[2] >= b[2] and a[3] <= b[3] and a[4] >= b[4]


class Prog:
    def __init__(self, nc):
        self.nc = nc
        self.E = dict(pe=nc.tensor, dve=nc.vector, act=nc.scalar, pool=nc.gpsimd, sp=nc.sync)
        self.ops = []
        self._ctx = []

    def sb(self, name, shape, dt):
        cm = self.nc.sbuf_tensor(name, list(shape), dt)
        t = cm.__enter__()
        self._ctx.append(cm)
        return t

    def ps(self, name, shape, dt=F32):
        cm = self.nc.psum_tensor(name, list(shape), dt)
        t = cm.__enter__()
        self._ctx.append(cm)
        return t

    def sem(self, name):
        cm = self.nc.semaphore(name)
        s = cm.__enter__()
        self._ctx.append(cm)
        return s

    def op(self, eng, fn, reads=(), writes=(), dkey=None):
        r = [b for b in (_box(a) for a in reads) if b is not None]
        w = [b for b in (_box(a) for a in writes) if b is not None]
        self.ops.append(dict(eng=eng, fn=fn, r=r, w=w, dkey=dkey))

    def dma(self, q, out, in_, dkey, **kw):
        e = self.E[q]
        self.op(q, lambda: e.dma_start(out=out, in_=in_, **kw), reads=[in_], writes=[out], dkey=dkey)

    def emit(self, final_dkeys=()):
        ops = self.ops
        n = len(ops)
        hist = {}
        deps = [None] * n
        for i, o in enumerate(ops):
            d = set()
            for b in o["r"]:
                psum = b[0].startswith("pb")
                for ent in hist.get(b[0], ()):
                    if _overlap(ent[0], b) and (ent[2] or (psum and ops[ent[1]]["eng"] != o["eng"])):
                        d.add(ent[1])
            for b in o["w"]:
                for ent in hist.get(b[0], ()):
                    if _overlap(ent[0], b):
                        d.add(ent[1])
            deps[i] = d
            for b in o["w"]:
                lst = hist.setdefault(b[0], [])
                lst[:] = [e for e in lst if not _covers(b, e[0])]
                lst.append([b, i, True])
            for b in o["r"]:
                lst = hist.setdefault(b[0], [])
                rep = False
                if o["dkey"] is None:
                    for e in lst:
                        if (not e[2]) and e[0] == b and ops[e[1]]["eng"] == o["eng"] and ops[e[1]]["dkey"] is None:
                            e[1] = i
                            rep = True
                            break
                if not rep:
                    lst.append([b, i, False])
        need_sig = [False] * n
        red = [None] * n
        for i, o in enumerate(ops):
            best = {}
            dm = []
            for j in deps[i]:
                oj = ops[j]
                if oj["dkey"] is not None:
                    dm.append(j)
                else:
                    if oj["eng"] == "pe" and o["eng"] == "pe" and o["dkey"] is None:
                        continue
                    if j > best.get(oj["eng"], -1):
                        best[oj["eng"]] = j
            red[i] = (best, dm)
            for j in best.values():
                need_sig[j] = True
        esem = {k: self.sem("s_" + k) for k in self.E}
        dsem = {}
        for o in ops:
            if o["dkey"] is not None and o["dkey"] not in dsem:
                dsem[o["dkey"]] = self.sem("d_" + str(o["dkey"]))
        sigcnt = {k: 0 for k in self.E}
        sigval = [0] * n
        dcnt = {k: 0 for k in dsem}
        dval = [0] * n
        waited = {k: {} for k in self.E}
        nwaits = 0
        for i, o in enumerate(ops):
            eng = o["eng"]
            e = self.E[eng]
            best, dm = red[i]
            ws = {}
            for k, j in best.items():
                ws[("e", k)] = max(ws.get(("e", k), 0), sigval[j])
            for j in dm:
                key = ("d", ops[j]["dkey"])
                ws[key] = max(ws.get(key, 0), dcnt[ops[j]["dkey"]])
            for key, v in ws.items():
                if waited[eng].get(key, 0) >= v:
                    continue
                waited[eng][key] = v
                s = esem[key[1]] if key[0] == "e" else dsem[key[1]]
                e.wait_ge(s, v)
                nwaits += 1
            ins = o["fn"]()
            if o["dkey"] is not None:
                dcnt[o["dkey"]] += 16
                dval[i] = dcnt[o["dkey"]]
                ins.then_inc(dsem[o["dkey"]], 16)
            elif need_sig[i]:
                sigcnt[eng] += 1
                sigval[i] = sigcnt[eng]
                ins.then_inc(esem[eng], 1)
        for k in final_dkeys:
            self.E["sp"].wait_ge(dsem[k], dcnt[k])
        self.stats = dict(n_ops=n, n_waits=nwaits, sig=dict(sigcnt))
        return self.stats


KT_IDENT = 0
KT_TRILE = 128
KT_SL = 256
KT_ATT = 384
KT_COLS = 512
ATT_CFG = ((128, 1), (512, 4), (2048, 16))
ATT_MBASE = (0, 2, 7)
ATT_NMASK = 23


def make_ktab():
    kt = np.zeros((128, KT_COLS), np.float32)
    kt[:, KT_IDENT:KT_IDENT + 128] = np.eye(128, dtype=np.float32)
    j = np.arange(128)[:, None]
    q = np.arange(128)[None, :]
    kt[:, KT_TRILE:KT_TRILE + 128] = (j <= q).astype(np.float32)
    kt[:, KT_SL:KT_SL + 128] = (q < j).astype(np.float32)
    for m in range(8):
        kt[m + 8, KT_ATT + m] = 1.0
        kt[m, KT_ATT + m + 8] = 1.0
    inv_freq = (500000.0 ** (-np.arange(0, 16, 2, dtype=np.float32) / 16.0)).astype(np.float32)
    for m in range(16):
        kt[m, KT_ATT + 64] = inv_freq[m % 8] / (2.0 * np.pi)
        kt[m, KT_ATT + 65] = -1.0 if m < 8 else 1.0
    return kt


def make_amask():
    am = np.full((128, ATT_NMASK, 128), -30000.0, np.float32)
    qi = np.arange(128)[:, None]
    ki = np.arange(128)[None, :]
    for g, (window, dil) in enumerate(ATT_CFG):
        nd = (2, 5, 16)[g]
        for dl in range(nd):
            dd = 128 * dl + qi - ki
            ok = (dd >= 0) & (dd % dil == 0) & (dd <= window)
            am[:, ATT_MBASE[g] + dl, :][ok] = 0.0
    return am.reshape(128, ATT_NMASK * 128)


class K:
    def __init__(self, layers):
        self.layers = layers
        nc = self.nc = bass.Bass("TRN2", target_bir_lowering=False)
        P = self.P = Prog(nc)
        dt = lambda name, shape, ty=F32, kind="ExternalInput": nc.dram_tensor(name, list(shape), ty, kind=kind).ap()
        self.d = d = {}
        d["x"] = dt("x", [S, D])
        d["c"] = dt("c", [8, 128])
        d["ktab"] = dt("ktab", [128, KT_COLS])
        d["ada_w"] = dt("ada_w", [4, D, 6 * D])
        d["ada_b"] = dt("ada_b", [4 * 48, 128])
        for nm in ("norm_mix_pre", "norm_mix_post", "norm_mlp_pre", "norm_mlp_post"):
            d[nm] = dt(nm, [4 * 8, 128])
        d["mlp_w1"] = dt("mlp_w1", [4, D, 4 * D])
        d["mlp_w2"] = dt("mlp_w2", [4, 4 * D, D])
        kinds = set(l % 3 for l in layers)
        if 0 in kinds:
            d["ssd_w_in"] = dt("ssd_w_in", [2, D, 6176])
            d["ssd_conv_w"] = dt("ssd_conv_w", [2 * 4 * 32, 128])
            d["ssd_conv_b"] = dt("ssd_conv_b", [2 * 32, 128])
            d["ssd_dt_bias"] = dt("ssd_dt_bias", [2, 32])
            d["ssd_a_log"] = dt("ssd_a_log", [2, 32])
            d["ssd_d"] = dt("ssd_d", [2, 32])
            d["ssd_norm"] = dt("ssd_norm", [2 * 16, 128])
            d["ssd_w_out"] = dt("ssd_w_out", [2, 2048, D])
        if 1 in kinds:
            d["lru_w_in"] = dt("lru_w_in", [1, D, 2048])
            d["lru_conv_w"] = dt("lru_conv_w", [4 * 8, 128])
            d["lru_conv_b"] = dt("lru_conv_b", [8, 128])
            d["lru_w_gate_a"] = dt("lru_w_gate_a", [1024, 256])
            d["lru_b_gate_a"] = dt("lru_b_gate_a", [8, 128])
            d["lru_w_gate_x"] = dt("lru_w_gate_x", [1024, 256])
            d["lru_b_gate_x"] = dt("lru_b_gate_x", [8, 128])
            d["lru_lambda"] = dt("lru_lambda", [8, 128])
            d["lru_w_out"] = dt("lru_w_out", [1, D, D])
        if 2 in kinds:
            d["attn_w_qkv"] = dt("attn_w_qkv", [1, D, 4608])
            d["attn_w_out"] = dt("attn_w_out", [1, 512, D])
            d["amask"] = dt("amask", [128, ATT_NMASK * 128])
            d["positions"] = dt("positions", [1, S], I32)
        d["out"] = dt("out", [S, D], F32, kind="ExternalOutput")

        self.XT = P.sb("XT", [128, NCH, S], F32)
        self.NRING = 4
        self.ring = [P.sb("ring%d" % i, [128, 4096], BF16) for i in range(self.NRING)]
        self.ring_i = 0
        self.SCRB = 104 * 1024
        self.SCR = P.sb("SCR", [128, self.SCRB // 2], BF16)
        self.CT = P.sb("CT", [128, 1024], F32)
        self.CB = P.sb("CB", [128, 512], BF16)
        self.ct_off = 0
        self.cb_off = 0
        self.pb = [P.ps("pb%d" % i, [128, 512], F32) for i in range(8)]
        self.acc_i = 0

        self.build()

    def scr(self, off, shape, dt):
        es = mybir.dt.size(dt)
        n = int(np.prod(shape[1:]))
        assert off % 4 == 0 and off + n * es <= self.SCRB, (off, shape)
        a = self.SCR[:, off // 2: off // 2 + n * es // 2]
        if dt != BF16:
            a = a.bitcast(dt)
        if len(shape) == 3:
            a = a.rearrange("p (a b) -> p a b", a=shape[1])
        elif len(shape) == 4:
            a = a.rearrange("p (a b c) -> p a b c", a=shape[1], b=shape[2])
        return a

    def ct(self, n):
        a = self.CT[:, self.ct_off:self.ct_off + n]
        self.ct_off += n
        assert self.ct_off <= 1024
        return a

    def cb(self, n):
        a = self.CB[:, self.cb_off:self.cb_off + n]
        self.cb_off += n
        assert self.cb_off <= 512
        return a

    def acc(self):
        b = self.pb[self.acc_i % 4]
        self.acc_i += 1
        return b

    def slot(self):
        s = self.ring_i % self.NRING
        self.ring_i += 1
        return s, self.ring[s]

    def wload(self, src, shape):
        s, t = self.slot()
        n = int(np.prod(shape[1:]))
        assert n <= 4096
        v = t[:, 0:n]
        if len(shape) == 3:
            v = v.rearrange("p (a b) -> p a b", a=shape[1])
        self.P.dma("pool", v, src, "ring%d" % s)
        return v

    def mm(self, out, lhsT, rhs, start, stop):
        nc = self.nc
        self.P.op("pe", lambda: nc.tensor.matmul(out, lhsT=lhsT, rhs=rhs, start=start, stop=stop), [lhsT, rhs], [out])

    def tr(self, out, in_, ident):
        nc = self.nc
        self.P.op("pe", lambda: nc.tensor.transpose(out=out, in_=in_, identity=ident), [in_, ident], [out])

    def act(self, out, in_, func, bias=None, scale=None):
        nc = self.nc
        kw = {}
        rd = [in_]
        if bias is not None:
            kw["bias"] = bias
            if not isinstance(bias, (int, float)):
                rd.append(bias)
        if scale is not None:
            kw["scale"] = scale
            if not isinstance(scale, (int, float)):
                rd.append(scale)
        self.P.op("act", lambda: nc.scalar.activation(out=out, in_=in_, func=func, **kw), rd, [out])

    def tt(self, out, in0, in1, op, eng="dve"):
        e = self.P.E[eng]
        self.P.op(eng, lambda: e.tensor_tensor(out=out, in0=in0, in1=in1, op=op), [in0, in1], [out])

    def ts(self, out, in0, s1, s2, op0, op1=None, eng="dve"):
        e = self.P.E[eng]
        rd = [in0] + [s for s in (s1, s2) if s is not None and not isinstance(s, (int, float))]
        if op1 is None:
            self.P.op(eng, lambda: e.tensor_scalar(out=out, in0=in0, scalar1=s1, scalar2=None, op0=op0), rd, [out])
        else:
            self.P.op(eng, lambda: e.tensor_scalar(out=out, in0=in0, scalar1=s1, scalar2=s2, op0=op0, op1=op1), rd, [out])

    def stt(self, out, in0, scalar, in1, op0, op1):
        nc = self.nc
        rd = [in0, in1] + ([] if isinstance(scalar, (int, float)) else [scalar])
        self.P.op("dve", lambda: nc.vector.scalar_tensor_tensor(out=out, in0=in0, scalar=scalar, in1=in1, op0=op0, op1=op1), rd, [out])

    def copy(self, out, in_, eng="dve"):
        if eng == "act":
            self.act(out, in_, AF.Copy)
        else:
            e = self.P.E[eng]
            self.P.op(eng, lambda: e.tensor_copy(out=out, in_=in_), [in_], [out])

    def recip(self, out, in_):
        nc = self.nc
        self.P.op("dve", lambda: nc.vector.reciprocal(out=out, in_=in_), [in_], [out])

    def memset(self, ap, val, eng="dve"):
        e = self.P.E[eng]
        self.P.op(eng, lambda: e.memset(ap, val), [], [ap])

    def scan(self, out, d0, d1, initial):
        nc = self.nc
        rd = [d0, d1] + ([] if isinstance(initial, (int, float)) else [initial])
        self.P.op("dve", lambda: nc.vector.tensor_tensor_scan(out=out, data0=d0, data1=d1, initial=initial, op0=ALU.mult, op1=ALU.add), rd, [out])

    def load_rowsT(self, rows_ap, nrows, dst, ncols=128, stoff=0):
        st = self.scr(stoff, (128, 128), F32)
        self.P.dma("sp", st[0:nrows, 0:ncols], rows_ap, "cst")
        ps = self.pb[7]
        self.tr(ps[0:ncols, 0:nrows], st[0:nrows, 0:ncols], self.ident[0:nrows, 0:nrows])
        self.copy(dst, ps[0:ncols, 0:nrows])

    def setup(self):
        P, d = self.P, self.d
        self.ident = self.ct(128)
        P.dma("sp", self.ident, d["ktab"][:, KT_IDENT:KT_IDENT + 128], "ident")
        self.identb = self.cb(128)
        self.copy(self.identb, self.ident)
        self.onesb = self.cb(128)
        self.memset(self.onesb, 1.0 / 1024.0)
        self.epsc = self.ct(1)
        self.memset(self.epsc, EPS)
        self.onec = self.ct(1)
        self.memset(self.onec, 1.0)
        self.NG = self.ct(128)
        for k, nm in enumerate(("norm_mix_pre", "norm_mix_post", "norm_mlp_pre", "norm_mlp_post")):
            self.load_rowsT(d[nm], 32, self.NG[:, k * 32:(k + 1) * 32])
        self.AB = self.ct(192)
        self.load_rowsT(d["ada_b"][0:96, :], 96, self.AB[:, 0:96])
        self.load_rowsT(d["ada_b"][96:192, :], 96, self.AB[:, 96:192])
        cT = self.ct(8)
        self.load_rowsT(d["c"], 8, cT)
        self.SCb = self.cb(8)
        self.act(self.SCb, cT, AF.Silu)
        self.MOD = self.ct(192)

    def ada_layer(self, l):
        d = self.d
        ps = self.pb[6]
        for blk in range(12):
            w = self.wload(d["ada_w"][l, :, blk * 512:(blk + 1) * 512].rearrange("(kc p) n -> p kc n", p=128), (128, 8, 512))
            for jj in range(4):
                j = blk * 4 + jj
                for kc in range(8):
                    self.mm(ps[:, j:j + 1], w[:, kc, jj * 128:(jj + 1) * 128], self.SCb[:, kc:kc + 1], kc == 0, kc == 7)
        self.tt(self.MOD[:, l * 48:(l + 1) * 48], ps[:, 0:48], self.AB[:, l * 48:(l + 1) * 48], ALU.add)
        m = lambda w_: self.MOD[:, l * 48 + w_ * 8: l * 48 + w_ * 8 + 8]
        g = lambda k: self.NG[:, k * 32 + l * 8: k * 32 + l * 8 + 8]
        V = {}
        for nm in ("A_m", "G_m", "A_f", "G_f"):
            V[nm] = self.ct(8)
        self.stt(V["A_m"], m(1), 1.0, g(0), ALU.add, ALU.mult)
        self.stt(V["G_m"], m(2), 1.0, g(1), ALU.add, ALU.mult)
        self.stt(V["A_f"], m(4), 1.0, g(2), ALU.add, ALU.mult)
        self.stt(V["G_f"], m(5), 1.0, g(3), ALU.add, ALU.mult)
        V["B_m"] = m(0)
        V["B_f"] = m(3)
        return V

    def load_x(self):
        P, d = self.P, self.d
        for i in range(16):
            st = self.scr((i % 2) * 4096, (128, 1024), F32)
            P.dma("sp", st, d["x"][i * 128:(i + 1) * 128, :], "xin%d" % (i % 2))
            for half in range(2):
                ps = self.pb[4 + half]
                for q in range(4):
                    kc = half * 4 + q
                    self.tr(ps[:, q * 128:(q + 1) * 128], st[:, kc * 128:(kc + 1) * 128], self.ident)
                dst = self.XT[:, half * 4:(half + 1) * 4, i * 128:(i + 1) * 128]
                src = ps[:, :].rearrange("p (a b) -> p a b", a=4)
                self.copy(dst, src, eng="act" if half else "dve")

    def store_x(self):
        P, d = self.P, self.d
        for i in range(16):
            st = self.scr((i % 2) * 4096, (128, 1024), F32)
            for half in range(2):
                ps = self.pb[4 + half]
                for q in range(4):
                    kc = half * 4 + q
                    self.tr(ps[:, q * 128:(q + 1) * 128], self.XT[:, kc, i * 128:(i + 1) * 128], self.ident)
                self.copy(st[:, half * 512:(half + 1) * 512], ps[:, :], eng="act" if half else "dve")
            P.dma("sp", d["out"][i * 128:(i + 1) * 128, :], st, "out%d" % (i % 2))

    def rstd512(self, srcs, rstd, sqoff, onesb=None, nchunks=8):
        ss = self.pb[7]
        onesb = self.onesb if onesb is None else onesb
        for kc, s in enumerate(srcs):
            sq = self.scr(sqoff + (kc % 2) * 1024, (128, 512), BF16)
            self.act(sq, s, AF.Square)
            self.mm(ss[:, :], onesb, sq, kc == 0, kc == len(srcs) - 1)
        self.act(rstd, ss[:, :], AF.Sqrt, bias=self.epsc, scale=1.0)
        self.recip(rstd, rstd)

    def prenorm(self, t0, T, A, B, hT, tmpoff):
        for sub in range(T // 512):
            a = t0 + sub * 512
            rstd = self.scr(tmpoff + 2048, (128, 512), F32)
            self.rstd512([self.XT[:, kc, a:a + 512] for kc in range(8)], rstd, tmpoff)
            for kc in range(8):
                tmp = self.scr(tmpoff + 4096 + (kc % 2) * 2048, (128, 512), F32)
                self.tt(tmp, self.XT[:, kc, a:a + 512], rstd, ALU.mult)
                self.act(hT[:, kc, sub * 512:(sub + 1) * 512], tmp, AF.Identity, bias=B[:, kc:kc + 1], scale=A[:, kc:kc + 1])

    def postnorm_add(self, t0, yT, G, tmpoff):
        rstd = self.scr(tmpoff + 2048, (128, 512), F32)
        self.rstd512([yT[:, kc, :] for kc in range(8)], rstd, tmpoff)
        for kc in range(8):
            tmp = self.scr(tmpoff + 4096 + (kc % 2) * 2048, (128, 512), F32)
            self.stt(tmp, yT[:, kc, :], G[:, kc:kc + 1], rstd, ALU.mult, ALU.mult)
            xs = self.XT[:, kc, t0:t0 + 512]
            self.tt(xs, xs, tmp, ALU.add)

    def mlp(self, l, V):
        d = self.d
        T = 1024
        HID = 0
        HT = 65536
        TMP = 65536 + 32768
        SQ = TMP + 4096
        hid = self.scr(HID, (128, 32, T), BF16)
        hT = self.scr(HT, (128, 8, T), BF16)
        yT = self.scr(HT, (128, 16, 512), F32)
        for tt_ in range(S // T):
            t0 = tt_ * T
            self.prenorm(t0, T, V["A_f"], V["B_f"], hT, TMP)
            for hb in range(8):
                w = self.wload(d["mlp_w1"][l, :, hb * 512:(hb + 1) * 512].rearrange("(kc p) n -> p kc n", p=128), (128, 8, 512))
                for j in range(4):
                    hc = hb * 4 + j
                    for sub in range(2):
                        ps = self.acc()
                        for kc in range(8):
                            self.mm(ps[:, :], w[:, kc, j * 128:(j + 1) * 128], hT[:, kc, sub * 512:(sub + 1) * 512], kc == 0, kc == 7)
                        sq = self.scr(SQ + (self.acc_i % 2) * 2048, (128, 512), F32)
                        self.act(sq, ps[:, :], AF.Square)
                        self.stt(hid[:, hc, sub * 512:(sub + 1) * 512], ps[:, :], 0.0, sq, ALU.is_gt, ALU.mult)
            for oc in range(8):
                w2 = self.wload(d["mlp_w2"][l, :, oc * 128:(oc + 1) * 128].rearrange("(hc p) n -> p hc n", p=128), (128, 32, 128))
                for sub in range(2):
                    ps = self.acc()
                    for hc in range(32):
                        self.mm(ps[:, :], w2[:, hc, :], hid[:, hc, sub * 512:(sub + 1) * 512], hc == 0, hc == 31)
                    self.copy(yT[:, sub * 8 + oc, :], ps[:, :], eng="act" if oc % 2 else "dve")
            for sub in range(2):
                self.postnorm_add(t0 + sub * 512, yT[:, sub * 8:(sub + 1) * 8, :], V["G_f"], TMP)

    def mixer_lru(self, l, V):
        d = self.d
        o = [0]

        def A(nb):
            r = o[0]
            o[0] += (nb + 63) // 64 * 64
            return r
        hT = self.scr(A(8192), (128, 8, 512), BF16)
        xrpre = self.scr(A(8 * 516 * 4), (128, 8, 516), F32)
        xro = A(16384)
        xr = self.scr(xro, (128, 8, 512), F32)
        yT = self.scr(xro, (128, 8, 512), F32)
        xrb = self.scr(A(8192), (128, 8, 512), BF16)
        ylru = self.scr(A(8192), (128, 8, 512), BF16)
        TMP = A(8192)
        tmp = [[self.scr(A(2048), (128, 512), F32) for _ in range(2)] for _ in range(6)]
        CW = self.ct(32)
        self.load_rowsT(d["lru_conv_w"], 32, CW)
        misc = self.ct(32)
        self.load_rowsT(d["lru_conv_b"], 8, misc[:, 0:8])
        self.load_rowsT(d["lru_b_gate_a"], 8, misc[:, 8:16])
        self.load_rowsT(d["lru_b_gate_x"], 8, misc[:, 16:24])
        self.load_rowsT(d["lru_lambda"], 8, misc[:, 24:32])
        cbias, bga, bgx, lam = misc[:, 0:8], misc[:, 8:16], misc[:, 16:24], misc[:, 24:32]
        cL = self.ct(8)
        cL2 = self.ct(8)
        state = self.ct(8)
        self.act(cL, lam, AF.Exp, scale=-1.0)
        self.act(cL, cL, AF.Ln, bias=self.onec, scale=1.0)
        self.ts(cL2, cL, -16.0, None, ALU.mult)
        self.ts(cL, cL, -8.0, None, ALU.mult)
        self.memset(state, 0.0)
        self.memset(xrpre[:, :, 0:3], 0.0)
        w_in = d["lru_w_in"][0]
        for tt_ in range(4):
            t0 = tt_ * 512
            self.prenorm(t0, 512, V["A_m"], V["B_m"], hT, TMP)
            for cb in range(8):
                if cb % 4 == 0:
                    w = self.wload(w_in[:, 1024 + (cb // 4) * 512: 1024 + (cb // 4 + 1) * 512].rearrange("(kc p) n -> p kc n", p=128), (128, 8, 512))
                ps = self.acc()
                for kc in range(8):
                    self.mm(ps[:, :], w[:, kc, (cb % 4) * 128:(cb % 4 + 1) * 128], hT[:, kc, :], kc == 0, kc == 7)
                self.copy(xrpre[:, cb, 3:515], ps[:, :], eng="act")
                self.act(xr[:, cb, :], xrpre[:, cb, 3:515], AF.Identity, bias=cbias[:, cb:cb + 1], scale=CW[:, 24 + cb:25 + cb])
                for k in range(3):
                    self.stt(xr[:, cb, :], xrpre[:, cb, k:k + 512], CW[:, k * 8 + cb:k * 8 + cb + 1], xr[:, cb, :], ALU.mult, ALU.add)
                self.copy(xrb[:, cb, :], xr[:, cb, :], eng="act")
                self.copy(xrpre[:, cb, 0:3], xrpre[:, cb, 512:515])
            Wa = self.wload(d["lru_w_gate_a"].rearrange("(q p) j -> p q j", p=128), (128, 8, 256))
            Wx = self.wload(d["lru_w_gate_x"].rearrange("(q p) j -> p q j", p=128), (128, 8, 256))
            for cb in range(8):
                blk, jc = cb // 2, cb % 2
                if cb % 4 == 0:
                    wg = self.wload(w_in[:, (cb // 4) * 512:(cb // 4 + 1) * 512].rearrange("(kc p) n -> p kc n", p=128), (128, 8, 512))
                pa = self.acc()
                for ic in range(2):
                    self.mm(pa[:, :], Wa[:, blk * 2 + ic, jc * 128:(jc + 1) * 128], xrb[:, blk * 2 + ic, :], ic == 0, ic == 1)
                px = self.acc()
                for ic in range(2):
                    self.mm(px[:, :], Wx[:, blk * 2 + ic, jc * 128:(jc + 1) * 128], xrb[:, blk * 2 + ic, :], ic == 0, ic == 1)
                pg = self.acc()
                for kc in range(8):
                    self.mm(pg[:, :], wg[:, kc, (cb % 4) * 128:(cb % 4 + 1) * 128], hT[:, kc, :], kc == 0, kc == 7)
                b = cb % 2
                tr_, ti, ta, te, ths, tg = (tmp[q][b] for q in range(6))
                self.act(tr_, pa[:, :], AF.Sigmoid, bias=bga[:, cb:cb + 1], scale=1.0)
                self.act(ti, px[:, :], AF.Sigmoid, bias=bgx[:, cb:cb + 1], scale=1.0)
                self.act(ta, tr_, AF.Exp, scale=cL[:, cb:cb + 1])
                self.act(te, tr_, AF.Exp, scale=cL2[:, cb:cb + 1])
                self.act(te, te, AF.Sqrt, bias=self.onec, scale=-1.0)
                self.tt(ti, ti, xr[:, cb, :], ALU.mult)
                self.tt(ti, ti, te, ALU.mult)
                self.scan(ths, ta, ti, state[:, cb:cb + 1])
                self.copy(state[:, cb:cb + 1], ths[:, 511:512])
                self.act(tg, pg[:, :], AF.Gelu_apprx_tanh)
                self.tt(ylru[:, cb, :], ths, tg, ALU.mult)
            for ob in range(2):
                wo = self.wload(d["lru_w_out"][0][:, ob * 512:(ob + 1) * 512].rearrange("(kc p) n -> p kc n", p=128), (128, 8, 512))
                for j in range(4):
                    oc = ob * 4 + j
                    ps = self.acc()
                    for kc in range(8):
                        self.mm(ps[:, :], wo[:, kc, j * 128:(j + 1) * 128], ylru[:, kc, :], kc == 0, kc == 7)
                    self.copy(yT[:, oc, :], ps[:, :], eng="act" if oc % 2 else "dve")
            self.postnorm_add(t0, yT, V["G_m"], TMP)

    def act_acc(self, out, in_, func, accum_out):
        nc = self.nc
        self.P.op("act", lambda: nc.scalar.activation(out=out, in_=in_, func=func, accum_out=accum_out), [in_], [out, accum_out])

    def red(self, out, in_, op=ALU.add):
        nc = self.nc
        self.P.op("dve", lambda: nc.vector.tensor_reduce(out=out, in_=in_, axis=mybir.AxisListType.X, op=op), [in_], [out])

    def mixer_ssd(self, l, V):
        d, P = self.d, self.P
        occ = l // 3
        o = [0]

        def A(nb):
            r = o[0]
            o[0] += (nb + 63) // 64 * 64
            return r
        bc = lambda ap, shape, axis: ap.unsqueeze(axis).broadcast_to(list(shape))
        ng_bc = self.scr(A(8192), (128, 2048), F32)
        triLE = self.scr(A(512), (128, 128), F32)
        onesF = self.scr(A(512), (128, 128), F32)
        SLb = self.scr(A(1024), (128, 4, 128), BF16)
        negI = self.scr(A(256), (128, 128), BF16)
        triB = self.scr(A(256), (128, 128), BF16)
        onesB = self.scr(A(256), (128, 128), BF16)
        dhl = self.scr(A(512), (128, 2, 4, 32), BF16) if False else self.scr(A(512), (128, 8, 32), BF16)
        Dh_bc = self.scr(A(128), (128, 32), F32)
        convw = self.scr(A(512), (128, 128), F32)
        convb = self.scr(A(128), (128, 32), F32)
        hcol = self.scr(A(64), (128, 4), F32)
        stF = self.scr(A(8192), (128, 8, 256), F32)
        stB = self.scr(A(4096), (128, 8, 256), BF16)
        HAL = self.scr(A(384), (128, 32, 3), F32)
        hT = self.scr(A(8192), (128, 8, 512), BF16)
        dtT = self.scr(A(2048), (128, 512), F32)
        dtAT = self.scr(A(2048), (128, 512), F32)
        dt_tok, dtA_tok, acs_tok, nacs, eacs, wdec, cdec = (self.scr(A(512), (128, 4, 32), F32) for _ in range(7))
        xc = self.scr(A(4096), (128, 4, 512), BF16)
        z_tok = self.scr(A(2048), (128, 4, 256), BF16)
        x_tok = self.scr(A(2048), (128, 4, 256), BF16)
        xdt = self.scr(A(2048), (128, 4, 256), BF16)
        xdec = self.scr(A(2048), (128, 4, 256), BF16)
        B_tok = self.scr(A(1024), (128, 4, 128), BF16)
        ynT = self.scr(A(16384), (128, 16, 512), BF16)
        U = A(16640)
        yT = self.scr(U, (128, 8, 512), F32)
        pre = [self.scr(U + i * 2064, (128, 516), F32) for i in range(2)]
        cacc = [self.scr(U + 4128 + i * 2048, (128, 512), F32) for i in range(2)]
        R1 = [self.scr(U + 8224 + i * 2048, (128, 4, 128), F32) for i in range(2)]
        dec = [self.scr(U + 12320 + i * 2048, (128, 4, 128), F32) for i in range(2)]
        Mb = [self.scr(A(1024), (128, 4, 128), BF16) for _ in range(2)]
        t1 = [self.scr(A(1024), (128, 256), F32) for _ in range(2)]
        yg = [self.scr(A(1024), (128, 256), F32) for _ in range(2)]
        ynb = [self.scr(A(512), (128, 256), BF16) for _ in range(2)]
        ssq = self.scr(A(64), (128, 4), F32)
        TMP = A(8192)
        P.dma("sp", ng_bc, d["ssd_norm"].rearrange("a b -> (a b)")[occ * 2048:(occ + 1) * 2048].partition_broadcast(128), "sc0")
        P.dma("sp", triLE, d["ktab"][:, KT_TRILE:KT_TRILE + 128], "sc1")
        P.dma("sp", Dh_bc, d["ssd_d"][occ].partition_broadcast(128), "sc2")
        sl = self.scr(TMP, (128, 128), F32)
        P.dma("sp", sl, d["ktab"][:, KT_SL:KT_SL + 128], "sc3")
        self.copy(SLb, bc(sl, (128, 4, 128), 1))
        self.ts(negI, self.ident, -30000.0, None, ALU.mult)
        self.memset(onesF, 1.0)
        self.memset(onesB, 1.0)
        self.copy(triB, triLE)
        for k in range(4):
            self.load_rowsT(d["ssd_conv_w"][occ * 128 + k * 32: occ * 128 + (k + 1) * 32, :], 32, convw[:, k * 32:(k + 1) * 32], stoff=TMP + 1024)
        self.load_rowsT(d["ssd_conv_b"][occ * 32:(occ + 1) * 32, :], 32, convb, stoff=TMP + 1024)
        dtb_bc = self.scr(A(128), (128, 32), F32)
        a_bc = self.scr(A(128), (128, 32), F32)
        P.dma("sp", dtb_bc, d["ssd_dt_bias"][occ].partition_broadcast(128), "sc4")
        P.dma("sp", a_bc, d["ssd_a_log"][occ].partition_broadcast(128), "sc5")
        self.act(a_bc, a_bc, AF.Exp)
        self.ts(a_bc, a_bc, -1.0, None, ALU.mult)
        self.memset(stF, 0.0)
        self.memset(stB, 0.0)
        self.memset(HAL, 0.0)
        w_in = d["ssd_w_in"][occ]
        wv = lambda c0, n: w_in[:, c0:c0 + n].rearrange("(kc p) n -> p kc n", p=128)
        it = 0
        import os
        STG = int(os.environ.get("SSD_STAGE", "9"))
        for tt_ in range(4):
            t0 = tt_ * 512
            self.prenorm(t0, 512, V["A_m"], V["B_m"], hT, TMP)
            if STG < 2:
                continue
            SUB = int(os.environ.get("SSD_SUB", "99"))
            wdt = self.wload(wv(5664, 512), (128, 8, 512))
            pd = self.pb[4]
            pc = self.pb[5]
            dhi, dlo = dhl[:, 0:4, :], dhl[:, 4:8, :]
            f2 = lambda ap: ap.rearrange("p a b -> p (a b)")
            v3 = lambda ap: ap.rearrange("p (a b) -> p a b", a=4)

            def mmdt():
                for c in range(4):
                    for kc in range(8):
                        self.mm(pd[:, c * 32:(c + 1) * 32], hT[:, kc, c * 128:(c + 1) * 128], wdt[:, kc, 480:512], kc == 0, kc == 7)
            steps = [
                mmdt,
                lambda: self.tt(dt_tok, v3(pd[:, 0:128]), bc(dtb_bc, (128, 4, 32), 1), ALU.add),
                lambda: self.act(dt_tok, dt_tok, AF.Exp),
                lambda: self.act(dt_tok, dt_tok, AF.Ln, bias=self.onec, scale=1.0),
                lambda: self.tt(dtA_tok, dt_tok, bc(a_bc, (128, 4, 32), 1), ALU.mult),
                lambda: self.copy(dhi, dtA_tok),
                lambda: self.tt(dlo, dtA_tok, dhi, ALU.subtract),
                lambda: self.mm(pc[:, 0:128], triB, f2(dhi), True, False),
                lambda: self.mm(pc[:, 0:128], triB, f2(dlo), False, True),
                lambda: self.mm(pc[:, 128:256], onesB, f2(dhi), True, False),
                lambda: self.mm(pc[:, 128:256], onesB, f2(dlo), False, True),
                lambda: self.copy(acs_tok, v3(pc[:, 0:128])),
                lambda: self.ts(nacs, acs_tok, -1.0, None, ALU.mult),
                lambda: self.act(eacs, acs_tok, AF.Exp),
                lambda: self.copy(wdec, v3(pc[:, 128:256])),
                lambda: self.act(cdec, wdec, AF.Exp),
                lambda: self.tt(wdec, wdec, acs_tok, ALU.subtract),
                lambda: self.act(wdec, wdec, AF.Exp),
                lambda: self.tt(wdec, wdec, dt_tok, ALU.mult),
            ]
            for st_ in steps[:SUB]:
                st_()
            for g in range(8):
                if STG < 3:
                    continue
                s_, slotA = self.slot()
                wA = slotA[:, 0:4096].rearrange("p (a b) -> p a b", a=8)
                P.dma("pool", wA[:, :, 0:256], wv(2048 + g * 256, 256), "ring%d" % s_)
                P.dma("pool", wA[:, :, 256:384], wv(4096 + g * 128, 128), "ring%d" % s_)
                P.dma("pool", wA[:, :, 384:512], wv(5120 + g * 128, 128), "ring%d" % s_)
                wz = self.wload(wv(g * 256, 256), (128, 8, 256))
                qidx = [2 * g, 2 * g + 1, 16 + g, 24 + g]
                for q in range(4):
                    qq = qidx[q]
                    ps = self.acc()
                    for kc in range(8):
                        self.mm(ps[:, :], wA[:, kc, q * 128:(q + 1) * 128], hT[:, kc, :], kc == 0, kc == 7)
                    pr = pre[q % 2]
                    ca = cacc[q % 2]
                    self.copy(pr[:, 0:3], HAL[:, qq, :])
                    self.copy(pr[:, 3:515], ps[:, :], eng="act")
                    self.copy(HAL[:, qq, :], pr[:, 512:515])
                    self.act(ca, pr[:, 3:515], AF.Identity, bias=convb[:, qq:qq + 1], scale=convw[:, 96 + qq:97 + qq])
                    for k in range(3):
                        self.stt(ca, pr[:, k:k + 512], convw[:, k * 32 + qq:k * 32 + qq + 1], ca, ALU.mult, ALU.add)
                    self.act(xc[:, q, :], ca, AF.Silu)
                for c in range(4):
                    ps = self.acc()
                    for kc in range(8):
                        self.mm(ps[:, 0:256], hT[:, kc, c * 128:(c + 1) * 128], wz[:, kc, :], kc == 0, kc == 7)
                    self.act(z_tok[:, c, :], ps[:, 0:256], AF.Silu)
                for c in range(4):
                    if STG < 4:
                        continue
                    pt = self.pb[4 + c % 2]
                    ptb = pt[:, 0:256].bitcast(BF16)
                    for j in range(3):
                        self.tr(ptb[:, j * 128:(j + 1) * 128], xc[:, j, c * 128:(c + 1) * 128], self.identb)
                    self.copy(x_tok[:, c, :], ptb[:, 0:256], eng="act")
                    self.copy(B_tok[:, c, :], ptb[:, 256:384], eng="act")
                    xv = x_tok[:, c, :].rearrange("p (a b) -> p a b", a=4)
                    self.tt(xdt[:, c, :].rearrange("p (a b) -> p a b", a=4), xv, bc(dt_tok[:, c, g * 4:(g + 1) * 4], (128, 4, 64), 2), ALU.mult)
                    self.tt(xdec[:, c, :].rearrange("p (a b) -> p a b", a=4), xv, bc(wdec[:, c, g * 4:(g + 1) * 4], (128, 4, 64), 2), ALU.mult)
                v4 = lambda ap: ap.rearrange("p (a b) -> p a b", a=4)

                def front(c, g=g):
                    b = c % 2
                    cs = slice(c * 128, (c + 1) * 128)
                    pcb = self.pb[6]
                    self.mm(pcb[:, 0:128], xc[:, 2, cs], xc[:, 3, cs], True, True)
                    r1h = R1[b].rearrange("p a b -> p (a b)").bitcast(BF16).rearrange("p (h a b) -> p h a b", h=2, a=4)
                    self.tt(r1h[:, 0], bc(triB, (128, 4, 128), 1), bc(dhi[:, c, g * 4:(g + 1) * 4], (128, 4, 128), 2), ALU.mult)
                    self.tt(r1h[:, 1], bc(triB, (128, 4, 128), 1), bc(dlo[:, c, g * 4:(g + 1) * 4], (128, 4, 128), 2), ALU.mult)
                    psg = self.acc()
                    self.mm(psg[:, :], onesB, r1h[:, 0].rearrange("p a b -> p (a b)"), True, False)
                    self.mm(psg[:, :], onesB, r1h[:, 1].rearrange("p a b -> p (a b)"), False, False)
                    self.mm(psg[:, :], negI, SLb.rearrange("p a b -> p (a b)"), False, True)
                    for e in range(4):
                        self.act(dec[b][:, e, :], psg[:, e * 128:(e + 1) * 128], AF.Exp, bias=nacs[:, c, g * 4 + e:g * 4 + e + 1], scale=1.0)
                    self.tt(Mb[b], dec[b], bc(pcb[:, 0:128], (128, 4, 128), 1), ALU.mult)

                def back(c, g=g):
                    b = c % 2
                    cs = slice(c * 128, (c + 1) * 128)
                    py = self.acc()
                    for e in range(4):
                        self.mm(py[:, e * 64:(e + 1) * 64], Mb[b][:, e, :], xdt[:, c, e * 64:(e + 1) * 64], True, True)
                    self.mm(py[:, 256:512], xc[:, 3, cs], stB[:, g, :], True, True)
                    pst = self.pb[7]
                    self.mm(pst[:, 0:256], B_tok[:, c, :], xdec[:, c, :], True, True)
                    self.tt(v4(stF[:, g, :]), v4(stF[:, g, :]), bc(cdec[:, c, g * 4:(g + 1) * 4], (128, 4, 64), 2), ALU.mult)
                    self.tt(stF[:, g, :], stF[:, g, :], pst[:, 0:256], ALU.add)
                    self.copy(stB[:, g, :], stF[:, g, :], eng="act")
                    self.tt(v4(t1[b]), v4(py[:, 256:512]), bc(eacs[:, c, g * 4:(g + 1) * 4], (128, 4, 64), 2), ALU.mult)
                    self.tt(t1[b], t1[b], py[:, 0:256], ALU.add)
                    self.tt(v4(yg[b]), v4(x_tok[:, c, :]), bc(Dh_bc[:, g * 4:(g + 1) * 4], (128, 4, 64), 2), ALU.mult)
                    self.tt(t1[b], t1[b], yg[b], ALU.add)
                    self.tt(yg[b], t1[b], z_tok[:, c, :], ALU.mult)
                    self.act(t1[b], yg[b], AF.Square)
                    self.red(ssq[:, b:b + 1], t1[b])
                    self.act(ssq[:, 2 + b:3 + b], ssq[:, b:b + 1], AF.Sqrt, bias=self.epsc, scale=1.0 / 256.0)
                    self.recip(ssq[:, 2 + b:3 + b], ssq[:, 2 + b:3 + b])
                    self.stt(ynb[b], yg[b], ssq[:, 2 + b:3 + b], ng_bc[:, g * 256:(g + 1) * 256], ALU.mult, ALU.mult)
                    pt = self.pb[4 + c % 2]
                    ptb = pt[:, 0:256].bitcast(BF16)
                    for j in range(2):
                        self.tr(ptb[:, j * 128:(j + 1) * 128], ynb[b][:, j * 128:(j + 1) * 128], self.identb)
                    self.copy(ynT[:, 2 * g:2 * g + 2, cs], ptb[:, 0:256].rearrange("p (a b) -> p a b", a=2), eng="act")

                if STG >= 5:
                    front(0)
                    for c in range(4):
                        if c < 3:
                            front(c + 1)
                        back(c)
            if STG < 6:
                continue
            w_out = d["ssd_w_out"][occ]
            for ob in range(2):
                wo = [self.wload(w_out[hf * 1024:(hf + 1) * 1024, ob * 512:(ob + 1) * 512].rearrange("(kc p) n -> p kc n", p=128), (128, 8, 512)) for hf in range(2)]
                for j in range(4):
                    oc = ob * 4 + j
                    ps = self.acc()
                    for q in range(16):
                        self.mm(ps[:, :], wo[q // 8][:, q % 8, j * 128:(j + 1) * 128], ynT[:, q, :], q == 0, q == 15)
                    self.copy(yT[:, oc, :], ps[:, :], eng="act" if oc % 2 else "dve")
            self.postnorm_add(t0, yT, V["G_m"], TMP)

    def mixer_attn(self, l, V):
        d, P = self.d, self.P
        o = [0]

        def A(nb):
            r = o[0]
            o[0] += (nb + 63) // 64 * 64
            return r
        bc = lambda ap, shape, axis: ap.unsqueeze(axis).broadcast_to(list(shape))
        HT0 = A(32768)
        hT = self.scr(HT0, (128, 8, 2048), BF16)
        yT = self.scr(HT0 + 16384, (128, 8, 512), F32)
        masks = self.scr(A(ATT_NMASK * 256), (128, ATT_NMASK, 128), BF16)
        cosT = self.scr(A(4096), (128, 2048), BF16)
        sinT = self.scr(A(4096), (128, 2048), BF16)
        Pm = self.scr(A(128), (128, 64), BF16)
        QT0 = o[0]
        oT = self.scr(QT0, (128, 4, 2048), BF16)
        qT = self.scr(A(4096), (128, 2048), BF16)
        kT = self.scr(A(4096), (128, 2048), BF16)
        vv = self.scr(A(2048), (128, 16, 64), BF16)
        qraw = [self.scr(A(1024), (128, 512), BF16) for _ in range(2)]
        T1 = A(4096)
        t1 = [self.scr(T1 + i * 2048, (128, 512), F32) for i in range(2)]
        S0 = A(8192)
        sbuf = [self.scr(S0 + i * 4096, (128, 1024), F32) for i in range(2)]
        TMP = S0
        pbuf = [self.scr(A(2048), (128, 1024), BF16) for _ in range(2)]
        pT = [self.scr(A(1024), (128, 4, 128), BF16) for _ in range(2)]
        OST0 = A(8192)
        Ost = self.scr(OST0, (128, 16, 4, 64), BF16)
        stat = self.scr(A(512), (128, 16, 8), F32)
        OTK0 = A(16384)
        o_tok = self.scr(OTK0, (128, 16, 512), BF16)
        sm = self.scr(A(1024), (128, 256), F32)
        oacc = self.scr(A(1024), (128, 4, 64), F32)
        assert o[0] <= self.SCRB, o[0]
        att = self.ct(128)
        P.dma("sp", att, d["ktab"][:, KT_ATT:KT_ATT + 128], "att0")
        self.copy(Pm[0:64, :], att[0:64, 0:64])
        invf, sgn = att[0:64, 64:65], att[0:64, 65:66]
        P.dma("pool", masks, d["amask"].rearrange("p (a b) -> p a b", a=ATT_NMASK), "att1", max_dma_last_dim=2048)
        posi = self.scr(OTK0, (128, 2048), I32)
        posf = self.scr(OTK0 + 8192, (128, 2048), F32)
        r1 = self.scr(OST0, (128, 2048), F32)
        ki = self.scr(OTK0, (128, 2048), I32)
        R = slice(0, 64)
        P.dma("sp", posi[R, :], d["positions"][0].partition_broadcast(64), "att2")
        self.copy(posf[R, :], posi[R, :])
        self.ts(r1[R, :], posf[R, :], invf, None, ALU.mult)
        self.copy(ki[R, :], r1[R, :])
        self.copy(posf[R, :], ki[R, :])
        self.tt(r1[R, :], r1[R, :], posf[R, :], ALU.subtract)
        tmpm = self.scr(OTK0, (128, 2048), F32)
        for which, dst in ((0, sinT), (1, cosT)):
            f = posf
            if which == 0:
                self.copy(f[R, :], r1[R, :])
            else:
                self.ts(f[R, :], r1[R, :], 0.25, None, ALU.add)
            self.ts(tmpm[R, :], f[R, :], 0.5, None, ALU.is_gt)
            self.tt(f[R, :], f[R, :], tmpm[R, :], ALU.subtract)
            self.ts(tmpm[R, :], f[R, :], -0.5, None, ALU.is_lt)
            self.tt(f[R, :], f[R, :], tmpm[R, :], ALU.add)
            self.ts(f[R, :], f[R, :], 0.49999997, -0.49999997, ALU.min, ALU.max)
            if which == 0:
                self.act(tmpm[R, :], f[R, :], AF.Sin, scale=2.0 * math.pi)
                self.ts(dst[R, :], tmpm[R, :], sgn, None, ALU.mult)
            else:
                self.act(dst[R, :], f[R, :], AF.Sin, scale=2.0 * math.pi)
        self.prenorm(0, 2048, V["A_m"], V["B_m"], hT, TMP)
        self.memset(Ost, 0.0)
        import os
        ASTG = int(os.environ.get("ATT_STAGE", "9"))
        if ASTG < 2:
            return
        w_qkv = d["attn_w_qkv"][0]
        wv_ = lambda c0, n: w_qkv[:, c0:c0 + n].rearrange("(kc p) n -> p kc n", p=128)
        pbTb = [self.pb[4][:, :].bitcast(BF16), self.pb[5][:, :].bitcast(BF16)]
        it = 0
        tcount = 0
        pi_ = 0
        for h in range(8):
            self.memset(stat[:, :, 0:8:2], -30000.0)
            self.memset(stat[:, :, 1:8:2], 0.0)
            for g in range(3):
                s_, slt = self.slot()
                w = slt[:, 0:1536].rearrange("p (a b) -> p a b", a=8)
                for j in range(3):
                    P.dma("pool", w[:, :, j * 64:(j + 1) * 64], wv_(g * 1536 + j * 512 + h * 64, 64), "ring%d" % s_)
                for j, dst in ((0, qT), (1, kT)):
                    for tq in range(4):
                        ts_ = slice(tq * 512, (tq + 1) * 512)
                        ps = self.pb[6 + pi_ % 2]
                        sw = self.pb[4 + pi_ % 2]
                        qr = qraw[pi_ % 2]
                        ta = t1[pi_ % 2]
                        pi_ += 1
                        for kc in range(8):
                            self.mm(ps[R, :], w[:, kc, j * 64:(j + 1) * 64], hT[:, kc, ts_], kc == 0, kc == 7)
                        self.copy(qr[R, :], ps[R, :], eng="act")
                        self.mm(sw[R, :], Pm[R, :], qr[R, :], True, True)
                        self.tt(ta[R, :], qr[R, :], cosT[R, ts_], ALU.mult)
                        self.tt(dst[R, ts_], sw[R, :], sinT[R, ts_], ALU.mult)
                        self.tt(dst[R, ts_], dst[R, ts_], ta[R, :], ALU.add)
                for half in range(2):
                    ps = self.pb[6 + half]
                    for k8 in range(8):
                        kt = half * 8 + k8
                        for kc in range(8):
                            self.mm(ps[:, k8 * 64:(k8 + 1) * 64], hT[:, kc, kt * 128:(kt + 1) * 128], w[:, kc, 128:192], kc == 0, kc == 7)
                    self.copy(vv[:, half * 8:(half + 1) * 8, :], ps[:, :].rearrange("p (a b) -> p a b", a=8), eng="act")
                plist = []
                for qt in range(16):
                    if g == 0:
                        plist.append((qt, 0, list(range(0, min(1, qt) + 1))))
                    elif g == 1:
                        plist.append((qt, 1, list(range(0, min(4, qt) + 1))))
                    else:
                        plist.append((qt, 2, list(range(0, min(7, qt) + 1))))
                        if qt >= 8:
                            plist.append((qt, 3, list(range(8, qt + 1))))

                def pfront(i, g=g, plist=plist):
                    qt, pidx, deltas = plist[i]
                    n = len(deltas)
                    b = i % 2
                    Rg = (self.pb[2 * b], self.pb[2 * b + 1])
                    for j, dl in enumerate(deltas):
                        kt = qt - dl
                        bank, col = Rg[j // 4], (j % 4) * 128
                        self.mm(bank[:, col:col + 128], qT[R, qt * 128:(qt + 1) * 128], kT[R, kt * 128:(kt + 1) * 128], True, True)
                    sb_, pb_ = sbuf[b], pbuf[b]
                    mi = ATT_MBASE[g] + deltas[0]
                    for bk in range((n + 3) // 4):
                        cnt = min(4, n - bk * 4)
                        mk = masks[:, mi + bk * 4: mi + bk * 4 + cnt, :].rearrange("p a b -> p (a b)")
                        self.stt(sb_[:, bk * 512: bk * 512 + cnt * 128], Rg[bk][:, 0:cnt * 128], 0.125, mk, ALU.mult, ALU.add)
                    mcol = stat[:, qt, 2 * pidx:2 * pidx + 1]
                    lcol = stat[:, qt, 2 * pidx + 1:2 * pidx + 2]
                    nm = sm[:, 8 + b:9 + b]
                    self.red(mcol, sb_[:, 0:n * 128], ALU.max)
                    self.ts(nm, mcol, -1.0, None, ALU.mult)
                    self.act(pb_[:, 0:n * 128], sb_[:, 0:n * 128], AF.Exp, bias=nm, scale=1.0)
                    self.red(lcol, pb_[:, 0:n * 128], ALU.add)

                def pback(i, plist=plist):
                    nonlocal tcount
                    qt, pidx, deltas = plist[i]
                    n = len(deltas)
                    b = i % 2
                    pb_ = pbuf[b]
                    Ops = self.pb[6 + b][:, 0:64]
                    for b4 in range((n + 3) // 4):
                        cnt = min(4, n - b4 * 4)
                        hf = tcount % 2
                        tcount += 1
                        tb = pbTb[hf][:, 0: cnt * 128]
                        for jj in range(cnt):
                            j = b4 * 4 + jj
                            self.tr(tb[:, jj * 128:(jj + 1) * 128], pb_[:, j * 128:(j + 1) * 128], self.identb)
                        self.copy(pT[hf][:, 0:cnt, :], tb.rearrange("p (a b) -> p a b", a=cnt), eng="act")
                        for jj in range(cnt):
                            j = b4 * 4 + jj
                            kt = qt - deltas[j]
                            self.mm(Ops, pT[hf][:, jj, :], vv[:, kt, :], j == 0, j == n - 1)
                    self.copy(Ost[:, qt, pidx, :], Ops, eng="act")

                if ASTG >= 3:
                    pfront(0)
                    for i in range(len(plist)):
                        if i + 1 < len(plist):
                            pfront(i + 1)
                        pback(i)
            if ASTG < 4:
                continue
            mview, lview = stat[:, :, 0:8:2], stat[:, :, 1:8:2]
            M = sm[:, 16:32]
            e = sm[:, 32:96].rearrange("p (a b) -> p a b", a=16)
            L = sm[:, 96:112]
            el = sm[:, 112:176].rearrange("p (a b) -> p a b", a=16)
            self.red(M, mview, ALU.max)
            self.tt(e, mview, bc(M, (128, 16, 4), 2), ALU.subtract)
            self.act(e, e, AF.Exp)
            self.tt(el, e, lview, ALU.mult)
            self.red(L, el, ALU.add)
            self.recip(L, L)
            self.tt(e, e, bc(L, (128, 16, 4), 2), ALU.mult)
            for q4 in range(4):
                tO = self.scr(T1, (128, 4, 4, 64), F32)
                self.tt(tO, Ost[:, q4 * 4:(q4 + 1) * 4, :, :], bc(e[:, q4 * 4:(q4 + 1) * 4, :], (128, 4, 4, 64), 3), ALU.mult)
                self.red(oacc, tO.rearrange("p q i d -> p q d i"), ALU.add)
                self.copy(o_tok[:, q4 * 4:(q4 + 1) * 4, h * 64:(h + 1) * 64], oacc, eng="act")
        if ASTG < 5:
            return
        for qt in range(16):
            hf = qt % 2
            tb = pbTb[hf][:, 0:512]
            for c in range(4):
                self.tr(tb[:, c * 128:(c + 1) * 128], o_tok[:, qt, c * 128:(c + 1) * 128], self.identb)
            ATX = int(os.environ.get("ATT_X", "0"))
            if ATX == 1:
                self.copy(oT[:, :, qt * 128:(qt + 1) * 128], tb.rearrange("p (a b) -> p a b", a=4), eng="dve")
            elif ATX == 0:
                self.copy(oT[:, :, qt * 128:(qt + 1) * 128], tb.rearrange("p (a b) -> p a b", a=4), eng="act")
        ASUB = int(os.environ.get("ATT_SUB", "9"))
        if ASUB < 1:
            return
        wo = [self.wload(d["attn_w_out"][0][:, hf * 512:(hf + 1) * 512].rearrange("(c p) n -> p c n", p=128), (128, 4, 512)) for hf in range(2)]
        for tq in range(4):
            for oc in range(8):
                ps = self.acc()
                for c in range(4):
                    self.mm(ps[:, :], wo[oc // 4][:, c, (oc % 4) * 128:(oc % 4 + 1) * 128], oT[:, c, tq * 512:(tq + 1) * 512], c == 0, c == 3)
                self.copy(yT[:, oc, :], ps[:, :], eng="act" if oc % 2 else "dve")
            if ASUB >= 2:
                self.postnorm_add(tq * 512, yT, V["G_m"], TMP)

    def build(self):
        self.setup()
        self.load_x()
        for l in self.layers:
            V = self.ada_layer(l)
            kind = l % 3
            if kind == 0:
                self.mixer_ssd(l, V)
            elif kind == 1:
                self.mixer_lru(l, V)
            else:
                self.mixer_attn(l, V)
            self.mlp(l, V)
        self.store_x()
        self.stats = self.P.emit(final_dkeys=["out0", "out1"])


_NC_CACHE = {}


def _prep_common(inputs):
    f = lambda a: np.ascontiguousarray(np.asarray(a, dtype=np.float32))
    com = {}
    com["ktab"] = make_ktab()
    com["amask"] = make_amask()
    com["ada_w"] = f(inputs["ada_w"])
    com["ada_b"] = f(inputs["ada_b"]).reshape(192, 128)
    for nm in ("norm_mix_pre", "norm_mix_post", "norm_mlp_pre", "norm_mlp_post"):
        com[nm] = f(inputs[nm]).reshape(32, 128)
    com["mlp_w1"] = f(inputs["mlp_w1"])
    com["mlp_w2"] = f(inputs["mlp_w2"])
    com["ssd_w_in"] = f(inputs["ssd_w_in"])
    com["ssd_conv_w"] = f(inputs["ssd_conv_w"]).reshape(256, 128)
    com["ssd_conv_b"] = f(inputs["ssd_conv_b"]).reshape(64, 128)
    com["ssd_dt_bias"] = f(inputs["ssd_dt_bias"])
    com["ssd_a_log"] = f(inputs["ssd_a_log"])
    com["ssd_d"] = f(inputs["ssd_d"])
    com["ssd_norm"] = f(inputs["ssd_norm"]).reshape(32, 128)
    com["ssd_w_out"] = f(inputs["ssd_w_out"])
    com["lru_w_in"] = f(inputs["lru_w_in"])
    com["lru_conv_w"] = f(inputs["lru_conv_w"]).reshape(32, 128)
    com["lru_conv_b"] = f(inputs["lru_conv_b"]).reshape(8, 128)
    com["lru_w_gate_a"] = f(inputs["lru_w_gate_a"]).reshape(1024, 256)
    com["lru_b_gate_a"] = f(inputs["lru_b_gate_a"]).reshape(8, 128)
    com["lru_w_gate_x"] = f(inputs["lru_w_gate_x"]).reshape(1024, 256)
    com["lru_b_gate_x"] = f(inputs["lru_b_gate_x"]).reshape(8, 128)
    com["lru_lambda"] = f(inputs["lru_lambda"]).reshape(8, 128)
    com["lru_w_out"] = f(inputs["lru_w_out"])
    com["attn_w_qkv"] = f(inputs["attn_w_qkv"])
    com["attn_w_out"] = f(inputs["attn_w_out"])
    return com


def run_layers(inputs, x, layers, cores=8, trace=False):
    key = tuple(layers)
    if key not in _NC_CACHE:
        _NC_CACHE[key] = K(list(layers))
    kb = _NC_CACHE[key]
    com = _prep_common(inputs)
    c = np.asarray(inputs["c"], np.float32)
    pos = np.asarray(inputs["positions"], np.int32)
    in_maps = []
    for b in range(cores):
        m = dict(com)
        m["x"] = np.ascontiguousarray(x[b])
        m["c"] = np.ascontiguousarray(c[b].reshape(8, 128))
        m["positions"] = np.ascontiguousarray(pos[b].reshape(1, S))
        in_maps.append({k: v for k, v in m.items() if k in kb.d})
    res = run_bass_kernel_spmd(kb.nc, in_maps, core_ids=list(range(cores)), trace=trace)
    out = np.stack([r["out"] for r in res.results], axis=0)
    return out, res


def kernel(**inputs):
    x = np.asarray(inputs["x"], np.float32)
    out, _ = run_layers(inputs, x, (0, 1, 2, 3))
    return out.astype(np.float32)
```

```python
import math
import numpy as np
import concourse.bass as bass
import concourse.mybir as mybir
from concourse.bass_utils import run_bass_kernel_spmd

F32 = mybir.dt.float32
BF16 = mybir.dt.bfloat16
I32 = mybir.dt.int32
AF = mybir.ActivationFunctionType
ALU = mybir.AluOpType

D = 1024
S = 2048
DEPTH = 4
EPS = 1e-6
NCH = 8


def _box(ap):
    t = ap.tensor
    if type(t).__name__ == "DRamTensorHandle":
        return None
    if type(t).__name__ == "PSumTensorHandle":
        return (t.name, 0, 128, 0, 2048)
    es = mybir.dt.size(ap.dtype)
    dims = list(ap.ap)
    pstep, pcnt = dims[0]
    off = ap.offset
    if pstep == 0:
        p0, f0 = 0, off
    else:
        p0 = off // pstep
        f0 = off - p0 * pstep
    f1 = f0 + sum((c - 1) * abs(s) for s, c in dims[1:]) + 1
    return (t.name, p0, p0 + pcnt, f0 * es, f1 * es)


def _overlap(a, b):
    return a[1] < b[2] and b[1] < a[2] and a[3] < b[4] and b[3] < a[4]


def _covers(a, b):
    return a[1] <= b[1] and a[2] >= b[2] and a[3] <= b[3] and a[4] >= b[4]


class Prog:
    def __init__(self, nc):
        self.nc = nc
        self.E = dict(pe=nc.tensor, dve=nc.vector, act=nc.scalar, pool=nc.gpsimd, sp=nc.sync)
        self.ops = []
        self._ctx = []

    def sb(self, name, shape, dt):
        cm = self.nc.sbuf_tensor(name, list(shape), dt)
        t = cm.__enter__()
        self._ctx.append(cm)
        return t

    def ps(self, name, shape, dt=F32):
        cm = self.nc.psum_tensor(name, list(shape), dt)
        t = cm.__enter__()
        self._ctx.append(cm)
        return t

    def sem(self, name):
        cm = self.nc.semaphore(name)
        s = cm.__enter__()
        self._ctx.append(cm)
        return s

    def op(self, eng, fn, reads=(), writes=(), dkey=None):
        r = [b for b in (_box(a) for a in reads) if b is not None]
        w = [b for b in (_box(a) for a in writes) if b is not None]
        self.ops.append(dict(eng=eng, fn=fn, r=r, w=w, dkey=dkey))

    def dma(self, q, out, in_, dkey, **kw):
        e = self.E[q]
        self.op(q, lambda: e.dma_start(out=out, in_=in_, **kw), reads=[in_], writes=[out], dkey=dkey)

    def emit(self, final_dkeys=()):
        ops = self.ops
        n = len(ops)
        hist = {}
        deps = [None] * n
        for i, o in enumerate(ops):
            d = set()
            for b in o["r"]:
                psum = b[0].startswith("pb")
                for ent in hist.get(b[0], ()):
                    if _overlap(ent[0], b) and (ent[2] or (psum and ops[ent[1]]["eng"] != o["eng"])):
                        d.add(ent[1])
            for b in o["w"]:
                for ent in hist.get(b[0], ()):
                    if _overlap(ent[0], b):
                        d.add(ent[1])
            deps[i] = d
            for b in o["w"]:
                lst = hist.setdefault(b[0], [])
                lst[:] = [e for e in lst if not _covers(b, e[0])]
                lst.append([b, i, True])
            for b in o["r"]:
                lst = hist.setdefault(b[0], [])
                rep = False
                if o["dkey"] is None:
                    for e in lst:
                        if (not e[2]) and e[0] == b and ops[e[1]]["eng"] == o["eng"] and ops[e[1]]["dkey"] is None:
                            e[1] = i
                            rep = True
                            break
                if not rep:
                    lst.append([b, i, False])
        need_sig = [False] * n
        red = [None] * n
        for i, o in enumerate(ops):
            best = {}
            dm = []
            for j in deps[i]:
                oj = ops[j]
                if oj["dkey"] is not None:
                    dm.append(j)
                else:
                    if oj["eng"] == "pe" and o["eng"] == "pe" and o["dkey"] is None:
                        continue
                    if j > best.get(oj["eng"], -1):
                        best[oj["eng"]] = j
            red[i] = (best, dm)
            for j in best.values():
                need_sig[j] = True
        esem = {k: self.sem("s_" + k) for k in self.E}
        dsem = {}
        for o in ops:
            if o["dkey"] is not None and o["dkey"] not in dsem:
                dsem[o["dkey"]] = self.sem("d_" + str(o["dkey"]))
        sigcnt = {k: 0 for k in self.E}
        sigval = [0] * n
        dcnt = {k: 0 for k in dsem}
        dval = [0] * n
        waited = {k: {} for k in self.E}
        nwaits = 0
        for i, o in enumerate(ops):
            eng = o["eng"]
            e = self.E[eng]
            best, dm = red[i]
            ws = {}
            for k, j in best.items():
                ws[("e", k)] = max(ws.get(("e", k), 0), sigval[j])
            for j in dm:
                key = ("d", ops[j]["dkey"])
                ws[key] = max(ws.get(key, 0), dcnt[ops[j]["dkey"]])
            for key, v in ws.items():
                if waited[eng].get(key, 0) >= v:
                    continue
                waited[eng][key] = v
                s = esem[key[1]] if key[0] == "e" else dsem[key[1]]
                e.wait_ge(s, v)
                nwaits += 1
            ins = o["fn"]()
            if o["dkey"] is not None:
                dcnt[o["dkey"]] += 16
                dval[i] = dcnt[o["dkey"]]
                ins.then_inc(dsem[o["dkey"]], 16)
            elif need_sig[i]:
                sigcnt[eng] += 1
                sigval[i] = sigcnt[eng]
                ins.then_inc(esem[eng], 1)
        for k in final_dkeys:
            self.E["sp"].wait_ge(dsem[k], dcnt[k])
        self.stats = dict(n_ops=n, n_waits=nwaits, sig=dict(sigcnt))
        return self.stats


KT_IDENT = 0
KT_TRILE = 128
KT_SL = 256
KT_ATT = 384
KT_COLS = 512
ATT_CFG = ((128, 1), (512, 4), (2048, 16))
ATT_MBASE = (0, 2, 7)
ATT_NMASK = 23


def make_ktab():
    kt = np.zeros((128, KT_COLS), np.float32)
    kt[:, KT_IDENT:KT_IDENT + 128] = np.eye(128, dtype=np.float32)
    j = np.arange(128)[:, None]
    q = np.arange(128)[None, :]
    kt[:, KT_TRILE:KT_TRILE + 128] = (j <= q).astype(np.float32)
    kt[:, KT_SL:KT_SL + 128] = (q < j).astype(np.float32)
    for m in range(8):
        kt[m + 8, KT_ATT + m] = 1.0
        kt[m, KT_ATT + m + 8] = 1.0
    inv_freq = (500000.0 ** (-np.arange(0, 16, 2, dtype=np.float32) / 16.0)).astype(np.float32)
    for m in range(16):
        kt[m, KT_ATT + 64] = inv_freq[m % 8] / (2.0 * np.pi)
        kt[m, KT_ATT + 65] = -1.0 if m < 8 else 1.0
    return kt


def make_amask():
    am = np.full((128, ATT_NMASK, 128), -30000.0, np.float32)
    qi = np.arange(128)[:, None]
    ki = np.arange(128)[None, :]
    for g, (window, dil) in enumerate(ATT_CFG):
        nd = (2, 5, 16)[g]
        for dl in range(nd):
            dd = 128 * dl + qi - ki
            ok = (dd >= 0) & (dd % dil == 0) & (dd <= window)
            am[:, ATT_MBASE[g] + dl, :][ok] = 0.0
    return am.reshape(128, ATT_NMASK * 128)


class K:
    def __init__(self, layers):
        self.layers = layers
        nc = self.nc = bass.Bass("TRN2", target_bir_lowering=False)
        P = self.P = Prog(nc)
        dt = lambda name, shape, ty=F32, kind="ExternalInput": nc.dram_tensor(name, list(shape), ty, kind=kind).ap()
        self.d = d = {}
        d["x"] = dt("x", [S, D])
        d["c"] = dt("c", [8, 128])
        d["ktab"] = dt("ktab", [128, KT_COLS])
        d["ada_w"] = dt("ada_w", [4, D, 6 * D])
        d["ada_b"] = dt("ada_b", [4 * 48, 128])
        for nm in ("norm_mix_pre", "norm_mix_post", "norm_mlp_pre", "norm_mlp_post"):
            d[nm] = dt(nm, [4 * 8, 128])
        d["mlp_w1"] = dt("mlp_w1", [4, D, 4 * D])
        d["mlp_w2"] = dt("mlp_w2", [4, 4 * D, D])
        kinds = set(l % 3 for l in layers)
        if 0 in kinds:
            d["ssd_w_in"] = dt("ssd_w_in", [2, D, 6176])
            d["ssd_conv_w"] = dt("ssd_conv_w", [2 * 4 * 32, 128])
            d["ssd_conv_b"] = dt("ssd_conv_b", [2 * 32, 128])
            d["ssd_dt_bias"] = dt("ssd_dt_bias", [2, 32])
            d["ssd_a_log"] = dt("ssd_a_log", [2, 32])
            d["ssd_d"] = dt("ssd_d", [2, 32])
            d["ssd_norm"] = dt("ssd_norm", [2 * 16, 128])
            d["ssd_w_out"] = dt("ssd_w_out", [2, 2048, D])
        if 1 in kinds:
            d["lru_w_in"] = dt("lru_w_in", [1, D, 2048])
            d["lru_conv_w"] = dt("lru_conv_w", [4 * 8, 128])
            d["lru_conv_b"] = dt("lru_conv_b", [8, 128])
            d["lru_w_gate_a"] = dt("lru_w_gate_a", [1024, 256])
            d["lru_b_gate_a"] = dt("lru_b_gate_a", [8, 128])
            d["lru_w_gate_x"] = dt("lru_w_gate_x", [1024, 256])
            d["lru_b_gate_x"] = dt("lru_b_gate_x", [8, 128])
            d["lru_lambda"] = dt("lru_lambda", [8, 128])
            d["lru_w_out"] = dt("lru_w_out", [1, D, D])
        if 2 in kinds:
            d["attn_w_qkv"] = dt("attn_w_qkv", [1, D, 4608])
            d["attn_w_out"] = dt("attn_w_out", [1, 512, D])
            d["amask"] = dt("amask", [128, ATT_NMASK * 128])
            d["positions"] = dt("positions", [1, S], I32)
        d["out"] = dt("out", [S, D], F32, kind="ExternalOutput")

        self.XT = P.sb("XT", [128, NCH, S], F32)
        self.NRING = 4
        self.ring = [P.sb("ring%d" % i, [128, 4096], BF16) for i in range(self.NRING)]
        self.ring_i = 0
        self.SCRB = 104 * 1024
        self.SCR = P.sb("SCR", [128, self.SCRB // 2], BF16)
        self.CT = P.sb("CT", [128, 1024], F32)
        self.CB = P.sb("CB", [128, 512], BF16)
        self.ct_off = 0
        self.cb_off = 0
        self.pb = [P.ps("pb%d" % i, [128, 512], F32) for i in range(8)]
        self.acc_i = 0

        self.build()

    def scr(self, off, shape, dt):
        es = mybir.dt.size(dt)
        n = int(np.prod(shape[1:]))
        assert off % 4 == 0 and off + n * es <= self.SCRB, (off, shape)
        a = self.SCR[:, off // 2: off // 2 + n * es // 2]
        if dt != BF16:
            a = a.bitcast(dt)
        if len(shape) == 3:
            a = a.rearrange("p (a b) -> p a b", a=shape[1])
        elif len(shape) == 4:
            a = a.rearrange("p (a b c) -> p a b c", a=shape[1], b=shape[2])
        return a

    def ct(self, n):
        a = self.CT[:, self.ct_off:self.ct_off + n]
        self.ct_off += n
        assert self.ct_off <= 1024
        return a

    def cb(self, n):
        a = self.CB[:, self.cb_off:self.cb_off + n]
        self.cb_off += n
        assert self.cb_off <= 512
        return a

    def acc(self):
        b = self.pb[self.acc_i % 4]
        self.acc_i += 1
        return b

    def slot(self):
        s = self.ring_i % self.NRING
        self.ring_i += 1
        return s, self.ring[s]

    def wload(self, src, shape):
        s, t = self.slot()
        n = int(np.prod(shape[1:]))
        assert n <= 4096
        v = t[:, 0:n]
        if len(shape) == 3:
            v = v.rearrange("p (a b) -> p a b", a=shape[1])
        self.P.dma("pool", v, src, "ring%d" % s)
        return v

    def mm(self, out, lhsT, rhs, start, stop):
        nc = self.nc
        self.P.op("pe", lambda: nc.tensor.matmul(out, lhsT=lhsT, rhs=rhs, start=start, stop=stop), [lhsT, rhs], [out])

    def tr(self, out, in_, ident):
        nc = self.nc
        self.P.op("pe", lambda: nc.tensor.transpose(out=out, in_=in_, identity=ident), [in_, ident], [out])

    def act(self, out, in_, func, bias=None, scale=None):
        nc = self.nc
        kw = {}
        rd = [in_]
        if bias is not None:
            kw["bias"] = bias
            if not isinstance(bias, (int, float)):
                rd.append(bias)
        if scale is not None:
            kw["scale"] = scale
            if not isinstance(scale, (int, float)):
                rd.append(scale)
        self.P.op("act", lambda: nc.scalar.activation(out=out, in_=in_, func=func, **kw), rd, [out])

    def tt(self, out, in0, in1, op, eng="dve"):
        e = self.P.E[eng]
        self.P.op(eng, lambda: e.tensor_tensor(out=out, in0=in0, in1=in1, op=op), [in0, in1], [out])

    def ts(self, out, in0, s1, s2, op0, op1=None, eng="dve"):
        e = self.P.E[eng]
        rd = [in0] + [s for s in (s1, s2) if s is not None and not isinstance(s, (int, float))]
        if op1 is None:
            self.P.op(eng, lambda: e.tensor_scalar(out=out, in0=in0, scalar1=s1, scalar2=None, op0=op0), rd, [out])
        else:
            self.P.op(eng, lambda: e.tensor_scalar(out=out, in0=in0, scalar1=s1, scalar2=s2, op0=op0, op1=op1), rd, [out])

    def stt(self, out, in0, scalar, in1, op0, op1):
        nc = self.nc
        rd = [in0, in1] + ([] if isinstance(scalar, (int, float)) else [scalar])
        self.P.op("dve", lambda: nc.vector.scalar_tensor_tensor(out=out, in0=in0, scalar=scalar, in1=in1, op0=op0, op1=op1), rd, [out])

    def copy(self, out, in_, eng="dve"):
        if eng == "act":
            self.act(out, in_, AF.Copy)
        else:
            e = self.P.E[eng]
            self.P.op(eng, lambda: e.tensor_copy(out=out, in_=in_), [in_], [out])

    def recip(self, out, in_):
        nc = self.nc
        self.P.op("dve", lambda: nc.vector.reciprocal(out=out, in_=in_), [in_], [out])

    def memset(self, ap, val, eng="dve"):
        e = self.P.E[eng]
        self.P.op(eng, lambda: e.memset(ap, val), [], [ap])

    def scan(self, out, d0, d1, initial):
        nc = self.nc
        rd = [d0, d1] + ([] if isinstance(initial, (int, float)) else [initial])
        self.P.op("dve", lambda: nc.vector.tensor_tensor_scan(out=out, data0=d0, data1=d1, initial=initial, op0=ALU.mult, op1=ALU.add), rd, [out])

    def load_rowsT(self, rows_ap, nrows, dst, ncols=128, stoff=0):
        st = self.scr(stoff, (128, 128), F32)
        self.P.dma("sp", st[0:nrows, 0:ncols], rows_ap, "cst")
        ps = self.pb[7]
        self.tr(ps[0:ncols, 0:nrows], st[0:nrows, 0:ncols], self.ident[0:nrows, 0:nrows])
        self.copy(dst, ps[0:ncols, 0:nrows])

    def setup(self):
        P, d = self.P, self.d
        self.ident = self.ct(128)
        P.dma("sp", self.ident, d["ktab"][:, KT_IDENT:KT_IDENT + 128], "ident")
        self.identb = self.cb(128)
        self.copy(self.identb, self.ident)
        self.onesb = self.cb(128)
        self.memset(self.onesb, 1.0 / 1024.0)
        self.epsc = self.ct(1)
        self.memset(self.epsc, EPS)
        self.onec = self.ct(1)
        self.memset(self.onec, 1.0)
        self.NG = self.ct(128)
        for k, nm in enumerate(("norm_mix_pre", "norm_mix_post", "norm_mlp_pre", "norm_mlp_post")):
            self.load_rowsT(d[nm], 32, self.NG[:, k * 32:(k + 1) * 32])
        self.AB = self.ct(192)
        self.load_rowsT(d["ada_b"][0:96, :], 96, self.AB[:, 0:96])
        self.load_rowsT(d["ada_b"][96:192, :], 96, self.AB[:, 96:192])
        cT = self.ct(8)
        self.load_rowsT(d["c"], 8, cT)
        self.SCb = self.cb(8)
        self.act(self.SCb, cT, AF.Silu)
        self.MOD = self.ct(192)

    def ada_layer(self, l):
        d = self.d
        ps = self.pb[6]
        for blk in range(12):
            w = self.wload(d["ada_w"][l, :, blk * 512:(blk + 1) * 512].rearrange("(kc p) n -> p kc n", p=128), (128, 8, 512))
            for jj in range(4):
                j = blk * 4 + jj
                for kc in range(8):
                    self.mm(ps[:, j:j + 1], w[:, kc, jj * 128:(jj + 1) * 128], self.SCb[:, kc:kc + 1], kc == 0, kc == 7)
        self.tt(self.MOD[:, l * 48:(l + 1) * 48], ps[:, 0:48], self.AB[:, l * 48:(l + 1) * 48], ALU.add)
        m = lambda w_: self.MOD[:, l * 48 + w_ * 8: l * 48 + w_ * 8 + 8]
        g = lambda k: self.NG[:, k * 32 + l * 8: k * 32 + l * 8 + 8]
        V = {}
        for nm in ("A_m", "G_m", "A_f", "G_f"):
            V[nm] = self.ct(8)
        self.stt(V["A_m"], m(1), 1.0, g(0), ALU.add, ALU.mult)
        self.stt(V["G_m"], m(2), 1.0, g(1), ALU.add, ALU.mult)
        self.stt(V["A_f"], m(4), 1.0, g(2), ALU.add, ALU.mult)
        self.stt(V["G_f"], m(5), 1.0, g(3), ALU.add, ALU.mult)
        V["B_m"] = m(0)
        V["B_f"] = m(3)
        return V

    def load_x(self):
        P, d = self.P, self.d
        for i in range(16):
            st = self.scr((i % 2) * 4096, (128, 1024), F32)
            P.dma("sp", st, d["x"][i * 128:(i + 1) * 128, :], "xin%d" % (i % 2))
            for half in range(2):
                ps = self.pb[4 + half]
                for q in range(4):
                    kc = half * 4 + q
                    self.tr(ps[:, q * 128:(q + 1) * 128], st[:, kc * 128:(kc + 1) * 128], self.ident)
                dst = self.XT[:, half * 4:(half + 1) * 4, i * 128:(i + 1) * 128]
                src = ps[:, :].rearrange("p (a b) -> p a b", a=4)
                self.copy(dst, src, eng="act" if half else "dve")

    def store_x(self):
        P, d = self.P, self.d
        for i in range(16):
            st = self.scr((i % 2) * 4096, (128, 1024), F32)
            for half in range(2):
                ps = self.pb[4 + half]
                for q in range(4):
                    kc = half * 4 + q
                    self.tr(ps[:, q * 128:(q + 1) * 128], self.XT[:, kc, i * 128:(i + 1) * 128], self.ident)
                self.copy(st[:, half * 512:(half + 1) * 512], ps[:, :], eng="act" if half else "dve")
            P.dma("sp", d["out"][i * 128:(i + 1) * 128, :], st, "out%d" % (i % 2))

    def rstd512(self, srcs, rstd, sqoff, onesb=None, nchunks=8):
        ss = self.pb[7]
        onesb = self.onesb if onesb is None else onesb
        for kc, s in enumerate(srcs):
            sq = self.scr(sqoff + (kc % 2) * 1024, (128, 512), BF16)
            self.act(sq, s, AF.Square)
            self.mm(ss[:, :], onesb, sq, kc == 0, kc == len(srcs) - 1)
        self.act(rstd, ss[:, :], AF.Sqrt, bias=self.epsc, scale=1.0)
        self.recip(rstd, rstd)

    def prenorm(self, t0, T, A, B, hT, tmpoff):
        for sub in range(T // 512):
            a = t0 + sub * 512
            rstd = self.scr(tmpoff + 2048, (128, 512), F32)
            self.rstd512([self.XT[:, kc, a:a + 512] for kc in range(8)], rstd, tmpoff)
            for kc in range(8):
                tmp = self.scr(tmpoff + 4096 + (kc % 2) * 2048, (128, 512), F32)
                self.tt(tmp, self.XT[:, kc, a:a + 512], rstd, ALU.mult)
                self.act(hT[:, kc, sub * 512:(sub + 1) * 512], tmp, AF.Identity, bias=B[:, kc:kc + 1], scale=A[:, kc:kc + 1])

    def postnorm_add(self, t0, yT, G, tmpoff):
        rstd = self.scr(tmpoff + 2048, (128, 512), F32)
        self.rstd512([yT[:, kc, :] for kc in range(8)], rstd, tmpoff)
        for kc in range(8):
            tmp = self.scr(tmpoff + 4096 + (kc % 2) * 2048, (128, 512), F32)
            self.stt(tmp, yT[:, kc, :], G[:, kc:kc + 1], rstd, ALU.mult, ALU.mult)
            xs = self.XT[:, kc, t0:t0 + 512]
            self.tt(xs, xs, tmp, ALU.add)

    def mlp(self, l, V):
        d = self.d
        T = 1024
        HID = 0
        HT = 65536
        TMP = 65536 + 32768
        SQ = TMP + 4096
        hid = self.scr(HID, (128, 32, T), BF16)
        hT = self.scr(HT, (128, 8, T), BF16)
        yT = self.scr(HT, (128, 16, 512), F32)
        for tt_ in range(S // T):
            t0 = tt_ * T
            self.prenorm(t0, T, V["A_f"], V["B_f"], hT, TMP)
            for hb in range(8):
                w = self.wload(d["mlp_w1"][l, :, hb * 512:(hb + 1) * 512].rearrange("(kc p) n -> p kc n", p=128), (128, 8, 512))
                for j in range(4):
                    hc = hb * 4 + j
                    for sub in range(2):
                        ps = self.acc()
                        for kc in range(8):
                            self.mm(ps[:, :], w[:, kc, j * 128:(j + 1) * 128], hT[:, kc, sub * 512:(sub + 1) * 512], kc == 0, kc == 7)
                        sq = self.scr(SQ + (self.acc_i % 2) * 2048, (128, 512), F32)
                        self.act(sq, ps[:, :], AF.Square)
                        self.stt(hid[:, hc, sub * 512:(sub + 1) * 512], ps[:, :], 0.0, sq, ALU.is_gt, ALU.mult)
            for oc in range(8):
                w2 = self.wload(d["mlp_w2"][l, :, oc * 128:(oc + 1) * 128].rearrange("(hc p) n -> p hc n", p=128), (128, 32, 128))
                for sub in range(2):
                    ps = self.acc()
                    for hc in range(32):
                        self.mm(ps[:, :], w2[:, hc, :], hid[:, hc, sub * 512:(sub + 1) * 512], hc == 0, hc == 31)
                    self.copy(yT[:, sub * 8 + oc, :], ps[:, :], eng="act" if oc % 2 else "dve")
            for sub in range(2):
                self.postnorm_add(t0 + sub * 512, yT[:, sub * 8:(sub + 1) * 8, :], V["G_f"], TMP)

    def mixer_lru(self, l, V):
        d = self.d
        o = [0]

        def A(nb):
            r = o[0]
            o[0] += (nb + 63) // 64 * 64
            return r
        hT = self.scr(A(8192), (128, 8, 512), BF16)
        xrpre = self.scr(A(8 * 516 * 4), (128, 8, 516), F32)
        xro = A(16384)
        xr = self.scr(xro, (128, 8, 512), F32)
        yT = self.scr(xro, (128, 8, 512), F32)
        xrb = self.scr(A(8192), (128, 8, 512), BF16)
        ylru = self.scr(A(8192), (128, 8, 512), BF16)
        TMP = A(8192)
        tmp = [[self.scr(A(2048), (128, 512), F32) for _ in range(2)] for _ in range(6)]
        CW = self.ct(32)
        self.load_rowsT(d["lru_conv_w"], 32, CW)
        misc = self.ct(32)
        self.load_rowsT(d["lru_conv_b"], 8, misc[:, 0:8])
        self.load_rowsT(d["lru_b_gate_a"], 8, misc[:, 8:16])
        self.load_rowsT(d["lru_b_gate_x"], 8, misc[:, 16:24])
        self.load_rowsT(d["lru_lambda"], 8, misc[:, 24:32])
        cbias, bga, bgx, lam = misc[:, 0:8], misc[:, 8:16], misc[:, 16:24], misc[:, 24:32]
        cL = self.ct(8)
        cL2 = self.ct(8)
        state = self.ct(8)
        self.act(cL, lam, AF.Exp, scale=-1.0)
        self.act(cL, cL, AF.Ln, bias=self.onec, scale=1.0)
        self.ts(cL2, cL, -16.0, None, ALU.mult)
        self.ts(cL, cL, -8.0, None, ALU.mult)
        self.memset(state, 0.0)
        self.memset(xrpre[:, :, 0:3], 0.0)
        w_in = d["lru_w_in"][0]
        for tt_ in range(4):
            t0 = tt_ * 512
            self.prenorm(t0, 512, V["A_m"], V["B_m"], hT, TMP)
            for cb in range(8):
                if cb % 4 == 0:
                    w = self.wload(w_in[:, 1024 + (cb // 4) * 512: 1024 + (cb // 4 + 1) * 512].rearrange("(kc p) n -> p kc n", p=128), (128, 8, 512))
                ps = self.acc()
                for kc in range(8):
                    self.mm(ps[:, :], w[:, kc, (cb % 4) * 128:(cb % 4 + 1) * 128], hT[:, kc, :], kc == 0, kc == 7)
                self.copy(xrpre[:, cb, 3:515], ps[:, :], eng="act")
                self.act(xr[:, cb, :], xrpre[:, cb, 3:515], AF.Identity, bias=cbias[:, cb:cb + 1], scale=CW[:, 24 + cb:25 + cb])
                for k in range(3):
                    self.stt(xr[:, cb, :], xrpre[:, cb, k:k + 512], CW[:, k * 8 + cb:k * 8 + cb + 1], xr[:, cb, :], ALU.mult, ALU.add)
                self.copy(xrb[:, cb, :], xr[:, cb, :], eng="act")
                self.copy(xrpre[:, cb, 0:3], xrpre[:, cb, 512:515])
            Wa = self.wload(d["lru_w_gate_a"].rearrange("(q p) j -> p q j", p=128), (128, 8, 256))
            Wx = self.wload(d["lru_w_gate_x"].rearrange("(q p) j -> p q j", p=128), (128, 8, 256))
            for cb in range(8):
                blk, jc = cb // 2, cb % 2
                if cb % 4 == 0:
                    wg = self.wload(w_in[:, (cb // 4) * 512:(cb // 4 + 1) * 512].rearrange("(kc p) n -> p kc n", p=128), (128, 8, 512))
                pa = self.acc()
                for ic in range(2):
                    self.mm(pa[:, :], Wa[:, blk * 2 + ic, jc * 128:(jc + 1) * 128], xrb[:, blk * 2 + ic, :], ic == 0, ic == 1)
                px = self.acc()
                for ic in range(2):
                    self.mm(px[:, :], Wx[:, blk * 2 + ic, jc * 128:(jc + 1) * 128], xrb[:, blk * 2 + ic, :], ic == 0, ic == 1)
                pg = self.acc()
                for kc in range(8):
                    self.mm(pg[:, :], wg[:, kc, (cb % 4) * 128:(cb % 4 + 1) * 128], hT[:, kc, :], kc == 0, kc == 7)
                b = cb % 2
                tr_, ti, ta, te, ths, tg = (tmp[q][b] for q in range(6))
                self.act(tr_, pa[:, :], AF.Sigmoid, bias=bga[:, cb:cb + 1], scale=1.0)
                self.act(ti, px[:, :], AF.Sigmoid, bias=bgx[:, cb:cb + 1], scale=1.0)
                self.act(ta, tr_, AF.Exp, scale=cL[:, cb:cb + 1])
                self.act(te, tr_, AF.Exp, scale=cL2[:, cb:cb + 1])
                self.act(te, te, AF.Sqrt, bias=self.onec, scale=-1.0)
                self.tt(ti, ti, xr[:, cb, :], ALU.mult)
                self.tt(ti, ti, te, ALU.mult)
                self.scan(ths, ta, ti, state[:, cb:cb + 1])
                self.copy(state[:, cb:cb + 1], ths[:, 511:512])
                self.act(tg, pg[:, :], AF.Gelu_apprx_tanh)
                self.tt(ylru[:, cb, :], ths, tg, ALU.mult)
            for ob in range(2):
                wo = self.wload(d["lru_w_out"][0][:, ob * 512:(ob + 1) * 512].rearrange("(kc p) n -> p kc n", p=128), (128, 8, 512))
                for j in range(4):
                    oc = ob * 4 + j
                    ps = self.acc()
                    for kc in range(8):
                        self.mm(ps[:, :], wo[:, kc, j * 128:(j + 1) * 128], ylru[:, kc, :], kc == 0, kc == 7)
                    self.copy(yT[:, oc, :], ps[:, :], eng="act" if oc % 2 else "dve")
            self.postnorm_add(t0, yT, V["G_m"], TMP)

    def act_acc(self, out, in_, func, accum_out):
        nc = self.nc
        self.P.op("act", lambda: nc.scalar.activation(out=out, in_=in_, func=func, accum_out=accum_out), [in_], [out, accum_out])

    def red(self, out, in_, op=ALU.add):
        nc = self.nc
        self.P.op("dve", lambda: nc.vector.tensor_reduce(out=out, in_=in_, axis=mybir.AxisListType.X, op=op), [in_], [out])

    def ttr(self, out, in0, in1, accum_out):
        nc = self.nc
        self.P.op("dve", lambda: nc.vector.tensor_tensor_reduce(out=out, in0=in0, in1=in1, scale=1.0, scalar=0.0, op0=ALU.mult, op1=ALU.add, accum_out=accum_out), [in0, in1], [out, accum_out])

    def mixer_ssd(self, l, V):
        d, P = self.d, self.P
        occ = l // 3
        o = [0]

        def A(nb):
            r = o[0]
            o[0] += (nb + 63) // 64 * 64
            return r
        bc = lambda ap, shape, axis: ap.unsqueeze(axis).broadcast_to(list(shape))
        ng_bc = self.scr(A(8192), (128, 2048), F32)
        triLE = self.scr(A(512), (128, 128), F32)
        onesF = self.scr(A(512), (128, 128), F32)
        SLb = self.scr(A(1024), (128, 4, 128), BF16)
        negI = self.scr(A(256), (128, 128), BF16)
        triB = self.scr(A(256), (128, 128), BF16)
        onesB = self.scr(A(256), (128, 128), BF16)
        dhl = self.scr(A(512), (128, 2, 4, 32), BF16) if False else self.scr(A(512), (128, 8, 32), BF16)
        Dh_bc = self.scr(A(128), (128, 32), F32)
        convw = self.scr(A(512), (128, 128), F32)
        convb = self.scr(A(128), (128, 32), F32)
        hcol = self.scr(A(64), (128, 4), F32)
        stF = self.scr(A(8192), (128, 8, 256), F32)
        stB = self.scr(A(4096), (128, 8, 256), BF16)
        HAL = self.scr(A(384), (128, 32, 3), F32)
        hT = self.scr(A(8192), (128, 8, 512), BF16)
        dtT = self.scr(A(2048), (128, 512), F32)
        dtAT = self.scr(A(2048), (128, 512), F32)
        dt_tok, dtA_tok, acs_tok, nacs, eacs, wdec, cdec = (self.scr(A(512), (128, 4, 32), F32) for _ in range(7))
        xc = self.scr(A(4096), (128, 4, 512), BF16)
        z_tok = self.scr(A(2048), (128, 4, 256), BF16)
        x_tok = self.scr(A(2048), (128, 4, 256), BF16)
        xdt = self.scr(A(2048), (128, 4, 256), BF16)
        xdec = self.scr(A(2048), (128, 4, 256), BF16)
        B_tok = self.scr(A(1024), (128, 4, 128), BF16)
        ynT = self.scr(A(16384), (128, 16, 512), BF16)
        U = A(16640)
        yT = self.scr(U, (128, 8, 512), F32)
        pre = [self.scr(U + i * 2064, (128, 516), F32) for i in range(2)]
        cacc = [self.scr(U + 4128 + i * 2048, (128, 512), F32) for i in range(2)]
        R1 = [self.scr(U + 8224 + i * 2048, (128, 4, 128), F32) for i in range(2)]
        dec = [self.scr(U + 12320 + i * 2048, (128, 4, 128), F32) for i in range(2)]
        Mb = [self.scr(A(1024), (128, 4, 128), BF16) for _ in range(2)]
        t1 = [self.scr(A(1024), (128, 256), F32) for _ in range(2)]
        yg = [self.scr(A(1024), (128, 256), F32) for _ in range(2)]
        ynb = [self.scr(A(512), (128, 256), BF16) for _ in range(2)]
        ssq = self.scr(A(64), (128, 4), F32)
        TMP = A(8192)
        P.dma("sp", ng_bc, d["ssd_norm"].rearrange("a b -> (a b)")[occ * 2048:(occ + 1) * 2048].partition_broadcast(128), "sc0")
        P.dma("sp", triLE, d["ktab"][:, KT_TRILE:KT_TRILE + 128], "sc1")
        P.dma("sp", Dh_bc, d["ssd_d"][occ].partition_broadcast(128), "sc2")
        sl = self.scr(TMP, (128, 128), F32)
        P.dma("sp", sl, d["ktab"][:, KT_SL:KT_SL + 128], "sc3")
        self.copy(SLb, bc(sl, (128, 4, 128), 1))
        self.ts(negI, self.ident, -30000.0, None, ALU.mult)
        self.memset(onesF, 1.0)
        self.memset(onesB, 1.0)
        self.copy(triB, triLE)
        for k in range(4):
            self.load_rowsT(d["ssd_conv_w"][occ * 128 + k * 32: occ * 128 + (k + 1) * 32, :], 32, convw[:, k * 32:(k + 1) * 32], stoff=TMP + 1024)
        self.load_rowsT(d["ssd_conv_b"][occ * 32:(occ + 1) * 32, :], 32, convb, stoff=TMP + 1024)
        dtb_bc = self.scr(A(128), (128, 32), F32)
        a_bc = self.scr(A(128), (128, 32), F32)
        P.dma("sp", dtb_bc, d["ssd_dt_bias"][occ].partition_broadcast(128), "sc4")
        P.dma("sp", a_bc, d["ssd_a_log"][occ].partition_broadcast(128), "sc5")
        self.act(a_bc, a_bc, AF.Exp)
        self.ts(a_bc, a_bc, -1.0, None, ALU.mult)
        self.memset(stF, 0.0)
        self.memset(stB, 0.0)
        self.memset(HAL, 0.0)
        w_in = d["ssd_w_in"][occ]
        wv = lambda c0, n: w_in[:, c0:c0 + n].rearrange("(kc p) n -> p kc n", p=128)
        it = 0
        import os
        STG = int(os.environ.get("SSD_STAGE", "9"))
        for tt_ in range(4):
            t0 = tt_ * 512
            self.prenorm(t0, 512, V["A_m"], V["B_m"], hT, TMP)
            if STG < 2:
                continue
            SUB = int(os.environ.get("SSD_SUB", "99"))
            wdt = self.wload(wv(5664, 512), (128, 8, 512))
            pd = self.pb[4]
            pc = self.pb[5]
            dhi, dlo = dhl[:, 0:4, :], dhl[:, 4:8, :]
            f2 = lambda ap: ap.rearrange("p a b -> p (a b)")
            v3 = lambda ap: ap.rearrange("p (a b) -> p a b", a=4)

            def mmdt():
                for c in range(4):
                    for kc in range(8):
                        self.mm(pd[:, c * 32:(c + 1) * 32], hT[:, kc, c * 128:(c + 1) * 128], wdt[:, kc, 480:512], kc == 0, kc == 7)
            steps = [
                mmdt,
                lambda: self.tt(dt_tok, v3(pd[:, 0:128]), bc(dtb_bc, (128, 4, 32), 1), ALU.add),
                lambda: self.act(dt_tok, dt_tok, AF.Exp),
                lambda: self.act(dt_tok, dt_tok, AF.Ln, bias=self.onec, scale=1.0),
                lambda: self.tt(dtA_tok, dt_tok, bc(a_bc, (128, 4, 32), 1), ALU.mult),
                lambda: self.copy(dhi, dtA_tok),
                lambda: self.tt(dlo, dtA_tok, dhi, ALU.subtract),
                lambda: self.mm(pc[:, 0:128], triB, f2(dhi), True, False),
                lambda: self.mm(pc[:, 0:128], triB, f2(dlo), False, True),
                lambda: self.mm(pc[:, 128:256], onesB, f2(dhi), True, False),
                lambda: self.mm(pc[:, 128:256], onesB, f2(dlo), False, True),
                lambda: self.copy(acs_tok, v3(pc[:, 0:128])),
                lambda: self.ts(nacs, acs_tok, -1.0, None, ALU.mult),
                lambda: self.act(eacs, acs_tok, AF.Exp),
                lambda: self.copy(wdec, v3(pc[:, 128:256])),
                lambda: self.act(cdec, wdec, AF.Exp),
                lambda: self.tt(wdec, wdec, acs_tok, ALU.subtract),
                lambda: self.act(wdec, wdec, AF.Exp),
                lambda: self.tt(wdec, wdec, dt_tok, ALU.mult),
            ]
            for st_ in steps[:SUB]:
                st_()
            for g in range(8):
                if STG < 3:
                    continue
                s_, slotA = self.slot()
                wA = slotA[:, 0:4096].rearrange("p (a b) -> p a b", a=8)
                P.dma("pool", wA[:, :, 0:256], wv(2048 + g * 256, 256), "ring%d" % s_)
                P.dma("pool", wA[:, :, 256:384], wv(4096 + g * 128, 128), "ring%d" % s_)
                P.dma("pool", wA[:, :, 384:512], wv(5120 + g * 128, 128), "ring%d" % s_)
                wz = self.wload(wv(g * 256, 256), (128, 8, 256))
                qidx = [2 * g, 2 * g + 1, 16 + g, 24 + g]
                for q in range(4):
                    qq = qidx[q]
                    ps = self.acc()
                    for kc in range(8):
                        self.mm(ps[:, :], wA[:, kc, q * 128:(q + 1) * 128], hT[:, kc, :], kc == 0, kc == 7)
                    pr = pre[q % 2]
                    ca = cacc[q % 2]
                    self.copy(pr[:, 0:3], HAL[:, qq, :])
                    self.copy(pr[:, 3:515], ps[:, :], eng="act")
                    self.copy(HAL[:, qq, :], pr[:, 512:515])
                    self.act(ca, pr[:, 3:515], AF.Identity, bias=convb[:, qq:qq + 1], scale=convw[:, 96 + qq:97 + qq])
                    for k in range(3):
                        self.stt(ca, pr[:, k:k + 512], convw[:, k * 32 + qq:k * 32 + qq + 1], ca, ALU.mult, ALU.add)
                    self.act(xc[:, q, :], ca, AF.Silu)
                for c in range(4):
                    ps = self.acc()
                    for kc in range(8):
                        self.mm(ps[:, 0:256], hT[:, kc, c * 128:(c + 1) * 128], wz[:, kc, :], kc == 0, kc == 7)
                    self.act(z_tok[:, c, :], ps[:, 0:256], AF.Silu)
                for c in range(4):
                    if STG < 4:
                        continue
                    pt = self.pb[4 + c % 2]
                    ptb = pt[:, 0:256].bitcast(BF16)
                    for j in range(3):
                        self.tr(ptb[:, j * 128:(j + 1) * 128], xc[:, j, c * 128:(c + 1) * 128], self.identb)
                    self.copy(x_tok[:, c, :], ptb[:, 0:256], eng="act")
                    self.copy(B_tok[:, c, :], ptb[:, 256:384], eng="act")
                    xv = x_tok[:, c, :].rearrange("p (a b) -> p a b", a=4)
                    self.tt(xdt[:, c, :].rearrange("p (a b) -> p a b", a=4), xv, bc(dt_tok[:, c, g * 4:(g + 1) * 4], (128, 4, 64), 2), ALU.mult)
                    self.tt(xdec[:, c, :].rearrange("p (a b) -> p a b", a=4), xv, bc(wdec[:, c, g * 4:(g + 1) * 4], (128, 4, 64), 2), ALU.mult)
                v4 = lambda ap: ap.rearrange("p (a b) -> p a b", a=4)

                def front(c, g=g):
                    b = c % 2
                    cs = slice(c * 128, (c + 1) * 128)
                    pcb = self.pb[6 + c % 2]
                    self.mm(pcb[:, 0:128], xc[:, 2, cs], xc[:, 3, cs], True, True)
                    r1h = R1[b].rearrange("p a b -> p (a b)").bitcast(BF16).rearrange("p (h a b) -> p h a b", h=2, a=4)
                    self.tt(r1h[:, 0], bc(triB, (128, 4, 128), 1), bc(dhi[:, c, g * 4:(g + 1) * 4], (128, 4, 128), 2), ALU.mult)
                    self.tt(r1h[:, 1], bc(triB, (128, 4, 128), 1), bc(dlo[:, c, g * 4:(g + 1) * 4], (128, 4, 128), 2), ALU.mult)
                    psg = self.acc()
                    self.mm(psg[:, :], onesB, r1h[:, 0].rearrange("p a b -> p (a b)"), True, False)
                    self.mm(psg[:, :], onesB, r1h[:, 1].rearrange("p a b -> p (a b)"), False, False)
                    self.mm(psg[:, :], negI, SLb.rearrange("p a b -> p (a b)"), False, True)
                    for e in range(4):
                        self.act(dec[b][:, e, :], psg[:, e * 128:(e + 1) * 128], AF.Exp, bias=nacs[:, c, g * 4 + e:g * 4 + e + 1], scale=1.0)

                def back(c, g=g):
                    b = c % 2
                    cs = slice(c * 128, (c + 1) * 128)
                    pcb = self.pb[6 + c % 2]
                    self.tt(Mb[b], dec[b], bc(pcb[:, 0:128], (128, 4, 128), 1), ALU.mult)
                    py = self.acc()
                    for e in range(4):
                        self.mm(py[:, e * 64:(e + 1) * 64], Mb[b][:, e, :], xdt[:, c, e * 64:(e + 1) * 64], True, True)
                    self.mm(py[:, 256:512], xc[:, 3, cs], stB[:, g, :], True, True)
                    pst = self.acc()
                    self.mm(pst[:, 0:256], B_tok[:, c, :], xdec[:, c, :], True, True)
                    self.tt(v4(stF[:, g, :]), v4(stF[:, g, :]), bc(cdec[:, c, g * 4:(g + 1) * 4], (128, 4, 64), 2), ALU.mult)
                    self.tt(stF[:, g, :], stF[:, g, :], pst[:, 0:256], ALU.add)
                    self.copy(stB[:, g, :], stF[:, g, :], eng="act")
                    self.tt(v4(t1[b]), v4(py[:, 256:512]), bc(eacs[:, c, g * 4:(g + 1) * 4], (128, 4, 64), 2), ALU.mult)
                    self.tt(t1[b], t1[b], py[:, 0:256], ALU.add)
                    self.tt(v4(yg[b]), v4(x_tok[:, c, :]), bc(Dh_bc[:, g * 4:(g + 1) * 4], (128, 4, 64), 2), ALU.mult)
                    self.tt(t1[b], t1[b], yg[b], ALU.add)
                    self.tt(yg[b], t1[b], z_tok[:, c, :], ALU.mult)
                    self.tt(t1[b], yg[b], yg[b], ALU.mult)
                    self.red(ssq[:, b:b + 1], t1[b])
                    self.act(ssq[:, 2 + b:3 + b], ssq[:, b:b + 1], AF.Sqrt, bias=self.epsc, scale=1.0 / 256.0)
                    self.recip(ssq[:, 2 + b:3 + b], ssq[:, 2 + b:3 + b])
                    self.stt(ynb[b], yg[b], ssq[:, 2 + b:3 + b], ng_bc[:, g * 256:(g + 1) * 256], ALU.mult, ALU.mult)
                    pt = self.pb[4 + c % 2]
                    ptb = pt[:, 0:256].bitcast(BF16)
                    for j in range(2):
                        self.tr(ptb[:, j * 128:(j + 1) * 128], ynb[b][:, j * 128:(j + 1) * 128], self.identb)
                    self.copy(ynT[:, 2 * g:2 * g + 2, cs], ptb[:, 0:256].rearrange("p (a b) -> p a b", a=2), eng="act")

                if STG >= 5:
                    front(0)
                    for c in range(4):
                        if c < 3:
                            front(c + 1)
                        back(c)
            if STG < 6:
                continue
            w_out = d["ssd_w_out"][occ]
            for ob in range(2):
                wo = [self.wload(w_out[hf * 1024:(hf + 1) * 1024, ob * 512:(ob + 1) * 512].rearrange("(kc p) n -> p kc n", p=128), (128, 8, 512)) for hf in range(2)]
                for j in range(4):
                    oc = ob * 4 + j
                    ps = self.acc()
                    for q in range(16):
                        self.mm(ps[:, :], wo[q // 8][:, q % 8, j * 128:(j + 1) * 128], ynT[:, q, :], q == 0, q == 15)
                    self.copy(yT[:, oc, :], ps[:, :], eng="act" if oc % 2 else "dve")
            self.postnorm_add(t0, yT, V["G_m"], TMP)

    def mixer_attn(self, l, V):
        d, P = self.d, self.P
        o = [0]

        def A(nb):
            r = o[0]
            o[0] += (nb + 63) // 64 * 64
            return r
        bc = lambda ap, shape, axis: ap.unsqueeze(axis).broadcast_to(list(shape))
        HT0 = A(32768)
        hT = self.scr(HT0, (128, 8, 2048), BF16)
        yT = self.scr(HT0 + 16384, (128, 8, 512), F32)
        masks = self.scr(A(ATT_NMASK * 256), (128, ATT_NMASK, 128), BF16)
        cosT = self.scr(A(4096), (128, 2048), BF16)
        sinT = self.scr(A(4096), (128, 2048), BF16)
        Pm = self.scr(A(128), (128, 64), BF16)
        QT0 = o[0]
        oT = self.scr(QT0, (128, 4, 2048), BF16)
        qT = self.scr(A(4096), (128, 2048), BF16)
        kT = self.scr(A(4096), (128, 2048), BF16)
        vv = self.scr(A(2048), (128, 16, 64), BF16)
        qraw = [self.scr(A(1024), (128, 512), BF16) for _ in range(2)]
        T1 = A(4096)
        t1 = [self.scr(T1 + i * 2048, (128, 512), F32) for i in range(2)]
        S0 = A(8192)
        sbuf = [self.scr(S0 + i * 4096, (128, 1024), F32) for i in range(2)]
        TMP = S0
        pbuf = [self.scr(A(2048), (128, 1024), BF16) for _ in range(2)]
        pT = [self.scr(A(1024), (128, 4, 128), BF16) for _ in range(2)]
        OST0 = A(8192)
        Ost = self.scr(OST0, (128, 16, 4, 64), BF16)
        stat = self.scr(A(512), (128, 16, 8), F32)
        OTK0 = A(16384)
        o_tok = self.scr(OTK0, (128, 16, 512), BF16)
        sm = self.scr(A(1024), (128, 256), F32)
        oacc = self.scr(A(1024), (128, 4, 64), F32)
        assert o[0] <= self.SCRB, o[0]
        att = self.ct(128)
        P.dma("sp", att, d["ktab"][:, KT_ATT:KT_ATT + 128], "att0")
        self.copy(Pm[0:64, :], att[0:64, 0:64])
        invf, sgn = att[0:64, 64:65], att[0:64, 65:66]
        P.dma("pool", masks, d["amask"].rearrange("p (a b) -> p a b", a=ATT_NMASK), "att1", max_dma_last_dim=2048)
        posi = self.scr(OTK0, (128, 2048), I32)
        posf = self.scr(OTK0 + 8192, (128, 2048), F32)
        r1 = self.scr(OST0, (128, 2048), F32)
        ki = self.scr(OTK0, (128, 2048), I32)
        R = slice(0, 64)
        P.dma("sp", posi[R, :], d["positions"][0].partition_broadcast(64), "att2")
        self.copy(posf[R, :], posi[R, :])
        self.ts(r1[R, :], posf[R, :], invf, None, ALU.mult)
        self.copy(ki[R, :], r1[R, :])
        self.copy(posf[R, :], ki[R, :])
        self.tt(r1[R, :], r1[R, :], posf[R, :], ALU.subtract)
        tmpm = self.scr(OTK0, (128, 2048), F32)
        for which, dst in ((0, sinT), (1, cosT)):
            f = posf
            if which == 0:
                self.copy(f[R, :], r1[R, :])
            else:
                self.ts(f[R, :], r1[R, :], 0.25, None, ALU.add)
            self.ts(tmpm[R, :], f[R, :], 0.5, None, ALU.is_gt)
            self.tt(f[R, :], f[R, :], tmpm[R, :], ALU.subtract)
            self.ts(tmpm[R, :], f[R, :], -0.5, None, ALU.is_lt)
            self.tt(f[R, :], f[R, :], tmpm[R, :], ALU.add)
            self.ts(f[R, :], f[R, :], 0.49999997, -0.49999997, ALU.min, ALU.max)
            if which == 0:
                self.act(tmpm[R, :], f[R, :], AF.Sin, scale=2.0 * math.pi)
                self.ts(dst[R, :], tmpm[R, :], sgn, None, ALU.mult)
            else:
                self.act(dst[R, :], f[R, :], AF.Sin, scale=2.0 * math.pi)
        self.prenorm(0, 2048, V["A_m"], V["B_m"], hT, TMP)
        self.memset(Ost, 0.0)
        import os
        ASTG = int(os.environ.get("ATT_STAGE", "9"))
        if ASTG < 2:
            return
        w_qkv = d["attn_w_qkv"][0]
        wv_ = lambda c0, n: w_qkv[:, c0:c0 + n].rearrange("(kc p) n -> p kc n", p=128)
        pbTb = [self.pb[4][:, :].bitcast(BF16), self.pb[5][:, :].bitcast(BF16)]
        it = 0
        tcount = 0
        pi_ = 0
        for h in range(8):
            self.memset(stat[:, :, 0:8:2], -30000.0)
            self.memset(stat[:, :, 1:8:2], 0.0)
            for g in range(3):
                s_, slt = self.slot()
                w = slt[:, 0:1536].rearrange("p (a b) -> p a b", a=8)
                for j in range(3):
                    P.dma("pool", w[:, :, j * 64:(j + 1) * 64], wv_(g * 1536 + j * 512 + h * 64, 64), "ring%d" % s_)
                for j, dst in ((0, qT), (1, kT)):
                    for tq in range(4):
                        ts_ = slice(tq * 512, (tq + 1) * 512)
                        ps = self.pb[6 + pi_ % 2]
                        sw = self.pb[4 + pi_ % 2]
                        qr = qraw[pi_ % 2]
                        ta = t1[pi_ % 2]
                        pi_ += 1
                        for kc in range(8):
                            self.mm(ps[R, :], w[:, kc, j * 64:(j + 1) * 64], hT[:, kc, ts_], kc == 0, kc == 7)
                        self.copy(qr[R, :], ps[R, :], eng="act")
                        self.mm(sw[R, :], Pm[R, :], qr[R, :], True, True)
                        self.tt(ta[R, :], qr[R, :], cosT[R, ts_], ALU.mult)
                        self.tt(dst[R, ts_], sw[R, :], sinT[R, ts_], ALU.mult)
                        self.tt(dst[R, ts_], dst[R, ts_], ta[R, :], ALU.add)
                for half in range(2):
                    ps = self.pb[6 + half]
                    for k8 in range(8):
                        kt = half * 8 + k8
                        for kc in range(8):
                            self.mm(ps[:, k8 * 64:(k8 + 1) * 64], hT[:, kc, kt * 128:(kt + 1) * 128], w[:, kc, 128:192], kc == 0, kc == 7)
                    self.copy(vv[:, half * 8:(half + 1) * 8, :], ps[:, :].rearrange("p (a b) -> p a b", a=8), eng="act")
                plist = []
                for qt in range(16):
                    if g == 0:
                        plist.append((qt, 0, list(range(0, min(1, qt) + 1))))
                    elif g == 1:
                        plist.append((qt, 1, list(range(0, min(4, qt) + 1))))
                    else:
                        plist.append((qt, 2, list(range(0, min(7, qt) + 1))))
                        if qt >= 8:
                            plist.append((qt, 3, list(range(8, qt + 1))))

                def pfront(i, g=g, plist=plist):
                    qt, pidx, deltas = plist[i]
                    n = len(deltas)
                    b = i % 2
                    Rg = (self.pb[2 * b], self.pb[2 * b + 1])
                    for j, dl in enumerate(deltas):
                        kt = qt - dl
                        bank, col = Rg[j // 4], (j % 4) * 128
                        self.mm(bank[:, col:col + 128], qT[R, qt * 128:(qt + 1) * 128], kT[R, kt * 128:(kt + 1) * 128], True, True)
                    sb_, pb_ = sbuf[b], pbuf[b]
                    mi = ATT_MBASE[g] + deltas[0]
                    for bk in range((n + 3) // 4):
                        cnt = min(4, n - bk * 4)
                        mk = masks[:, mi + bk * 4: mi + bk * 4 + cnt, :].rearrange("p a b -> p (a b)")
                        self.stt(sb_[:, bk * 512: bk * 512 + cnt * 128], Rg[bk][:, 0:cnt * 128], 0.125, mk, ALU.mult, ALU.add)
                    mcol = stat[:, qt, 2 * pidx:2 * pidx + 1]
                    lcol = stat[:, qt, 2 * pidx + 1:2 * pidx + 2]
                    nm = sm[:, 8 + b:9 + b]
                    self.red(mcol, sb_[:, 0:n * 128], ALU.max)
                    self.ts(nm, mcol, -1.0, None, ALU.mult)

                def pfront_b(i, plist=plist):
                    qt, pidx, deltas = plist[i]
                    n = len(deltas)
                    b = i % 2
                    self.act(pbuf[b][:, 0:n * 128], sbuf[b][:, 0:n * 128], AF.Exp, bias=sm[:, 8 + b:9 + b], scale=1.0)

                def pback(i, plist=plist):
                    nonlocal tcount
                    qt, pidx, deltas = plist[i]
                    n = len(deltas)
                    b = i % 2
                    pb_ = pbuf[b]
                    self.red(stat[:, qt, 2 * pidx + 1:2 * pidx + 2], pb_[:, 0:n * 128], ALU.add)
                    Ops = self.pb[6 + b][:, 0:64]
                    for b4 in range((n + 3) // 4):
                        cnt = min(4, n - b4 * 4)
                        hf = tcount % 2
                        tcount += 1
                        tb = pbTb[hf][:, 0: cnt * 128]
                        for jj in range(cnt):
                            j = b4 * 4 + jj
                            self.tr(tb[:, jj * 128:(jj + 1) * 128], pb_[:, j * 128:(j + 1) * 128], self.identb)
                        self.copy(pT[hf][:, 0:cnt, :], tb.rearrange("p (a b) -> p a b", a=cnt), eng="act")
                        for jj in range(cnt):
                            j = b4 * 4 + jj
                            kt = qt - deltas[j]
                            self.mm(Ops, pT[hf][:, jj, :], vv[:, kt, :], j == 0, j == n - 1)
                    self.copy(Ost[:, qt, pidx, :], Ops, eng="act")

                if ASTG >= 3:
                    pfront(0)
                    pfront_b(0)
                    for i in range(len(plist)):
                        if i + 1 < len(plist):
                            pfront(i + 1)
                        pback(i)
                        if i + 1 < len(plist):
                            pfront_b(i + 1)
            if ASTG < 4:
                continue
            mview, lview = stat[:, :, 0:8:2], stat[:, :, 1:8:2]
            M = sm[:, 16:32]
            e = sm[:, 32:96].rearrange("p (a b) -> p a b", a=16)
            L = sm[:, 96:112]
            el = sm[:, 112:176].rearrange("p (a b) -> p a b", a=16)
            self.red(M, mview, ALU.max)
            self.tt(e, mview, bc(M, (128, 16, 4), 2), ALU.subtract)
            self.act(e, e, AF.Exp)
            self.tt(el, e, lview, ALU.mult)
            self.red(L, el, ALU.add)
            self.recip(L, L)
            self.tt(e, e, bc(L, (128, 16, 4), 2), ALU.mult)
            for q4 in range(4):
                tO = self.scr(T1, (128, 4, 4, 64), F32)
                self.tt(tO, Ost[:, q4 * 4:(q4 + 1) * 4, :, :], bc(e[:, q4 * 4:(q4 + 1) * 4, :], (128, 4, 4, 64), 3), ALU.mult)
                self.red(oacc, tO.rearrange("p q i d -> p q d i"), ALU.add)
                self.copy(o_tok[:, q4 * 4:(q4 + 1) * 4, h * 64:(h + 1) * 64], oacc, eng="act")
        if ASTG < 5:
            return
        for qt in range(16):
            hf = qt % 2
            tb = pbTb[hf][:, 0:512]
            for c in range(4):
                self.tr(tb[:, c * 128:(c + 1) * 128], o_tok[:, qt, c * 128:(c + 1) * 128], self.identb)
            ATX = int(os.environ.get("ATT_X", "0"))
            if ATX == 1:
                self.copy(oT[:, :, qt * 128:(qt + 1) * 128], tb.rearrange("p (a b) -> p a b", a=4), eng="dve")
            elif ATX == 0:
                self.copy(oT[:, :, qt * 128:(qt + 1) * 128], tb.rearrange("p (a b) -> p a b", a=4), eng="act")
        ASUB = int(os.environ.get("ATT_SUB", "9"))
        if ASUB < 1:
            return
        wo = [self.wload(d["attn_w_out"][0][:, hf * 512:(hf + 1) * 512].rearrange("(c p) n -> p c n", p=128), (128, 4, 512)) for hf in range(2)]
        for tq in range(4):
            for oc in range(8):
                ps = self.acc()
                for c in range(4):
                    self.mm(ps[:, :], wo[oc // 4][:, c, (oc % 4) * 128:(oc % 4 + 1) * 128], oT[:, c, tq * 512:(tq + 1) * 512], c == 0, c == 3)
                self.copy(yT[:, oc, :], ps[:, :], eng="act" if oc % 2 else "dve")
            if ASUB >= 2:
                self.postnorm_add(tq * 512, yT, V["G_m"], TMP)

    def build(self):
        self.setup()
        self.load_x()
        for l in self.layers:
            V = self.ada_layer(l)
            kind = l % 3
            if kind == 0:
                self.mixer_ssd(l, V)
            elif kind == 1:
                self.mixer_lru(l, V)
            else:
                self.mixer_attn(l, V)
            self.mlp(l, V)
        self.store_x()
        self.stats = self.P.emit(final_dkeys=["out0", "out1"])


_NC_CACHE = {}


def _prep_common(inputs):
    f = lambda a: np.ascontiguousarray(np.asarray(a, dtype=np.float32))
    com = {}
    com["ktab"] = make_ktab()
    com["amask"] = make_amask()
    com["ada_w"] = f(inputs["ada_w"])
    com["ada_b"] = f(inputs["ada_b"]).reshape(192, 128)
    for nm in ("norm_mix_pre", "norm_mix_post", "norm_mlp_pre", "norm_mlp_post"):
        com[nm] = f(inputs[nm]).reshape(32, 128)
    com["mlp_w1"] = f(inputs["mlp_w1"])
    com["mlp_w2"] = f(inputs["mlp_w2"])
    com["ssd_w_in"] = f(inputs["ssd_w_in"])
    com["ssd_conv_w"] = f(inputs["ssd_conv_w"]).reshape(256, 128)
    com["ssd_conv_b"] = f(inputs["ssd_conv_b"]).reshape(64, 128)
    com["ssd_dt_bias"] = f(inputs["ssd_dt_bias"])
    com["ssd_a_log"] = f(inputs["ssd_a_log"])
    com["ssd_d"] = f(inputs["ssd_d"])
    com["ssd_norm"] = f(inputs["ssd_norm"]).reshape(32, 128)
    com["ssd_w_out"] = f(inputs["ssd_w_out"])
    com["lru_w_in"] = f(inputs["lru_w_in"])
    com["lru_conv_w"] = f(inputs["lru_conv_w"]).reshape(32, 128)
    com["lru_conv_b"] = f(inputs["lru_conv_b"]).reshape(8, 128)
    com["lru_w_gate_a"] = f(inputs["lru_w_gate_a"]).reshape(1024, 256)
    com["lru_b_gate_a"] = f(inputs["lru_b_gate_a"]).reshape(8, 128)
    com["lru_w_gate_x"] = f(inputs["lru_w_gate_x"]).reshape(1024, 256)
    com["lru_b_gate_x"] = f(inputs["lru_b_gate_x"]).reshape(8, 128)
    com["lru_lambda"] = f(inputs["lru_lambda"]).reshape(8, 128)
    com["lru_w_out"] = f(inputs["lru_w_out"])
    com["attn_w_qkv"] = f(inputs["attn_w_qkv"])
    com["attn_w_out"] = f(inputs["attn_w_out"])
    return com


def run_layers(inputs, x, layers, cores=8, trace=False):
    key = tuple(layers)
    if key not in _NC_CACHE:
        _NC_CACHE[key] = K(list(layers))
    kb = _NC_CACHE[key]
    com = _prep_common(inputs)
    c = np.asarray(inputs["c"], np.float32)
    pos = np.asarray(inputs["positions"], np.int32)
    in_maps = []
    for b in range(cores):
        m = dict(com)
        m["x"] = np.ascontiguousarray(x[b])
        m["c"] = np.ascontiguousarray(c[b].reshape(8, 128))
        m["positions"] = np.ascontiguousarray(pos[b].reshape(1, S))
        in_maps.append({k: v for k, v in m.items() if k in kb.d})
    res = run_bass_kernel_spmd(kb.nc, in_maps, core_ids=list(range(cores)), trace=trace)
    out = np.stack([r["out"] for r in res.results], axis=0)
    return out, res


def kernel(**inputs):
    x = np.asarray(inputs["x"], np.float32)
    out, _ = run_layers(inputs, x, (0, 1, 2, 3))
    return out.astype(np.float32)
```

```python
import math
import numpy as np
import concourse.bass as bass
import concourse.mybir as mybir
from concourse.bass_utils import run_bass_kernel_spmd

F32 = mybir.dt.float32
BF16 = mybir.dt.bfloat16
I32 = mybir.dt.int32
AF = mybir.ActivationFunctionType
ALU = mybir.AluOpType

D = 1024
S = 2048
DEPTH = 4
EPS = 1e-6
NCH = 8


def _box(ap):
    t = ap.tensor
    if type(t).__name__ == "DRamTensorHandle":
        return None
    if type(t).__name__ == "PSumTensorHandle":
        return (t.name, 0, 128, 0, 2048)
    es = mybir.dt.size(ap.dtype)
    dims = list(ap.ap)
    pstep, pcnt = dims[0]
    off = ap.offset
    if pstep == 0:
        p0, f0 = 0, off
    else:
        p0 = off // pstep
        f0 = off - p0 * pstep
    f1 = f0 + sum((c - 1) * abs(s) for s, c in dims[1:]) + 1
    return (t.name, p0, p0 + pcnt, f0 * es, f1 * es)


def _overlap(a, b):
    return a[1] < b[2] and b[1] < a[2] and a[3] < b[4] and b[3] < a[4]


def _covers(a, b):
    return a[1] <= b[1] and a[2] >= b[2] and a[3] <= b[3] and a[4] >= b[4]


class Prog:
    def __init__(self, nc):
        self.nc = nc
        self.E = dict(pe=nc.tensor, dve=nc.vector, act=nc.scalar, pool=nc.gpsimd, sp=nc.sync)
        self.ops = []
        self._ctx = []

    def sb(self, name, shape, dt):
        cm = self.nc.sbuf_tensor(name, list(shape), dt)
        t = cm.__enter__()
        self._ctx.append(cm)
        return t

    def ps(self, name, shape, dt=F32):
        cm = self.nc.psum_tensor(name, list(shape), dt)
        t = cm.__enter__()
        self._ctx.append(cm)
        return t

    def sem(self, name):
        cm = self.nc.semaphore(name)
        s = cm.__enter__()
        self._ctx.append(cm)
        return s

    def op(self, eng, fn, reads=(), writes=(), dkey=None):
        r = [b for b in (_box(a) for a in reads) if b is not None]
        w = [b for b in (_box(a) for a in writes) if b is not None]
        self.ops.append(dict(eng=eng, fn=fn, r=r, w=w, dkey=dkey))

    def dma(self, q, out, in_, dkey, **kw):
        e = self.E[q]
        self.op(q, lambda: e.dma_start(out=out, in_=in_, **kw), reads=[in_], writes=[out], dkey=dkey)

    def emit(self, final_dkeys=()):
        ops = self.ops
        n = len(ops)
        hist = {}
        deps = [None] * n
        for i, o in enumerate(ops):
            d = set()
            for b in o["r"]:
                psum = b[0].startswith("pb")
                for ent in hist.get(b[0], ()):
                    if _overlap(ent[0], b) and (ent[2] or (psum and ops[ent[1]]["eng"] != o["eng"])):
                        d.add(ent[1])
            for b in o["w"]:
                for ent in hist.get(b[0], ()):
                    if _overlap(ent[0], b):
                        d.add(ent[1])
            deps[i] = d
            for b in o["w"]:
                lst = hist.setdefault(b[0], [])
                lst[:] = [e for e in lst if not _covers(b, e[0])]
                lst.append([b, i, True])
            for b in o["r"]:
                lst = hist.setdefault(b[0], [])
                rep = False
                if o["dkey"] is None:
                    for e in lst:
                        if (not e[2]) and e[0] == b and ops[e[1]]["eng"] == o["eng"] and ops[e[1]]["dkey"] is None:
                            e[1] = i
                            rep = True
                            break
                if not rep:
                    lst.append([b, i, False])
        need_sig = [False] * n
        red = [None] * n
        for i, o in enumerate(ops):
            best = {}
            dm = []
            for j in deps[i]:
                oj = ops[j]
                if oj["dkey"] is not None:
                    dm.append(j)
                else:
                    if oj["eng"] == "pe" and o["eng"] == "pe" and o["dkey"] is None:
                        continue
                    if j > best.get(oj["eng"], -1):
                        best[oj["eng"]] = j
            red[i] = (best, dm)
            for j in best.values():
                need_sig[j] = True
        esem = {k: self.sem("s_" + k) for k in self.E}
        dsem = {}
        for o in ops:
            if o["dkey"] is not None and o["dkey"] not in dsem:
                dsem[o["dkey"]] = self.sem("d_" + str(o["dkey"]))
        sigcnt = {k: 0 for k in self.E}
        sigval = [0] * n
        dcnt = {k: 0 for k in dsem}
        dval = [0] * n
        waited = {k: {} for k in self.E}
        nwaits = 0
        for i, o in enumerate(ops):
            eng = o["eng"]
            e = self.E[eng]
            best, dm = red[i]
            ws = {}
            for k, j in best.items():
                ws[("e", k)] = max(ws.get(("e", k), 0), sigval[j])
            for j in dm:
                key = ("d", ops[j]["dkey"])
                ws[key] = max(ws.get(key, 0), dcnt[ops[j]["dkey"]])
            for key, v in ws.items():
                if waited[eng].get(key, 0) >= v:
                    continue
                waited[eng][key] = v
                s = esem[key[1]] if key[0] == "e" else dsem[key[1]]
                e.wait_ge(s, v)
                nwaits += 1
            ins = o["fn"]()
            if o["dkey"] is not None:
                dcnt[o["dkey"]] += 16
                dval[i] = dcnt[o["dkey"]]
                ins.then_inc(dsem[o["dkey"]], 16)
            elif need_sig[i]:
                sigcnt[eng] += 1
                sigval[i] = sigcnt[eng]
                ins.then_inc(esem[eng], 1)
        for k in final_dkeys:
            self.E["sp"].wait_ge(dsem[k], dcnt[k])
        self.stats = dict(n_ops=n, n_waits=nwaits, sig=dict(sigcnt))
        return self.stats


KT_IDENT = 0
KT_TRILE = 128
KT_SL = 256
KT_ATT = 384
KT_COLS = 512
ATT_CFG = ((128, 1), (512, 4), (2048, 16))
ATT_MBASE = (0, 2, 7)
ATT_NMASK = 23


def make_ktab():
    kt = np.zeros((128, KT_COLS), np.float32)
    kt[:, KT_IDENT:KT_IDENT + 128] = np.eye(128, dtype=np.float32)
    j = np.arange(128)[:, None]
    q = np.arange(128)[None, :]
    kt[:, KT_TRILE:KT_TRILE + 128] = (j <= q).astype(np.float32)
    kt[:, KT_SL:KT_SL + 128] = (q < j).astype(np.float32)
    for m in range(8):
        kt[m + 8, KT_ATT + m] = 1.0
        kt[m, KT_ATT + m + 8] = 1.0
    inv_freq = (500000.0 ** (-np.arange(0, 16, 2, dtype=np.float32) / 16.0)).astype(np.float32)
    for m in range(16):
        kt[m, KT_ATT + 64] = inv_freq[m % 8] / (2.0 * np.pi)
        kt[m, KT_ATT + 65] = -1.0 if m < 8 else 1.0
    return kt


def make_amask():
    am = np.full((128, ATT_NMASK, 128), -30000.0, np.float32)
    qi = np.arange(128)[:, None]
    ki = np.arange(128)[None, :]
    for g, (window, dil) in enumerate(ATT_CFG):
        nd = (2, 5, 16)[g]
        for dl in range(nd):
            dd = 128 * dl + qi - ki
            ok = (dd >= 0) & (dd % dil == 0) & (dd <= window)
            am[:, ATT_MBASE[g] + dl, :][ok] = 0.0
    return am.reshape(128, ATT_NMASK * 128)


class K:
    def __init__(self, layers):
        self.layers = layers
        nc = self.nc = bass.Bass("TRN2", target_bir_lowering=False)
        P = self.P = Prog(nc)
        dt = lambda name, shape, ty=F32, kind="ExternalInput": nc.dram_tensor(name, list(shape), ty, kind=kind).ap()
        self.d = d = {}
        d["x"] = dt("x", [S, D])
        d["c"] = dt("c", [8, 128])
        d["ktab"] = dt("ktab", [128, KT_COLS])
        d["ada_w"] = dt("ada_w", [4, D, 6 * D])
        d["ada_b"] = dt("ada_b", [4 * 48, 128])
        for nm in ("norm_mix_pre", "norm_mix_post", "norm_mlp_pre", "norm_mlp_post"):
            d[nm] = dt(nm, [4 * 8, 128])
        d["mlp_w1"] = dt("mlp_w1", [4, D, 4 * D])
        d["mlp_w2"] = dt("mlp_w2", [4, 4 * D, D])
        kinds = set(l % 3 for l in layers)
        if 0 in kinds:
            d["ssd_w_in"] = dt("ssd_w_in", [2, D, 6176])
            d["ssd_conv_w"] = dt("ssd_conv_w", [2 * 4 * 32, 128])
            d["ssd_conv_b"] = dt("ssd_conv_b", [2 * 32, 128])
            d["ssd_dt_bias"] = dt("ssd_dt_bias", [2, 32])
            d["ssd_a_log"] = dt("ssd_a_log", [2, 32])
            d["ssd_d"] = dt("ssd_d", [2, 32])
            d["ssd_norm"] = dt("ssd_norm", [2 * 16, 128])
            d["ssd_w_out"] = dt("ssd_w_out", [2, 2048, D])
        if 1 in kinds:
            d["lru_w_in"] = dt("lru_w_in", [1, D, 2048])
            d["lru_conv_w"] = dt("lru_conv_w", [4 * 8, 128])
            d["lru_conv_b"] = dt("lru_conv_b", [8, 128])
            d["lru_w_gate_a"] = dt("lru_w_gate_a", [1024, 256])
            d["lru_b_gate_a"] = dt("lru_b_gate_a", [8, 128])
            d["lru_w_gate_x"] = dt("lru_w_gate_x", [1024, 256])
            d["lru_b_gate_x"] = dt("lru_b_gate_x", [8, 128])
            d["lru_lambda"] = dt("lru_lambda", [8, 128])
            d["lru_w_out"] = dt("lru_w_out", [1, D, D])
        if 2 in kinds:
            d["attn_w_qkv"] = dt("attn_w_qkv", [1, D, 4608])
            d["attn_w_out"] = dt("attn_w_out", [1, 512, D])
            d["amask"] = dt("amask", [128, ATT_NMASK * 128])
            d["positions"] = dt("positions", [1, S], I32)
        d["out"] = dt("out", [S, D], F32, kind="ExternalOutput")

        self.XT = P.sb("XT", [128, NCH, S], F32)
        self.NRING = 4
        self.ring = [P.sb("ring%d" % i, [128, 4096], BF16) for i in range(self.NRING)]
        self.ring_i = 0
        self.SCRB = 104 * 1024
        self.SCR = P.sb("SCR", [128, self.SCRB // 2], BF16)
        self.CT = P.sb("CT", [128, 1024], F32)
        self.CB = P.sb("CB", [128, 512], BF16)
        self.ct_off = 0
        self.cb_off = 0
        self.pb = [P.ps("pb%d" % i, [128, 512], F32) for i in range(8)]
        self.acc_i = 0

        self.build()

    def scr(self, off, shape, dt):
        es = mybir.dt.size(dt)
        n = int(np.prod(shape[1:]))
        assert off % 4 == 0 and off + n * es <= self.SCRB, (off, shape)
        a = self.SCR[:, off // 2: off // 2 + n * es // 2]
        if dt != BF16:
            a = a.bitcast(dt)
        if len(shape) == 3:
            a = a.rearrange("p (a b) -> p a b", a=shape[1])
        elif len(shape) == 4:
            a = a.rearrange("p (a b c) -> p a b c", a=shape[1], b=shape[2])
        return a

    def ct(self, n):
        a = self.CT[:, self.ct_off:self.ct_off + n]
        self.ct_off += n
        assert self.ct_off <= 1024
        return a

    def cb(self, n):
        a = self.CB[:, self.cb_off:self.cb_off + n]
        self.cb_off += n
        assert self.cb_off <= 512
        return a

    def acc(self):
        b = self.pb[self.acc_i % 4]
        self.acc_i += 1
        return b

    def slot(self):
        s = self.ring_i % self.NRING
        self.ring_i += 1
        return s, self.ring[s]

    def wload(self, src, shape):
        s, t = self.slot()
        n = int(np.prod(shape[1:]))
        assert n <= 4096
        v = t[:, 0:n]
        if len(shape) == 3:
            v = v.rearrange("p (a b) -> p a b", a=shape[1])
        self.P.dma("pool", v, src, "ring%d" % s)
        return v

    def mm(self, out, lhsT, rhs, start, stop):
        nc = self.nc
        self.P.op("pe", lambda: nc.tensor.matmul(out, lhsT=lhsT, rhs=rhs, start=start, stop=stop), [lhsT, rhs], [out])

    def tr(self, out, in_, ident):
        nc = self.nc
        self.P.op("pe", lambda: nc.tensor.transpose(out=out, in_=in_, identity=ident), [in_, ident], [out])

    def act(self, out, in_, func, bias=None, scale=None):
        nc = self.nc
        kw = {}
        rd = [in_]
        if bias is not None:
            kw["bias"] = bias
            if not isinstance(bias, (int, float)):
                rd.append(bias)
        if scale is not None:
            kw["scale"] = scale
            if not isinstance(scale, (int, float)):
                rd.append(scale)
        self.P.op("act", lambda: nc.scalar.activation(out=out, in_=in_, func=func, **kw), rd, [out])

    def tt(self, out, in0, in1, op, eng="dve"):
        e = self.P.E[eng]
        self.P.op(eng, lambda: e.tensor_tensor(out=out, in0=in0, in1=in1, op=op), [in0, in1], [out])

    def ts(self, out, in0, s1, s2, op0, op1=None, eng="dve"):
        e = self.P.E[eng]
        rd = [in0] + [s for s in (s1, s2) if s is not None and not isinstance(s, (int, float))]
        if op1 is None:
            self.P.op(eng, lambda: e.tensor_scalar(out=out, in0=in0, scalar1=s1, scalar2=None, op0=op0), rd, [out])
        else:
            self.P.op(eng, lambda: e.tensor_scalar(out=out, in0=in0, scalar1=s1, scalar2=s2, op0=op0, op1=op1), rd, [out])

    def stt(self, out, in0, scalar, in1, op0, op1):
        nc = self.nc
        rd = [in0, in1] + ([] if isinstance(scalar, (int, float)) else [scalar])
        self.P.op("dve", lambda: nc.vector.scalar_tensor_tensor(out=out, in0=in0, scalar=scalar, in1=in1, op0=op0, op1=op1), rd, [out])

    def copy(self, out, in_, eng="dve"):
        if eng == "act":
            self.act(out, in_, AF.Copy)
        else:
            e = self.P.E[eng]
            self.P.op(eng, lambda: e.tensor_copy(out=out, in_=in_), [in_], [out])

    def recip(self, out, in_):
        nc = self.nc
        self.P.op("dve", lambda: nc.vector.reciprocal(out=out, in_=in_), [in_], [out])

    def memset(self, ap, val, eng="dve"):
        e = self.P.E[eng]
        self.P.op(eng, lambda: e.memset(ap, val), [], [ap])

    def scan(self, out, d0, d1, initial):
        nc = self.nc
        rd = [d0, d1] + ([] if isinstance(initial, (int, float)) else [initial])
        self.P.op("dve", lambda: nc.vector.tensor_tensor_scan(out=out, data0=d0, data1=d1, initial=initial, op0=ALU.mult, op1=ALU.add), rd, [out])

    def load_rowsT(self, rows_ap, nrows, dst, ncols=128, stoff=0):
        st = self.scr(stoff, (128, 128), F32)
        self.P.dma("sp", st[0:nrows, 0:ncols], rows_ap, "cst")
        ps = self.pb[7]
        self.tr(ps[0:ncols, 0:nrows], st[0:nrows, 0:ncols], self.ident[0:nrows, 0:nrows])
        self.copy(dst, ps[0:ncols, 0:nrows])

    def setup(self):
        P, d = self.P, self.d
        self.ident = self.ct(128)
        P.dma("sp", self.ident, d["ktab"][:, KT_IDENT:KT_IDENT + 128], "ident")
        self.identb = self.cb(128)
        self.copy(self.identb, self.ident)
        self.onesb = self.cb(128)
        self.memset(self.onesb, 1.0 / 1024.0)
        self.epsc = self.ct(1)
        self.memset(self.epsc, EPS)
        self.onec = self.ct(1)
        self.memset(self.onec, 1.0)
        self.NG = self.ct(128)
        for k, nm in enumerate(("norm_mix_pre", "norm_mix_post", "norm_mlp_pre", "norm_mlp_post")):
            self.load_rowsT(d[nm], 32, self.NG[:, k * 32:(k + 1) * 32])
        self.AB = self.ct(192)
        self.load_rowsT(d["ada_b"][0:96, :], 96, self.AB[:, 0:96])
        self.load_rowsT(d["ada_b"][96:192, :], 96, self.AB[:, 96:192])
        cT = self.ct(8)
        self.load_rowsT(d["c"], 8, cT)
        self.SCb = self.cb(8)
        self.act(self.SCb, cT, AF.Silu)
        self.MOD = self.ct(192)

    def ada_layer(self, l):
        d = self.d
        ps = self.pb[6]
        for blk in range(12):
            w = self.wload(d["ada_w"][l, :, blk * 512:(blk + 1) * 512].rearrange("(kc p) n -> p kc n", p=128), (128, 8, 512))
            for jj in range(4):
                j = blk * 4 + jj
                for kc in range(8):
                    self.mm(ps[:, j:j + 1], w[:, kc, jj * 128:(jj + 1) * 128], self.SCb[:, kc:kc + 1], kc == 0, kc == 7)
        self.tt(self.MOD[:, l * 48:(l + 1) * 48], ps[:, 0:48], self.AB[:, l * 48:(l + 1) * 48], ALU.add)
        m = lambda w_: self.MOD[:, l * 48 + w_ * 8: l * 48 + w_ * 8 + 8]
        g = lambda k: self.NG[:, k * 32 + l * 8: k * 32 + l * 8 + 8]
        V = {}
        for nm in ("A_m", "G_m", "A_f", "G_f"):
            V[nm] = self.ct(8)
        self.stt(V["A_m"], m(1), 1.0, g(0), ALU.add, ALU.mult)
        self.stt(V["G_m"], m(2), 1.0, g(1), ALU.add, ALU.mult)
        self.stt(V["A_f"], m(4), 1.0, g(2), ALU.add, ALU.mult)
        self.stt(V["G_f"], m(5), 1.0, g(3), ALU.add, ALU.mult)
        V["B_m"] = m(0)
        V["B_f"] = m(3)
        return V

    def load_x(self):
        P, d = self.P, self.d
        for i in range(16):
            st = self.scr((i % 2) * 4096, (128, 1024), F32)
            P.dma("sp", st, d["x"][i * 128:(i + 1) * 128, :], "xin%d" % (i % 2))
            for half in range(2):
                ps = self.pb[4 + half]
                for q in range(4):
                    kc = half * 4 + q
                    self.tr(ps[:, q * 128:(q + 1) * 128], st[:, kc * 128:(kc + 1) * 128], self.ident)
                dst = self.XT[:, half * 4:(half + 1) * 4, i * 128:(i + 1) * 128]
                src = ps[:, :].rearrange("p (a b) -> p a b", a=4)
                self.copy(dst, src, eng="act" if half else "dve")

    def store_x(self):
        P, d = self.P, self.d
        for i in range(16):
            st = self.scr((i % 2) * 4096, (128, 1024), F32)
            for half in range(2):
                ps = self.pb[4 + half]
                for q in range(4):
                    kc = half * 4 + q
                    self.tr(ps[:, q * 128:(q + 1) * 128], self.XT[:, kc, i * 128:(i + 1) * 128], self.ident)
                self.copy(st[:, half * 512:(half + 1) * 512], ps[:, :], eng="act" if half else "dve")
            P.dma("sp", d["out"][i * 128:(i + 1) * 128, :], st, "out%d" % (i % 2))

    def rstd512(self, srcs, rstd, sqoff, onesb=None, nchunks=8):
        ss = self.pb[7]
        onesb = self.onesb if onesb is None else onesb
        for kc, s in enumerate(srcs):
            sq = self.scr(sqoff + (kc % 2) * 1024, (128, 512), BF16)
            self.act(sq, s, AF.Square)
            self.mm(ss[:, :], onesb, sq, kc == 0, kc == len(srcs) - 1)
        self.act(rstd, ss[:, :], AF.Sqrt, bias=self.epsc, scale=1.0)
        self.recip(rstd, rstd)

    def prenorm(self, t0, T, A, B, hT, tmpoff):
        for sub in range(T // 512):
            a = t0 + sub * 512
            rstd = self.scr(tmpoff + 2048, (128, 512), F32)
            self.rstd512([self.XT[:, kc, a:a + 512] for kc in range(8)], rstd, tmpoff)
            for kc in range(8):
                tmp = self.scr(tmpoff + 4096 + (kc % 2) * 2048, (128, 512), F32)
                self.tt(tmp, self.XT[:, kc, a:a + 512], rstd, ALU.mult)
                self.act(hT[:, kc, sub * 512:(sub + 1) * 512], tmp, AF.Identity, bias=B[:, kc:kc + 1], scale=A[:, kc:kc + 1])

    def postnorm_add(self, t0, yT, G, tmpoff):
        rstd = self.scr(tmpoff + 2048, (128, 512), F32)
        self.rstd512([yT[:, kc, :] for kc in range(8)], rstd, tmpoff)
        for kc in range(8):
            tmp = self.scr(tmpoff + 4096 + (kc % 2) * 2048, (128, 512), F32)
            self.stt(tmp, yT[:, kc, :], G[:, kc:kc + 1], rstd, ALU.mult, ALU.mult)
            xs = self.XT[:, kc, t0:t0 + 512]
            self.tt(xs, xs, tmp, ALU.add)

    def mlp(self, l, V):
        d = self.d
        T = 1024
        HID = 0
        HT = 65536
        TMP = 65536 + 32768
        SQ = TMP + 4096
        hid = self.scr(HID, (128, 32, T), BF16)
        hT = self.scr(HT, (128, 8, T), BF16)
        yT = self.scr(HT, (128, 16, 512), F32)
        for tt_ in range(S // T):
            t0 = tt_ * T
            self.prenorm(t0, T, V["A_f"], V["B_f"], hT, TMP)
            for hb in range(8):
                w = self.wload(d["mlp_w1"][l, :, hb * 512:(hb + 1) * 512].rearrange("(kc p) n -> p kc n", p=128), (128, 8, 512))
                for j in range(4):
                    hc = hb * 4 + j
                    for sub in range(2):
                        ps = self.acc()
                        for kc in range(8):
                            self.mm(ps[:, :], w[:, kc, j * 128:(j + 1) * 128], hT[:, kc, sub * 512:(sub + 1) * 512], kc == 0, kc == 7)
                        sq = self.scr(SQ + (self.acc_i % 2) * 2048, (128, 512), F32)
                        self.act(sq, ps[:, :], AF.Square)
                        self.stt(hid[:, hc, sub * 512:(sub + 1) * 512], ps[:, :], 0.0, sq, ALU.is_gt, ALU.mult)
            for oc in range(8):
                w2 = self.wload(d["mlp_w2"][l, :, oc * 128:(oc + 1) * 128].rearrange("(hc p) n -> p hc n", p=128), (128, 32, 128))
                for sub in range(2):
                    ps = self.acc()
                    for hc in range(32):
                        self.mm(ps[:, :], w2[:, hc, :], hid[:, hc, sub * 512:(sub + 1) * 512], hc == 0, hc == 31)
                    self.copy(yT[:, sub * 8 + oc, :], ps[:, :], eng="act" if oc % 2 else "dve")
            for sub in range(2):
                self.postnorm_add(t0 + sub * 512, yT[:, sub * 8:(sub + 1) * 8, :], V["G_f"], TMP)

    def mixer_lru(self, l, V):
        d = self.d
        o = [0]

        def A(nb):
            r = o[0]
            o[0] += (nb + 63) // 64 * 64
            return r
        hT = self.scr(A(8192), (128, 8, 512), BF16)
        xrpre = self.scr(A(8 * 516 * 4), (128, 8, 516), F32)
        xro = A(16384)
        xr = self.scr(xro, (128, 8, 512), F32)
        yT = self.scr(xro, (128, 8, 512), F32)
        xrb = self.scr(A(8192), (128, 8, 512), BF16)
        ylru = self.scr(A(8192), (128, 8, 512), BF16)
        TMP = A(8192)
        tmp = [[self.scr(A(2048), (128, 512), F32) for _ in range(2)] for _ in range(6)]
        CW = self.ct(32)
        self.load_rowsT(d["lru_conv_w"], 32, CW)
        misc = self.ct(32)
        self.load_rowsT(d["lru_conv_b"], 8, misc[:, 0:8])
        self.load_rowsT(d["lru_b_gate_a"], 8, misc[:, 8:16])
        self.load_rowsT(d["lru_b_gate_x"], 8, misc[:, 16:24])
        self.load_rowsT(d["lru_lambda"], 8, misc[:, 24:32])
        cbias, bga, bgx, lam = misc[:, 0:8], misc[:, 8:16], misc[:, 16:24], misc[:, 24:32]
        cL = self.ct(8)
        cL2 = self.ct(8)
        state = self.ct(8)
        self.act(cL, lam, AF.Exp, scale=-1.0)
        self.act(cL, cL, AF.Ln, bias=self.onec, scale=1.0)
        self.ts(cL2, cL, -16.0, None, ALU.mult)
        self.ts(cL, cL, -8.0, None, ALU.mult)
        self.memset(state, 0.0)
        self.memset(xrpre[:, :, 0:3], 0.0)
        w_in = d["lru_w_in"][0]
        for tt_ in range(4):
            t0 = tt_ * 512
            self.prenorm(t0, 512, V["A_m"], V["B_m"], hT, TMP)
            for cb in range(8):
                if cb % 4 == 0:
                    w = self.wload(w_in[:, 1024 + (cb // 4) * 512: 1024 + (cb // 4 + 1) * 512].rearrange("(kc p) n -> p kc n", p=128), (128, 8, 512))
                ps = self.acc()
                for kc in range(8):
                    self.mm(ps[:, :], w[:, kc, (cb % 4) * 128:(cb % 4 + 1) * 128], hT[:, kc, :], kc == 0, kc == 7)
                self.copy(xrpre[:, cb, 3:515], ps[:, :], eng="act")
                self.act(xr[:, cb, :], xrpre[:, cb, 3:515], AF.Identity, bias=cbias[:, cb:cb + 1], scale=CW[:, 24 + cb:25 + cb])
                for k in range(3):
                    self.stt(xr[:, cb, :], xrpre[:, cb, k:k + 512], CW[:, k * 8 + cb:k * 8 + cb + 1], xr[:, cb, :], ALU.mult, ALU.add)
                self.copy(xrb[:, cb, :], xr[:, cb, :], eng="act")
                self.copy(xrpre[:, cb, 0:3], xrpre[:, cb, 512:515])
            Wa = self.wload(d["lru_w_gate_a"].rearrange("(q p) j -> p q j", p=128), (128, 8, 256))
            Wx = self.wload(d["lru_w_gate_x"].rearrange("(q p) j -> p q j", p=128), (128, 8, 256))
            for cb in range(8):
                blk, jc = cb // 2, cb % 2
                if cb % 4 == 0:
                    wg = self.wload(w_in[:, (cb // 4) * 512:(cb // 4 + 1) * 512].rearrange("(kc p) n -> p kc n", p=128), (128, 8, 512))
                pa = self.acc()
                for ic in range(2):
                    self.mm(pa[:, :], Wa[:, blk * 2 + ic, jc * 128:(jc + 1) * 128], xrb[:, blk * 2 + ic, :], ic == 0, ic == 1)
                px = self.acc()
                for ic in range(2):
                    self.mm(px[:, :], Wx[:, blk * 2 + ic, jc * 128:(jc + 1) * 128], xrb[:, blk * 2 + ic, :], ic == 0, ic == 1)
                pg = self.acc()
                for kc in range(8):
                    self.mm(pg[:, :], wg[:, kc, (cb % 4) * 128:(cb % 4 + 1) * 128], hT[:, kc, :], kc == 0, kc == 7)
                b = cb % 2
                tr_, ti, ta, te, ths, tg = (tmp[q][b] for q in range(6))
                self.act(tr_, pa[:, :], AF.Sigmoid, bias=bga[:, cb:cb + 1], scale=1.0)
                self.act(ti, px[:, :], AF.Sigmoid, bias=bgx[:, cb:cb + 1], scale=1.0)
                self.act(ta, tr_, AF.Exp, scale=cL[:, cb:cb + 1])
                self.act(te, tr_, AF.Exp, scale=cL2[:, cb:cb + 1])
                self.act(te, te, AF.Sqrt, bias=self.onec, scale=-1.0)
                self.tt(ti, ti, xr[:, cb, :], ALU.mult)
                self.tt(ti, ti, te, ALU.mult)
                self.scan(ths, ta, ti, state[:, cb:cb + 1])
                self.copy(state[:, cb:cb + 1], ths[:, 511:512])
                self.act(tg, pg[:, :], AF.Gelu_apprx_tanh)
                self.tt(ylru[:, cb, :], ths, tg, ALU.mult)
            for ob in range(2):
                wo = self.wload(d["lru_w_out"][0][:, ob * 512:(ob + 1) * 512].rearrange("(kc p) n -> p kc n", p=128), (128, 8, 512))
                for j in range(4):
                    oc = ob * 4 + j
                    ps = self.acc()
                    for kc in range(8):
                        self.mm(ps[:, :], wo[:, kc, j * 128:(j + 1) * 128], ylru[:, kc, :], kc == 0, kc == 7)
                    self.copy(yT[:, oc, :], ps[:, :], eng="act" if oc % 2 else "dve")
            self.postnorm_add(t0, yT, V["G_m"], TMP)

    def act_acc(self, out, in_, func, accum_out):
        nc = self.nc
        self.P.op("act", lambda: nc.scalar.activation(out=out, in_=in_, func=func, accum_out=accum_out), [in_], [out, accum_out])

    def red(self, out, in_, op=ALU.add):
        nc = self.nc
        self.P.op("dve", lambda: nc.vector.tensor_reduce(out=out, in_=in_, axis=mybir.AxisListType.X, op=op), [in_], [out])

    def ttr(self, out, in0, in1, accum_out):
        nc = self.nc
        self.P.op("dve", lambda: nc.vector.tensor_tensor_reduce(out=out, in0=in0, in1=in1, scale=1.0, scalar=0.0, op0=ALU.mult, op1=ALU.add, accum_out=accum_out), [in0, in1], [out, accum_out])

    def mixer_ssd(self, l, V):
        d, P = self.d, self.P
        occ = l // 3
        o = [0]

        def A(nb):
            r = o[0]
            o[0] += (nb + 63) // 64 * 64
            return r
        bc = lambda ap, shape, axis: ap.unsqueeze(axis).broadcast_to(list(shape))
        ng_bc = self.scr(A(8192), (128, 2048), F32)
        triLE = self.scr(A(512), (128, 128), F32)
        onesF = self.scr(A(512), (128, 128), F32)
        SLb = self.scr(A(1024), (128, 4, 128), BF16)
        negI = self.scr(A(256), (128, 128), BF16)
        triB = self.scr(A(256), (128, 128), BF16)
        onesB = self.scr(A(256), (128, 128), BF16)
        dhl = self.scr(A(512), (128, 2, 4, 32), BF16) if False else self.scr(A(512), (128, 8, 32), BF16)
        Dh_bc = self.scr(A(128), (128, 32), F32)
        convw = self.scr(A(512), (128, 128), F32)
        convb = self.scr(A(128), (128, 32), F32)
        hcol = self.scr(A(64), (128, 4), F32)
        stF = self.scr(A(8192), (128, 8, 256), F32)
        stB = self.scr(A(4096), (128, 8, 256), BF16)
        HAL = self.scr(A(384), (128, 32, 3), F32)
        hT = self.scr(A(8192), (128, 8, 512), BF16)
        dtT = self.scr(A(2048), (128, 512), F32)
        dtAT = self.scr(A(2048), (128, 512), F32)
        dt_tok, dtA_tok, acs_tok, nacs, eacs, wdec, cdec = (self.scr(A(512), (128, 4, 32), F32) for _ in range(7))
        xc = self.scr(A(4096), (128, 4, 512), BF16)
        z_tok = self.scr(A(2048), (128, 4, 256), BF16)
        x_tok = self.scr(A(2048), (128, 4, 256), BF16)
        xdt = self.scr(A(2048), (128, 4, 256), BF16)
        xdec = self.scr(A(2048), (128, 4, 256), BF16)
        B_tok = self.scr(A(1024), (128, 4, 128), BF16)
        ynT = self.scr(A(16384), (128, 16, 512), BF16)
        U = A(16640)
        yT = self.scr(U, (128, 8, 512), F32)
        pre = [self.scr(U + i * 2064, (128, 516), F32) for i in range(2)]
        cacc = [self.scr(U + 4128 + i * 2048, (128, 512), F32) for i in range(2)]
        R1 = [self.scr(U + 8224 + i * 2048, (128, 4, 128), F32) for i in range(2)]
        dec = [self.scr(U + 12320 + i * 2048, (128, 4, 128), F32) for i in range(2)]
        Mb = [self.scr(A(1024), (128, 4, 128), BF16) for _ in range(2)]
        t1 = [self.scr(A(1024), (128, 256), F32) for _ in range(2)]
        yg = [self.scr(A(1024), (128, 256), F32) for _ in range(2)]
        ynb = [self.scr(A(512), (128, 256), BF16) for _ in range(2)]
        ssq = self.scr(A(64), (128, 4), F32)
        TMP = A(8192)
        P.dma("sp", ng_bc, d["ssd_norm"].rearrange("a b -> (a b)")[occ * 2048:(occ + 1) * 2048].partition_broadcast(128), "sc0")
        P.dma("sp", triLE, d["ktab"][:, KT_TRILE:KT_TRILE + 128], "sc1")
        P.dma("sp", Dh_bc, d["ssd_d"][occ].partition_broadcast(128), "sc2")
        sl = self.scr(TMP, (128, 128), F32)
        P.dma("sp", sl, d["ktab"][:, KT_SL:KT_SL + 128], "sc3")
        self.copy(SLb, bc(sl, (128, 4, 128), 1))
        self.ts(negI, self.ident, -30000.0, None, ALU.mult)
        self.memset(onesF, 1.0)
        self.memset(onesB, 1.0)
        self.copy(triB, triLE)
        for k in range(4):
            self.load_rowsT(d["ssd_conv_w"][occ * 128 + k * 32: occ * 128 + (k + 1) * 32, :], 32, convw[:, k * 32:(k + 1) * 32], stoff=TMP + 1024)
        self.load_rowsT(d["ssd_conv_b"][occ * 32:(occ + 1) * 32, :], 32, convb, stoff=TMP + 1024)
        dtb_bc = self.scr(A(128), (128, 32), F32)
        a_bc = self.scr(A(128), (128, 32), F32)
        P.dma("sp", dtb_bc, d["ssd_dt_bias"][occ].partition_broadcast(128), "sc4")
        P.dma("sp", a_bc, d["ssd_a_log"][occ].partition_broadcast(128), "sc5")
        self.act(a_bc, a_bc, AF.Exp)
        self.ts(a_bc, a_bc, -1.0, None, ALU.mult)
        self.memset(stF, 0.0)
        self.memset(stB, 0.0)
        self.memset(HAL, 0.0)
        w_in = d["ssd_w_in"][occ]
        wv = lambda c0, n: w_in[:, c0:c0 + n].rearrange("(kc p) n -> p kc n", p=128)
        it = 0
        import os
        STG = int(os.environ.get("SSD_STAGE", "9"))
        for tt_ in range(4):
            t0 = tt_ * 512
            self.prenorm(t0, 512, V["A_m"], V["B_m"], hT, TMP)
            if STG < 2:
                continue
            SUB = int(os.environ.get("SSD_SUB", "99"))
            wdt = self.wload(wv(5664, 512), (128, 8, 512))
            pd = self.pb[4]
            pc = self.pb[5]
            dhi, dlo = dhl[:, 0:4, :], dhl[:, 4:8, :]
            f2 = lambda ap: ap.rearrange("p a b -> p (a b)")
            v3 = lambda ap: ap.rearrange("p (a b) -> p a b", a=4)

            def mmdt():
                for c in range(4):
                    for kc in range(8):
                        self.mm(pd[:, c * 32:(c + 1) * 32], hT[:, kc, c * 128:(c + 1) * 128], wdt[:, kc, 480:512], kc == 0, kc == 7)
            steps = [
                mmdt,
                lambda: self.tt(dt_tok, v3(pd[:, 0:128]), bc(dtb_bc, (128, 4, 32), 1), ALU.add),
                lambda: self.act(dt_tok, dt_tok, AF.Exp),
                lambda: self.act(dt_tok, dt_tok, AF.Ln, bias=self.onec, scale=1.0),
                lambda: self.tt(dtA_tok, dt_tok, bc(a_bc, (128, 4, 32), 1), ALU.mult),
                lambda: self.copy(dhi, dtA_tok),
                lambda: self.tt(dlo, dtA_tok, dhi, ALU.subtract),
                lambda: self.mm(pc[:, 0:128], triB, f2(dhi), True, False),
                lambda: self.mm(pc[:, 0:128], triB, f2(dlo), False, True),
                lambda: self.mm(pc[:, 128:256], onesB, f2(dhi), True, False),
                lambda: self.mm(pc[:, 128:256], onesB, f2(dlo), False, True),
                lambda: self.copy(acs_tok, v3(pc[:, 0:128])),
                lambda: self.ts(nacs, acs_tok, -1.0, None, ALU.mult),
                lambda: self.act(eacs, acs_tok, AF.Exp),
                lambda: self.copy(wdec, v3(pc[:, 128:256])),
                lambda: self.act(cdec, wdec, AF.Exp),
                lambda: self.tt(wdec, wdec, acs_tok, ALU.subtract),
                lambda: self.act(wdec, wdec, AF.Exp),
                lambda: self.tt(wdec, wdec, dt_tok, ALU.mult),
            ]
            for st_ in steps[:SUB]:
                st_()
            for g in range(8):
                if STG < 3:
                    continue
                s_, slotA = self.slot()
                wA = slotA[:, 0:4096].rearrange("p (a b) -> p a b", a=8)
                P.dma("pool", wA[:, :, 0:256], wv(2048 + g * 256, 256), "ring%d" % s_)
                P.dma("pool", wA[:, :, 256:384], wv(4096 + g * 128, 128), "ring%d" % s_)
                P.dma("pool", wA[:, :, 384:512], wv(5120 + g * 128, 128), "ring%d" % s_)
                wz = self.wload(wv(g * 256, 256), (128, 8, 256))
                qidx = [2 * g, 2 * g + 1, 16 + g, 24 + g]
                for q in range(4):
                    qq = qidx[q]
                    ps = self.acc()
                    for kc in range(8):
                        self.mm(ps[:, :], wA[:, kc, q * 128:(q + 1) * 128], hT[:, kc, :], kc == 0, kc == 7)
                    pr = pre[q % 2]
                    ca = cacc[q % 2]
                    self.copy(pr[:, 0:3], HAL[:, qq, :])
                    self.copy(pr[:, 3:515], ps[:, :], eng="act")
                    self.copy(HAL[:, qq, :], pr[:, 512:515])
                    self.act(ca, pr[:, 3:515], AF.Identity, bias=convb[:, qq:qq + 1], scale=convw[:, 96 + qq:97 + qq])
                    for k in range(3):
                        self.stt(ca, pr[:, k:k + 512], convw[:, k * 32 + qq:k * 32 + qq + 1], ca, ALU.mult, ALU.add)
                    self.act(xc[:, q, :], ca, AF.Silu)
                for c in range(4):
                    ps = self.acc()
                    for kc in range(8):
                        self.mm(ps[:, 0:256], hT[:, kc, c * 128:(c + 1) * 128], wz[:, kc, :], kc == 0, kc == 7)
                    self.act(z_tok[:, c, :], ps[:, 0:256], AF.Silu)
                for c in range(4):
                    if STG < 4:
                        continue
                    pt = self.pb[4 + c % 2]
                    ptb = pt[:, 0:256].bitcast(BF16)
                    for j in range(3):
                        self.tr(ptb[:, j * 128:(j + 1) * 128], xc[:, j, c * 128:(c + 1) * 128], self.identb)
                    self.copy(x_tok[:, c, :], ptb[:, 0:256], eng="act")
                    self.copy(B_tok[:, c, :], ptb[:, 256:384], eng="act")
                    xv = x_tok[:, c, :].rearrange("p (a b) -> p a b", a=4)
                    self.tt(xdt[:, c, :].rearrange("p (a b) -> p a b", a=4), xv, bc(dt_tok[:, c, g * 4:(g + 1) * 4], (128, 4, 64), 2), ALU.mult)
                    self.tt(xdec[:, c, :].rearrange("p (a b) -> p a b", a=4), xv, bc(wdec[:, c, g * 4:(g + 1) * 4], (128, 4, 64), 2), ALU.mult)
                v4 = lambda ap: ap.rearrange("p (a b) -> p a b", a=4)

                def front(c, g=g):
                    b = c % 2
                    cs = slice(c * 128, (c + 1) * 128)
                    pcb = self.pb[6 + c % 2]
                    self.mm(pcb[:, 0:128], xc[:, 2, cs], xc[:, 3, cs], True, True)
                    r1h = R1[b].rearrange("p a b -> p (a b)").bitcast(BF16).rearrange("p (h a b) -> p h a b", h=2, a=4)
                    self.tt(r1h[:, 0], bc(triB, (128, 4, 128), 1), bc(dhi[:, c, g * 4:(g + 1) * 4], (128, 4, 128), 2), ALU.mult, eng="pool")
                    self.tt(r1h[:, 1], bc(triB, (128, 4, 128), 1), bc(dlo[:, c, g * 4:(g + 1) * 4], (128, 4, 128), 2), ALU.mult, eng="pool")
                    psg = self.acc()
                    self.mm(psg[:, :], onesB, r1h[:, 0].rearrange("p a b -> p (a b)"), True, False)
                    self.mm(psg[:, :], onesB, r1h[:, 1].rearrange("p a b -> p (a b)"), False, False)
                    self.mm(psg[:, :], negI, SLb.rearrange("p a b -> p (a b)"), False, True)
                    for e in range(4):
                        self.act(dec[b][:, e, :], psg[:, e * 128:(e + 1) * 128], AF.Exp, bias=nacs[:, c, g * 4 + e:g * 4 + e + 1], scale=1.0)

                def back(c, g=g):
                    b = c % 2
                    cs = slice(c * 128, (c + 1) * 128)
                    pcb = self.pb[6 + c % 2]
                    self.tt(Mb[b], dec[b], bc(pcb[:, 0:128], (128, 4, 128), 1), ALU.mult)
                    py = self.acc()
                    for e in range(4):
                        self.mm(py[:, e * 64:(e + 1) * 64], Mb[b][:, e, :], xdt[:, c, e * 64:(e + 1) * 64], True, True)
                    self.mm(py[:, 256:512], xc[:, 3, cs], stB[:, g, :], True, True)
                    pst = self.acc()
                    self.mm(pst[:, 0:256], B_tok[:, c, :], xdec[:, c, :], True, True)
                    self.tt(v4(stF[:, g, :]), v4(stF[:, g, :]), bc(cdec[:, c, g * 4:(g + 1) * 4], (128, 4, 64), 2), ALU.mult)
                    self.tt(stF[:, g, :], stF[:, g, :], pst[:, 0:256], ALU.add)
                    self.copy(stB[:, g, :], stF[:, g, :], eng="act")
                    self.tt(v4(t1[b]), v4(py[:, 256:512]), bc(eacs[:, c, g * 4:(g + 1) * 4], (128, 4, 64), 2), ALU.mult)
                    self.tt(t1[b], t1[b], py[:, 0:256], ALU.add)
                    self.tt(v4(yg[b]), v4(x_tok[:, c, :]), bc(Dh_bc[:, g * 4:(g + 1) * 4], (128, 4, 64), 2), ALU.mult, eng="pool")
                    self.tt(t1[b], t1[b], yg[b], ALU.add)
                    self.tt(yg[b], t1[b], z_tok[:, c, :], ALU.mult)
                    self.tt(t1[b], yg[b], yg[b], ALU.mult)
                    self.red(ssq[:, b:b + 1], t1[b])
                    self.act(ssq[:, 2 + b:3 + b], ssq[:, b:b + 1], AF.Ln, bias=self.epsc, scale=1.0 / 256.0)
                    self.act(ssq[:, 2 + b:3 + b], ssq[:, 2 + b:3 + b], AF.Exp, scale=-0.5)
                    self.stt(ynb[b], yg[b], ssq[:, 2 + b:3 + b], ng_bc[:, g * 256:(g + 1) * 256], ALU.mult, ALU.mult)
                    pt = self.pb[4 + c % 2]
                    ptb = pt[:, 0:256].bitcast(BF16)
                    for j in range(2):
                        self.tr(ptb[:, j * 128:(j + 1) * 128], ynb[b][:, j * 128:(j + 1) * 128], self.identb)
                    self.copy(ynT[:, 2 * g:2 * g + 2, cs], ptb[:, 0:256].rearrange("p (a b) -> p a b", a=2), eng="act")

                if STG >= 5:
                    front(0)
                    for c in range(4):
                        if c < 3:
                            front(c + 1)
                        back(c)
            if STG < 6:
                continue
            w_out = d["ssd_w_out"][occ]
            for ob in range(2):
                wo = [self.wload(w_out[hf * 1024:(hf + 1) * 1024, ob * 512:(ob + 1) * 512].rearrange("(kc p) n -> p kc n", p=128), (128, 8, 512)) for hf in range(2)]
                for j in range(4):
                    oc = ob * 4 + j
                    ps = self.acc()
                    for q in range(16):
                        self.mm(ps[:, :], wo[q // 8][:, q % 8, j * 128:(j + 1) * 128], ynT[:, q, :], q == 0, q == 15)
                    self.copy(yT[:, oc, :], ps[:, :], eng="act" if oc % 2 else "dve")
            self.postnorm_add(t0, yT, V["G_m"], TMP)

    def mixer_attn(self, l, V):
        d, P = self.d, self.P
        o = [0]

        def A(nb):
            r = o[0]
            o[0] += (nb + 63) // 64 * 64
            return r
        bc = lambda ap, shape, axis: ap.unsqueeze(axis).broadcast_to(list(shape))
        HT0 = A(32768)
        hT = self.scr(HT0, (128, 8, 2048), BF16)
        yT = self.scr(HT0 + 16384, (128, 8, 512), F32)
        masks = self.scr(A(ATT_NMASK * 256), (128, ATT_NMASK, 128), BF16)
        cosT = self.scr(A(4096), (128, 2048), BF16)
        sinT = self.scr(A(4096), (128, 2048), BF16)
        Pm = self.scr(A(128), (128, 64), BF16)
        QT0 = o[0]
        oT = self.scr(QT0, (128, 4, 2048), BF16)
        qT = self.scr(A(4096), (128, 2048), BF16)
        kT = self.scr(A(4096), (128, 2048), BF16)
        vv = self.scr(A(2048), (128, 16, 64), BF16)
        qraw = [self.scr(A(1024), (128, 512), BF16) for _ in range(2)]
        T1 = A(4096)
        t1 = [self.scr(T1 + i * 2048, (128, 512), F32) for i in range(2)]
        S0 = A(8192)
        sbuf = [self.scr(S0 + i * 4096, (128, 1024), F32) for i in range(2)]
        TMP = S0
        pbuf = [self.scr(A(2048), (128, 1024), BF16) for _ in range(2)]
        pT = [self.scr(A(1024), (128, 4, 128), BF16) for _ in range(2)]
        OST0 = A(8192)
        Ost = self.scr(OST0, (128, 16, 4, 64), BF16)
        stat = self.scr(A(512), (128, 16, 8), F32)
        OTK0 = A(16384)
        o_tok = self.scr(OTK0, (128, 16, 512), BF16)
        sm = self.scr(A(1024), (128, 256), F32)
        oacc = self.scr(A(1024), (128, 4, 64), F32)
        assert o[0] <= self.SCRB, o[0]
        att = self.ct(128)
        P.dma("sp", att, d["ktab"][:, KT_ATT:KT_ATT + 128], "att0")
        self.copy(Pm[0:64, :], att[0:64, 0:64])
        invf, sgn = att[0:64, 64:65], att[0:64, 65:66]
        P.dma("pool", masks, d["amask"].rearrange("p (a b) -> p a b", a=ATT_NMASK), "att1", max_dma_last_dim=2048)
        posi = self.scr(OTK0, (128, 2048), I32)
        posf = self.scr(OTK0 + 8192, (128, 2048), F32)
        r1 = self.scr(OST0, (128, 2048), F32)
        ki = self.scr(OTK0, (128, 2048), I32)
        R = slice(0, 64)
        P.dma("sp", posi[R, :], d["positions"][0].partition_broadcast(64), "att2")
        self.copy(posf[R, :], posi[R, :])
        self.ts(r1[R, :], posf[R, :], invf, None, ALU.mult)
        self.copy(ki[R, :], r1[R, :])
        self.copy(posf[R, :], ki[R, :])
        self.tt(r1[R, :], r1[R, :], posf[R, :], ALU.subtract)
        tmpm = self.scr(OTK0, (128, 2048), F32)
        for which, dst in ((0, sinT), (1, cosT)):
            f = posf
            if which == 0:
                self.copy(f[R, :], r1[R, :])
            else:
                self.ts(f[R, :], r1[R, :], 0.25, None, ALU.add)
            self.ts(tmpm[R, :], f[R, :], 0.5, None, ALU.is_gt)
            self.tt(f[R, :], f[R, :], tmpm[R, :], ALU.subtract)
            self.ts(tmpm[R, :], f[R, :], -0.5, None, ALU.is_lt)
            self.tt(f[R, :], f[R, :], tmpm[R, :], ALU.add)
            self.ts(f[R, :], f[R, :], 0.49999997, -0.49999997, ALU.min, ALU.max)
            if which == 0:
                self.act(tmpm[R, :], f[R, :], AF.Sin, scale=2.0 * math.pi)
                self.ts(dst[R, :], tmpm[R, :], sgn, None, ALU.mult)
            else:
                self.act(dst[R, :], f[R, :], AF.Sin, scale=2.0 * math.pi)
        self.prenorm(0, 2048, V["A_m"], V["B_m"], hT, TMP)
        self.memset(Ost, 0.0)
        import os
        ASTG = int(os.environ.get("ATT_STAGE", "9"))
        if ASTG < 2:
            return
        w_qkv = d["attn_w_qkv"][0]
        wv_ = lambda c0, n: w_qkv[:, c0:c0 + n].rearrange("(kc p) n -> p kc n", p=128)
        pbTb = [self.pb[4][:, :].bitcast(BF16), self.pb[5][:, :].bitcast(BF16)]
        it = 0
        tcount = 0
        pi_ = 0
        for h in range(8):
            self.memset(stat[:, :, 0:8:2], -30000.0)
            self.memset(stat[:, :, 1:8:2], 0.0)
            for g in range(3):
                s_, slt = self.slot()
                w = slt[:, 0:1536].rearrange("p (a b) -> p a b", a=8)
                for j in range(3):
                    P.dma("pool", w[:, :, j * 64:(j + 1) * 64], wv_(g * 1536 + j * 512 + h * 64, 64), "ring%d" % s_)
                for j, dst in ((0, qT), (1, kT)):
                    for tq in range(4):
                        ts_ = slice(tq * 512, (tq + 1) * 512)
                        ps = self.pb[6 + pi_ % 2]
                        sw = self.pb[4 + pi_ % 2]
                        qr = qraw[pi_ % 2]
                        ta = t1[pi_ % 2]
                        pi_ += 1
                        for kc in range(8):
                            self.mm(ps[R, :], w[:, kc, j * 64:(j + 1) * 64], hT[:, kc, ts_], kc == 0, kc == 7)
                        self.copy(qr[R, :], ps[R, :], eng="act")
                        self.mm(sw[R, :], Pm[R, :], qr[R, :], True, True)
                        self.tt(ta[R, :], qr[R, :], cosT[R, ts_], ALU.mult, eng="pool")
                        self.tt(dst[R, ts_], sw[R, :], sinT[R, ts_], ALU.mult)
                        self.tt(dst[R, ts_], dst[R, ts_], ta[R, :], ALU.add, eng="pool")
                for half in range(2):
                    ps = self.pb[6 + half]
                    for k8 in range(8):
                        kt = half * 8 + k8
                        for kc in range(8):
                            self.mm(ps[:, k8 * 64:(k8 + 1) * 64], hT[:, kc, kt * 128:(kt + 1) * 128], w[:, kc, 128:192], kc == 0, kc == 7)
                    self.copy(vv[:, half * 8:(half + 1) * 8, :], ps[:, :].rearrange("p (a b) -> p a b", a=8), eng="act")
                plist = []
                for qt in range(16):
                    if g == 0:
                        plist.append((qt, 0, list(range(0, min(1, qt) + 1))))
                    elif g == 1:
                        plist.append((qt, 1, list(range(0, min(4, qt) + 1))))
                    else:
                        plist.append((qt, 2, list(range(0, min(7, qt) + 1))))
                        if qt >= 8:
                            plist.append((qt, 3, list(range(8, qt + 1))))

                def pfront(i, g=g, plist=plist):
                    qt, pidx, deltas = plist[i]
                    n = len(deltas)
                    b = i % 2
                    Rg = (self.pb[2 * b], self.pb[2 * b + 1])
                    for j, dl in enumerate(deltas):
                        kt = qt - dl
                        bank, col = Rg[j // 4], (j % 4) * 128
                        self.mm(bank[:, col:col + 128], qT[R, qt * 128:(qt + 1) * 128], kT[R, kt * 128:(kt + 1) * 128], True, True)
                    sb_, pb_ = sbuf[b], pbuf[b]
                    mi = ATT_MBASE[g] + deltas[0]
                    for bk in range((n + 3) // 4):
                        cnt = min(4, n - bk * 4)
                        mk = masks[:, mi + bk * 4: mi + bk * 4 + cnt, :].rearrange("p a b -> p (a b)")
                        self.stt(sb_[:, bk * 512: bk * 512 + cnt * 128], Rg[bk][:, 0:cnt * 128], 0.125, mk, ALU.mult, ALU.add)
                    mcol = stat[:, qt, 2 * pidx:2 * pidx + 1]
                    lcol = stat[:, qt, 2 * pidx + 1:2 * pidx + 2]
                    nm = sm[:, 8 + b:9 + b]
                    self.red(mcol, sb_[:, 0:n * 128], ALU.max)
                    self.ts(nm, mcol, -1.0, None, ALU.mult)

                def pfront_b(i, plist=plist):
                    qt, pidx, deltas = plist[i]
                    n = len(deltas)
                    b = i % 2
                    self.act(pbuf[b][:, 0:n * 128], sbuf[b][:, 0:n * 128], AF.Exp, bias=sm[:, 8 + b:9 + b], scale=1.0)

                def pback(i, plist=plist):
                    nonlocal tcount
                    qt, pidx, deltas = plist[i]
                    n = len(deltas)
                    b = i % 2
                    pb_ = pbuf[b]
                    self.red(stat[:, qt, 2 * pidx + 1:2 * pidx + 2], pb_[:, 0:n * 128], ALU.add)
                    Ops = self.pb[6 + b][:, 0:64]
                    for b4 in range((n + 3) // 4):
                        cnt = min(4, n - b4 * 4)
                        hf = tcount % 2
                        tcount += 1
                        tb = pbTb[hf][:, 0: cnt * 128]
                        for jj in range(cnt):
                            j = b4 * 4 + jj
                            self.tr(tb[:, jj * 128:(jj + 1) * 128], pb_[:, j * 128:(j + 1) * 128], self.identb)
                        self.copy(pT[hf][:, 0:cnt, :], tb.rearrange("p (a b) -> p a b", a=cnt), eng="act")
                        for jj in range(cnt):
                            j = b4 * 4 + jj
                            kt = qt - deltas[j]
                            self.mm(Ops, pT[hf][:, jj, :], vv[:, kt, :], j == 0, j == n - 1)
                    self.copy(Ost[:, qt, pidx, :], Ops, eng="act")

                if ASTG >= 3:
                    pfront(0)
                    pfront_b(0)
                    for i in range(len(plist)):
                        if i + 1 < len(plist):
                            pfront(i + 1)
                        pback(i)
                        if i + 1 < len(plist):
                            pfront_b(i + 1)
            if ASTG < 4:
                continue
            mview, lview = stat[:, :, 0:8:2], stat[:, :, 1:8:2]
            M = sm[:, 16:32]
            e = sm[:, 32:96].rearrange("p (a b) -> p a b", a=16)
            L = sm[:, 96:112]
            el = sm[:, 112:176].rearrange("p (a b) -> p a b", a=16)
            self.red(M, mview, ALU.max)
            self.tt(e, mview, bc(M, (128, 16, 4), 2), ALU.subtract)
            self.act(e, e, AF.Exp)
            self.tt(el, e, lview, ALU.mult)
            self.red(L, el, ALU.add)
            self.recip(L, L)
            self.tt(e, e, bc(L, (128, 16, 4), 2), ALU.mult)
            for q4 in range(4):
                tO = self.scr(T1, (128, 4, 4, 64), F32)
                self.tt(tO, Ost[:, q4 * 4:(q4 + 1) * 4, :, :], bc(e[:, q4 * 4:(q4 + 1) * 4, :], (128, 4, 4, 64), 3), ALU.mult)
                self.red(oacc, tO.rearrange("p q i d -> p q d i"), ALU.add)
                self.copy(o_tok[:, q4 * 4:(q4 + 1) * 4, h * 64:(h + 1) * 64], oacc, eng="act")
        if ASTG < 5:
            return
        for qt in range(16):
            hf = qt % 2
            tb = pbTb[hf][:, 0:512]
            for c in range(4):
                self.tr(tb[:, c * 128:(c + 1) * 128], o_tok[:, qt, c * 128:(c + 1) * 128], self.identb)
            ATX = int(os.environ.get("ATT_X", "0"))
            if ATX == 1:
                self.copy(oT[:, :, qt * 128:(qt + 1) * 128], tb.rearrange("p (a b) -> p a b", a=4), eng="dve")
            elif ATX == 0:
                self.copy(oT[:, :, qt * 128:(qt + 1) * 128], tb.rearrange("p (a b) -> p a b", a=4), eng="act")
        ASUB = int(os.environ.get("ATT_SUB", "9"))
        if ASUB < 1:
            return
        wo = [self.wload(d["attn_w_out"][0][:, hf * 512:(hf + 1) * 512].rearrange("(c p) n -> p c n", p=128), (128, 4, 512)) for hf in range(2)]
        for tq in range(4):
            for oc in range(8):
                ps = self.acc()
                for c in range(4):
                    self.mm(ps[:, :], wo[oc // 4][:, c, (oc % 4) * 128:(oc % 4 + 1) * 128], oT[:, c, tq * 512:(tq + 1) * 512], c == 0, c == 3)
                self.copy(yT[:, oc, :], ps[:, :], eng="act" if oc % 2 else "dve")
            if ASUB >= 2:
                self.postnorm_add(tq * 512, yT, V["G_m"], TMP)

    def build(self):
        self.setup()
        self.load_x()
        for l in self.layers:
            V = self.ada_layer(l)
            kind = l % 3
            if kind == 0:
                self.mixer_ssd(l, V)
            elif kind == 1:
                self.mixer_lru(l, V)
            else:
                self.mixer_attn(l, V)
            self.mlp(l, V)
        self.store_x()
        self.stats = self.P.emit(final_dkeys=["out0", "out1"])


_NC_CACHE = {}


def _prep_common(inputs):
    f = lambda a: np.ascontiguousarray(np.asarray(a, dtype=np.float32))
    com = {}
    com["ktab"] = make_ktab()
    com["amask"] = make_amask()
    com["ada_w"] = f(inputs["ada_w"])
    com["ada_b"] = f(inputs["ada_b"]).reshape(192, 128)
    for nm in ("norm_mix_pre", "norm_mix_post", "norm_mlp_pre", "norm_mlp_post"):
        com[nm] = f(inputs[nm]).reshape(32, 128)
    com["mlp_w1"] = f(inputs["mlp_w1"])
    com["mlp_w2"] = f(inputs["mlp_w2"])
    com["ssd_w_in"] = f(inputs["ssd_w_in"])
    com["ssd_conv_w"] = f(inputs["ssd_conv_w"]).reshape(256, 128)
    com["ssd_conv_b"] = f(inputs["ssd_conv_b"]).reshape(64, 128)
    com["ssd_dt_bias"] = f(inputs["ssd_dt_bias"])
    com["ssd_a_log"] = f(inputs["ssd_a_log"])
    com["ssd_d"] = f(inputs["ssd_d"])
    com["ssd_norm"] = f(inputs["ssd_norm"]).reshape(32, 128)
    com["ssd_w_out"] = f(inputs["ssd_w_out"])
    com["lru_w_in"] = f(inputs["lru_w_in"])
    com["lru_conv_w"] = f(inputs["lru_conv_w"]).reshape(32, 128)
    com["lru_conv_b"] = f(inputs["lru_conv_b"]).reshape(8, 128)
    com["lru_w_gate_a"] = f(inputs["lru_w_gate_a"]).reshape(1024, 256)
    com["lru_b_gate_a"] = f(inputs["lru_b_gate_a"]).reshape(8, 128)
    com["lru_w_gate_x"] = f(inputs["lru_w_gate_x"]).reshape(1024, 256)
    com["lru_b_gate_x"] = f(inputs["lru_b_gate_x"]).reshape(8, 128)
    com["lru_lambda"] = f(inputs["lru_lambda"]).reshape(8, 128)
    com["lru_w_out"] = f(inputs["lru_w_out"])
    com["attn_w_qkv"] = f(inputs["attn_w_qkv"])
    com["attn_w_out"] = f(inputs["attn_w_out"])
    return com


def run_layers(inputs, x, layers, cores=8, trace=False):
    key = tuple(layers)
    if key not in _NC_CACHE:
        _NC_CACHE[key] = K(list(layers))
    kb = _NC_CACHE[key]
    com = _prep_common(inputs)
    c = np.asarray(inputs["c"], np.float32)
    pos = np.asarray(inputs["positions"], np.int32)
    in_maps = []
    for b in range(cores):
        m = dict(com)
        m["x"] = np.ascontiguousarray(x[b])
        m["c"] = np.ascontiguousarray(c[b].reshape(8, 128))
        m["positions"] = np.ascontiguousarray(pos[b].reshape(1, S))
        in_maps.append({k: v for k, v in m.items() if k in kb.d})
    res = run_bass_kernel_spmd(kb.nc, in_maps, core_ids=list(range(cores)), trace=trace)
    out = np.stack([r["out"] for r in res.results], axis=0)
    return out, res


def kernel(**inputs):
    x = np.asarray(inputs["x"], np.float32)
    out, _ = run_layers(inputs, x, (0, 1, 2, 3))
    return out.astype(np.float32)
```
